# Optimizing a Trainium2 kernel written in Bass

```python
import numpy as np
import jax
import jax.numpy as jnp
from jax import lax

D_MODEL = 2048
BATCH = 2
SEQ = 8192
DEPTH = 1

HEAD_DIM_A = 64
HQ_A = 16
HKV_A = 4
GQA_GROUP = HQ_A // HKV_A
WINDOW = 128
ROPE_THETA = 10000.0
H_B = 4
DK_B = D_MODEL // 2 // H_B
DV_B = D_MODEL // H_B
GK_RANK = 16
GK_NORMALIZER = 16.0
CHUNK = 64
D_FF = ((8 * D_MODEL // 3 + 255) // 256) * 256
CONV_WIDTH = 3
EPS = 1e-6

Q_A = HQ_A * HEAD_DIM_A
KV_A = HKV_A * HEAD_DIM_A
QK_B = H_B * DK_B
V_B = H_B * DV_B
SPLIT_SIZES = (Q_A, KV_A, KV_A, QK_B, QK_B, V_B, GK_RANK, V_B, D_MODEL, D_MODEL)
IN_COLS = sum(SPLIT_SIZES)

kernel_name = "hybrid_swa_gla_convffn_block"


def _rms_norm(x, w):
    xf = x.astype(jnp.float32)
    y = xf * lax.rsqrt(jnp.mean(xf * xf, axis=-1, keepdims=True) + EPS)
    return (y * w.astype(jnp.float32)).astype(x.dtype)


def _rope_tables(positions):
    inv_freq = ROPE_THETA ** (-jnp.arange(0, HEAD_DIM_A, 2, dtype=jnp.float32) / HEAD_DIM_A)
    ang = positions.astype(jnp.float32)[..., None] * inv_freq
    return jnp.cos(ang)[:, :, None, :], jnp.sin(ang)[:, :, None, :]


def _apply_rope(x, cos, sin):
    xf = x.astype(jnp.float32)
    x1, x2 = jnp.split(xf, 2, axis=-1)
    return jnp.concatenate([x1 * cos - x2 * sin, x2 * cos + x1 * sin], axis=-1).astype(x.dtype)


def _sliding_window_attention(q, k, v, sinks):
    B, S = q.shape[0], q.shape[1]
    nb = S // WINDOW
    qb = q.reshape(B, nb, WINDOW, HKV_A, GQA_GROUP, HEAD_DIM_A)

    def band(t):
        tp = jnp.pad(t, ((0, 0), (WINDOW, 0), (0, 0), (0, 0)))
        tp = tp.reshape(B, nb + 1, WINDOW, HKV_A, HEAD_DIM_A)
        return jnp.concatenate([tp[:, :-1], tp[:, 1:]], axis=2)

    kb, vb = band(k), band(v)
    s = jnp.einsum("bnqhgd,bnkhd->bnhgqk", qb, kb).astype(jnp.float32) * (HEAD_DIM_A ** -0.5)
    i = jnp.arange(WINDOW)[:, None]
    j = jnp.arange(2 * WINDOW)[None, :]
    dist = WINDOW + i - j
    in_win = (dist >= 0) & (dist < WINDOW)
    blk = jnp.arange(nb)[:, None, None]
    valid = in_win[None] & ((blk > 0) | (j[None] >= WINDOW))
    s = jnp.where(valid[None, :, None, None], s, -jnp.inf)
    sink = sinks.astype(jnp.float32).reshape(1, 1, HKV_A, GQA_GROUP, 1, 1)
    m = jnp.maximum(jnp.max(s, axis=-1, keepdims=True), sink)
    p = jnp.exp(s - m)
    denom = jnp.sum(p, axis=-1, keepdims=True) + jnp.exp(sink - m)
    probs = (p / denom).astype(v.dtype)
    o = jnp.einsum("bnhgqk,bnkhd->bnqhgd", probs, vb)
    return o.reshape(B, S, Q_A)


def _gla_chunked(q, k, v, g):
    B, S = q.shape[0], q.shape[1]
    nc = S // CHUNK

    def chunks(t):
        return t.reshape(B, nc, CHUNK, H_B, t.shape[-1])

    q, k, v, g = chunks(q), chunks(k), chunks(v), chunks(g)
    b = jnp.cumsum(g, axis=2)
    b_mid = b[:, :, CHUNK // 2:CHUNK // 2 + 1]
    b_last = b[:, :, CHUNK - 1:CHUNK]
    a = jnp.einsum("bnihd,bnjhd->bnhij", q * jnp.exp(b - b_mid), k * jnp.exp(b_mid - b))
    a = jnp.where(jnp.tril(jnp.ones((CHUNK, CHUNK), dtype=bool)), a, 0.0)
    o_intra = jnp.einsum("bnhij,bnjhe->bnihe", a, v)
    q_in = q * jnp.exp(b)
    k_st = k * jnp.exp(b_last - b)
    decay = jnp.exp(b_last[:, :, 0])

    def step(state, xs):
        qn, kn, vn, dn = xs
        o = jnp.einsum("bihd,bhde->bihe", qn, state)
        state = state * dn[..., None] + jnp.einsum("bjhd,bjhe->bhde", kn, vn)
        return state, o

    xs = tuple(jnp.moveaxis(t, 1, 0) for t in (q_in, k_st, v, decay))
    state0 = jnp.zeros((B, H_B, DK_B, DV_B), jnp.float32)
    _, o_inter = lax.scan(step, state0, xs)
    o = o_intra + jnp.moveaxis(o_inter, 0, 1)
    return o.reshape(B, S, H_B, DV_B)


def _token_mixer(h, cos, sin, w_in, sinks, w_gk_up, b_gk, gla_norm, w_br_a, w_br_b, w_out):
    B, S = h.shape[0], h.shape[1]
    proj = h @ w_in
    points = [int(p) for p in np.cumsum(SPLIT_SIZES)[:-1]]
    q_a, k_a, v_a, q_b, k_b, v_b, gk_lr, og_b, g_a, g_b = jnp.split(proj, points, axis=-1)
    q_a = _apply_rope(q_a.reshape(B, S, HQ_A, HEAD_DIM_A), cos, sin)
    k_a = _apply_rope(k_a.reshape(B, S, HKV_A, HEAD_DIM_A), cos, sin)
    v_a = v_a.reshape(B, S, HKV_A, HEAD_DIM_A)
    o_a = _sliding_window_attention(q_a, k_a, v_a, sinks)
    gk = jax.nn.log_sigmoid((gk_lr @ w_gk_up + b_gk).astype(jnp.float32)) / GK_NORMALIZER
    o_b = _gla_chunked(
        q_b.astype(jnp.float32).reshape(B, S, H_B, DK_B) * (DK_B ** -0.5),
        k_b.astype(jnp.float32).reshape(B, S, H_B, DK_B),
        v_b.astype(jnp.float32).reshape(B, S, H_B, DV_B),
        gk.reshape(B, S, H_B, DK_B))
    o_b = _rms_norm(o_b, gla_norm) * jax.nn.silu(og_b.astype(jnp.float32).reshape(B, S, H_B, DV_B))
    o_b = o_b.astype(h.dtype).reshape(B, S, V_B)
    merged = jax.nn.sigmoid(g_a) * (o_a @ w_br_a) + jax.nn.sigmoid(g_b) * (o_b @ w_br_b)
    return merged @ w_out


def _conv_ffn(h, w_up, conv_w, conv_b, w_down):
    u = h @ w_up
    u = lax.conv_general_dilated(
        u, conv_w[:, None, :], window_strides=(1,), padding=[(CONV_WIDTH - 1, 0)],
        dimension_numbers=("NWC", "WIO", "NWC"), feature_group_count=2 * D_FF) + conv_b
    gate, val = jnp.split(u, 2, axis=-1)
    return (jax.nn.silu(gate) * val) @ w_down


def setup_inputs(seed: int = 0) -> dict:
    key = jax.random.key(seed)
    ks = jax.random.split(key, 24)

    def nrm(k, shape, scale):
        return jax.random.normal(k, shape, jnp.float32) * scale

    L, D = DEPTH, D_MODEL
    return {
        "x": nrm(ks[0], (BATCH, SEQ, D), 1.0),
        "c": nrm(ks[1], (BATCH, D), 1.0),
        "positions": jnp.broadcast_to(jnp.arange(SEQ, dtype=jnp.int32), (BATCH, SEQ)),
        "w_mod": nrm(ks[2], (L, D, 6 * D), 0.5 * D ** -0.5),
        "b_mod": nrm(ks[3], (L, 6 * D), 0.02),
        "mix_norm_pre": 1.0 + nrm(ks[4], (L, D), 0.02),
        "mix_norm_post": 1.0 + nrm(ks[5], (L, D), 0.02),
        "w_in": nrm(ks[6], (L, D, IN_COLS), D ** -0.5),
        "attn_sinks": nrm(ks[7], (L, HQ_A), 0.5),
        "w_gk_up": nrm(ks[8], (L, GK_RANK, QK_B), GK_RANK ** -0.5),
        "b_gk": nrm(ks[9], (L, QK_B), 0.1),
        "gla_norm": 1.0 + nrm(ks[10], (L, DV_B), 0.02),
        "w_branch_attn": nrm(ks[11], (L, Q_A, D), Q_A ** -0.5),
        "w_branch_gla": nrm(ks[12], (L, V_B, D), V_B ** -0.5),
        "w_out": nrm(ks[13], (L, D, D), D ** -0.5),
        "ffn_norm_pre": 1.0 + nrm(ks[14], (L, D), 0.02),
        "ffn_norm_post": 1.0 + nrm(ks[15], (L, D), 0.02),
        "w_up": nrm(ks[16], (L, D, 2 * D_FF), D ** -0.5),
        "conv_w": nrm(ks[17], (L, CONV_WIDTH, 2 * D_FF), CONV_WIDTH ** -0.5),
        "conv_b": nrm(ks[18], (L, 2 * D_FF), 0.02),
        "w_down": nrm(ks[19], (L, D_FF, D), D_FF ** -0.5),
    }


def reference(x, c, positions, w_mod, b_mod, mix_norm_pre, mix_norm_post, w_in, attn_sinks,
              w_gk_up, b_gk, gla_norm, w_branch_attn, w_branch_gla, w_out,
              ffn_norm_pre, ffn_norm_post, w_up, conv_w, conv_b, w_down):
    cos, sin = _rope_tables(positions)
    c_act = jax.nn.silu(c)
    for l in range(DEPTH):
        mod = c_act @ w_mod[l] + b_mod[l]
        sh1, sc1, g1, sh2, sc2, g2 = jnp.split(mod, 6, axis=-1)
        h = _rms_norm(x, mix_norm_pre[l]) * (1.0 + sc1[:, None]) + sh1[:, None]
        y = _token_mixer(h, cos, sin, w_in[l], attn_sinks[l], w_gk_up[l], b_gk[l], gla_norm[l],
                         w_branch_attn[l], w_branch_gla[l], w_out[l])
        x = x + g1[:, None] * _rms_norm(y, mix_norm_post[l])
        h = _rms_norm(x, ffn_norm_pre[l]) * (1.0 + sc2[:, None]) + sh2[:, None]
        y = _conv_ffn(h, w_up[l], conv_w[l], conv_b[l], w_down[l])
        x = x + g2[:, None] * _rms_norm(y, ffn_norm_post[l])
    return x
```

```python
import contextlib
import os
import numpy as np
import concourse.bass as bass
import concourse.mybir as mybir
from concourse.bass_utils import run_bass_kernel_spmd

F32 = mybir.dt.float32
BF16 = mybir.dt.bfloat16
I32 = mybir.dt.int32
AF = mybir.ActivationFunctionType
ALU = mybir.AluOpType
AX = mybir.AxisListType

D = 2048
KT = 16
DFF = 5632
FT = DFF // 128
EPS = 1e-6
NEG = -30000.0
TWO_PI = float(2 * np.pi)
COMPUTE = ("pe", "act", "dve", "pool")


class Buf:
    __slots__ = ("w", "r")

    def __init__(self):
        self.w = None
        self.r = []


class Op:
    __slots__ = ("eng", "fn", "deps", "dma", "sem", "val", "inc", "rank", "stage")


class Prog:
    def __init__(self, nc):
        self.nc = nc
        self.ops = {e: [] for e in ("pe", "act", "dve", "pool", "sp")}
        self.n_dma_sems = {"sp": 10, "act": 4, "pool": 8}
        self.dma_use = {}
        self.dma_rr = {q: 0 for q in self.n_dma_sems}
        self.dma_last = {}

    def add(self, eng, fn, r=(), w=(), inc=None, dma=False, extra=(), cc=False):
        op = Op()
        op.eng = eng
        op.stage = getattr(self, "stage", "s")
        op.fn = fn
        op.dma = dma
        op.sem = None
        op.val = 0
        op.rank = None
        if inc is None:
            inc = eng != "pe"
        op.inc = inc or dma
        deps = list(extra)
        for b in r:
            if b.w is not None:
                deps.append(b.w)
        for b in w:
            if b.w is not None:
                deps.append(b.w)
            deps.extend(b.r)
        if cc:
            op.dma = True
            op.inc = True
            self.n_cc = getattr(self, "n_cc", 0) + 1
            op.sem = ("cc", self.n_cc)
            op.val = 1
        elif dma:
            slot = self.dma_rr[eng]
            self.dma_rr[eng] = (slot + 1) % self.n_dma_sems[eng]
            key = (eng, slot)
            prev = self.dma_last.get(key)
            if prev is not None:
                deps.append(prev)
            self.dma_last[key] = op
            n = self.dma_use.get(key, 0) + 1
            self.dma_use[key] = n
            op.sem = key
            op.val = 16 * n
        op.deps = deps
        self.ops[eng].append(op)
        for b in r:
            if not op.dma:
                b.r = [o for o in b.r if o.dma or o.eng != eng]
            b.r.append(op)
        for b in w:
            b.w = op
            b.r = []
        return op

    def pe(self, fn, r=(), w=(), inc=False):
        return self.add("pe", fn, r, w, inc=inc)

    def barrier(self):
        last = []
        for e in self.ops:
            lst = [o for o in self.ops[e] if o.fn is not None and not o.dma]
            if lst:
                if e == "pe":
                    lst[-1].inc = True
                last.append(lst[-1])
        last.extend(self.dma_last.values())
        last.extend(o for o in self.ops["pool"] if o.dma and o.sem[0] == "cc")
        for e in self.ops:
            self.add(e, None, inc=False, extra=list(last))

    def emit(self):
        nc = self.nc
        lst = [o for o in self.ops["pe"] if o.fn is not None]
        if lst:
            lst[-1].inc = True
        for e in self.ops:
            rank = 0
            pend = []
            for op in self.ops[e]:
                if op.dma or op.fn is None:
                    continue
                if op.inc:
                    rank += 1
                    op.rank = rank
                    for p in pend:
                        p.rank = rank
                    pend = []
                else:
                    pend.append(op)
            assert not pend, (e, len(pend))
        with contextlib.ExitStack() as st:
            sems = {}
            for e in COMPUTE:
                sems[e] = st.enter_context(nc.semaphore("s_" + e))
            for q, n in self.n_dma_sems.items():
                for i in range(n):
                    sems[(q, i)] = st.enter_context(nc.semaphore(f"d_{q}{i}"))
            for i in range(getattr(self, "n_cc", 0)):
                sems[("cc", i + 1)] = st.enter_context(nc.semaphore(f"cc{i}"))
            block = st.enter_context(nc.Block())

            def ev(op):
                if op.dma:
                    return op.sem, op.val
                return op.eng, op.rank

            prof = bool(os.environ.get("KPROF"))

            def replay(ename, eobj):
                known = {}
                cur = [None, None]
                for op in self.ops[ename]:
                    if prof and op.stage != cur[0]:
                        if cur[1] is not None:
                            cur[1].__exit__(None, None, None)
                        cur[0] = op.stage
                        cur[1] = nc.named_scope(op.stage)
                        cur[1].__enter__()
                    _replay_one(ename, eobj, op, known)
                if cur[1] is not None:
                    cur[1].__exit__(None, None, None)

            def _replay_one(ename, eobj, op, known):
                if True:
                    need = {}
                    for d in op.deps:
                        if d.fn is None:
                            continue
                        if d.eng == ename and not d.dma and ename in ("pe", "sp"):
                            continue
                        k, v = ev(d)
                        if known.get(k, 0) < v and need.get(k, 0) < v:
                            need[k] = v
                    for k, v in need.items():
                        eobj.wait_ge(sems[k], v)
                        known[k] = v
                    if op.fn is None:
                        return
                    ins = op.fn(eobj)
                    if op.dma and op.sem[0] == "cc":
                        ins.then_inc(sems[op.sem])
                    elif op.dma:
                        ins.then_inc(sems[op.sem], 16)
                    elif op.inc:
                        ins.then_inc(sems[ename], 1)

            @block.tensor
            def _(e):
                replay("pe", e)

            @block.scalar
            def _(e):
                replay("act", e)

            @block.vector
            def _(e):
                replay("dve", e)

            @block.gpsimd
            def _(e):
                replay("pool", e)

            @block.sync
            def _(e):
                replay("sp", e)


def build(T, dbg=False, stop_at=0, solo=False):
    NT = T // 128
    NG = T // 512
    NT1 = NT + 1
    TH = T + 128
    nc = bass.Bass("TRN2", target_bir_lowering=False)

    def din(name, shape, dt=F32):
        return nc.dram_tensor(name, list(shape), dt, kind="ExternalInput").ap()

    def dscr(name, shape, dt=F32, internal=False):
        if dbg and not internal:
            return nc.dram_tensor(name, list(shape), dt, kind="ExternalOutput").ap()
        return nc.dram_tensor(name, list(shape), dt).ap()

    x_d = din("x", [T, D])
    xh_d = din("xh", [128, D])
    c_d = din("c", [128, KT])
    pos_d = din("pos", [128, NT1], I32)
    cfg_d = din("cfg", [128, 16])
    invf_d = din("invf", [128, 32])
    ident_d = din("ident", [128, 128])
    band_d = din("band", [128, 256])
    tri_d = din("tri", [128, 128])
    scanm_d = din("scanm", [128, 1024])
    w_mod = din("w_mod", [D, 6 * D])
    bmod_in = din("b_mod", [6 * D])
    npre1 = din("mix_norm_pre", [D])
    npost1 = din("mix_norm_post", [D])
    w_in = din("w_in", [D, 11792])
    sinks_d = din("attn_sinks", [16])
    wgk_d = din("wgk", [17, 1024])
    glan_d = din("gla_norm", [512])
    w_bra = din("w_branch_attn", [1024, D])
    w_brb = din("w_branch_gla", [2048, D])
    w_out = din("w_out", [D, D])
    npre2 = din("ffn_norm_pre", [D])
    npost2 = din("ffn_norm_post", [D])
    w_up = din("w_up", [D, 2 * DFF])
    convp_d = din("convp", [128, 4, 2 * FT])
    w_down = din("w_down", [DFF, D])
    out_d = nc.dram_tensor("out", [T, D], F32, kind="ExternalOutput").ap()

    mod_d = dscr("mod_s", [6 * D])
    oaT_d = dscr("oaT_s", [1024, T], BF16)
    qbT_d = dscr("qbT_s", [1024, T])
    kbT_d = dscr("kbT_s", [1024, T])
    vb_d = dscr("vb_s", [T, 2048], BF16)
    sog_d = dscr("sog_s", [T, 2048], BF16)
    sgaT_d = dscr("sgaT_s", [2048, T], BF16)
    sgbT_d = dscr("sgbT_s", [2048, T], BF16)
    oloc_d = dscr("oloc_s", [T, 2048])
    sloc_q = [dscr(f"sloc{q}_s", [256, 520], internal=True) for q in range(4)]
    sall_q = [dscr(f"sall{q}_s", [4 * 256, 520], internal=True) for q in range(4)]
    obT_d = dscr("obT_s", [2048, T], BF16)
    mgT_d = dscr("mgT_s", [2048, T], BF16)
    x1_d = dscr("x1_s", [T, D])
    h2T_d = dscr("h2T_s", [2048, T], BF16)
    xl_d = dscr("xl_s", [2, D], internal=True)
    xla_d = dscr("xla_s", [8, D], internal=True)
    actT_d = dscr("actT_s", [DFF, T], BF16)
    y2_d = dscr("y2_s", [T, D])

    P = Prog(nc)
    st = contextlib.ExitStack()
    nstage = [0]

    def stage_end():
        nstage[0] += 1
        P.barrier()
        if nstage[0] == stop_at:
            P.emit()
            st.close()
            return True
        return False

    SB_BYTES = 206 * 1024
    big = st.enter_context(nc.sbuf_tensor("big", [128, SB_BYTES // 4], F32))
    PF = st.enter_context(nc.psum_tensor("PF", [128, 3072], F32))
    PB = st.enter_context(nc.psum_tensor("PB", [128, 2048], BF16))
    pbuf = [Buf() for _ in range(8)]

    def pf(bank, n=512, off=0):
        return PF[:, bank * 512 + off: bank * 512 + off + n]

    def pb(bank, n=1024, off=0):
        return PB[:, (bank - 6) * 1024 + off:(bank - 6) * 1024 + off + n]

    class Alloc:
        def __init__(self):
            self.off = 0

        def __call__(self, shape, dt=F32):
            esz = 4 if dt in (F32, I32) else 2
            n = int(np.prod(shape[1:])) * esz
            n = (n + 31) // 32 * 32
            assert self.off + n <= SB_BYTES, ("SBUF overflow", self.off + n)
            a = big[:, self.off // 4:(self.off + n) // 4]
            self.off += n
            if dt != F32:
                a = a.bitcast(dt)
            a = a[:, 0:int(np.prod(shape[1:]))]
            if len(shape) == 3:
                a = a.rearrange("p (a b) -> p a b", a=shape[1])
            elif len(shape) == 4:
                a = a.rearrange("p (a b c) -> p a b c", a=shape[1], b=shape[2])
            if shape[0] != 128:
                a = a[0:shape[0]]
            return a

    A = Alloc()

    def MM(out, lhsT, rhs, start=True, stop=True, r=(), w=(), inc=False, tp=None):
        kw = {} if tp is None else {"tile_position": tp}
        P.pe(lambda e: e.matmul(out, lhsT=lhsT, rhs=rhs, start=start, stop=stop, **kw), r=r, w=w, inc=inc)

    def TR(out, in_, idn, r=(), w=(), inc=False):
        P.pe(lambda e: e.transpose(out=out, in_=in_, identity=idn), r=r, w=w, inc=inc)

    def ACT(out, in_, func, r=(), w=(), bias=None, scale=None, accum=None):
        kw = {}
        if bias is not None:
            kw["bias"] = bias
        if scale is not None:
            kw["scale"] = scale
        if accum is not None:
            kw["accum_out"] = accum
        P.add("act", lambda e: e.activation(out=out, in_=in_, func=func, **kw), r, w)

    def TT(eng, out, in0, in1, op, r=(), w=()):
        P.add(eng, lambda e: e.tensor_tensor(out=out, in0=in0, in1=in1, op=op), r, w)

    def TS(eng, out, in0, s1, op0, s2=None, op1=None, r=(), w=()):
        if op1 is None:
            P.add(eng, lambda e: e.tensor_scalar(out=out, in0=in0, scalar1=s1, scalar2=None, op0=op0), r, w)
        else:
            P.add(eng, lambda e: e.tensor_scalar(out=out, in0=in0, scalar1=s1, scalar2=s2, op0=op0, op1=op1), r, w)

    def STT(out, in0, scalar, in1, op0, op1, r=(), w=()):
        P.add("dve", lambda e: e.scalar_tensor_tensor(out=out, in0=in0, scalar=scalar, in1=in1, op0=op0, op1=op1), r, w)

    def CP(eng, out, in_, r=(), w=()):
        if eng == "act":
            P.add("act", lambda e: e.activation(out=out, in_=in_, func=AF.Copy), r, w)
        else:
            P.add(eng, lambda e: e.tensor_copy(out=out, in_=in_), r, w)

    def DMA(q, out, in_, r=(), w=(), slow=False):
        if slow:
            P.add(q, lambda e: e.dma_start(out=out, in_=in_, allow_slow_non_contiguous=True), r, w, dma=True)
        else:
            P.add(q, lambda e: e.dma_start(out=out, in_=in_), r, w, dma=True)

    def RECIP(out, in_, r=(), w=()):
        P.add("dve", lambda e: e.reciprocal(out=out, in_=in_), r, w)

    def bc_mid(ap, n):
        return ap.unsqueeze(1).broadcast_to([ap.shape[0], n, ap.shape[1]])

    def bc_last(ap, n):
        return ap.unsqueeze(2).broadcast_to([ap.shape[0], ap.shape[1], n])

    def wview(w, c0, n):
        return w[:, c0:c0 + n].rearrange("(k p) n -> p k n", p=128)

    def load_w(dst, w, c0, n, bufs, nk=None):
        kt = dst.shape[1]
        step = max(1, 1024 // 128) if n >= 256 else kt
        step = min(step, kt)
        v = wview(w, c0, n)
        for k0 in range(0, kt, step):
            k1 = min(kt, k0 + step)
            DMA("pool", dst[:, k0:k1, :], v[:, k0:k1, :], w=bufs)

    def rms_scale(ssq, tmp, rstd, r, w):
        TS("dve", tmp, ssq, 1.0 / D, ALU.mult, EPS, ALU.add, r=r, w=w)
        ACT(tmp, tmp, AF.Sqrt, r=w, w=w)
        RECIP(rstd, tmp, r=w, w=w)

    ident_f = A([128, 128])
    ident = A([128, 128], BF16)
    cfg = A([128, 16])
    convp = A([128, 4, 2 * FT])
    sinkbc = A([128, 16])
    band = A([128, 256])
    band0 = A([128, 256])
    tri = A([128, 128])
    scanm = A([128, 1024])
    glan = A([128, 512])
    wgk = A([17, 1024])
    cosd = A([128, NT1, 64])
    sins = A([128, NT1, 64])
    gkT = A([17, T])
    h2Th = A([128, KT, 2], BF16)
    small = A([128, 64])
    cbP = A([128, KT], BF16)
    b_cb = Buf()
    bK = Buf()
    b_gkT = Buf()
    b_h2Th = Buf()
    b_small = Buf()
    PERSIST = A.off

    DMA("sp", ident_f, ident_d, w=[bK])
    DMA("sp", cfg, cfg_d, w=[bK])
    DMA("sp", convp, convp_d, w=[bK])
    DMA("sp", sinkbc, sinks_d.partition_broadcast(128), w=[bK])
    DMA("sp", band, band_d, w=[bK])
    DMA("sp", tri, tri_d, w=[bK])
    DMA("sp", scanm, scanm_d, w=[bK])
    DMA("sp", glan, glan_d.partition_broadcast(128), w=[bK])
    DMA("sp", wgk, wgk_d, w=[bK])
    CP("dve", ident, ident_f, r=[bK], w=[bK])
    CP("dve", band0, band, r=[bK], w=[bK])
    TS("dve", band0[:, 0:128], band[:, 0:128], cfg[:, 1:2], ALU.add, r=[bK], w=[bK])
    P.add("pool", lambda e: e.memset(gkT[0:17, :], 1.0), w=[b_gkT])

    P.stage = "stage_0"
    A.off = PERSIST
    s0_pos_i = A([128, NT1], I32)
    s0_pos_f = A([128, NT1])
    s0_invf = A([128, 32])
    s0_ang = A([128, NT1, 32])
    s0_a2 = A([128, NT1, 32])
    s0_ki = A([128, NT1, 32], I32)
    s0_kf = A([128, NT1, 32])
    s0_m = A([128, NT1, 32])
    s0_sin = A([128, NT1, 32])
    s0_cos = A([128, NT1, 32])
    bT = Buf()
    DMA("sp", s0_pos_i, pos_d, w=[bT])
    DMA("sp", s0_invf, invf_d, w=[bT])
    CP("dve", s0_pos_f, s0_pos_i, r=[bT], w=[bT])
    TT("dve", s0_ang, bc_mid(s0_invf, NT1), bc_last(s0_pos_f, 32), ALU.mult, r=[bT], w=[bT])

    def reduce_sin(dst, src, phase):
        TS("dve", s0_a2, src, phase, ALU.add, r=[bT], w=[bT])
        TS("dve", s0_ki, s0_a2, 1.0 / TWO_PI, ALU.mult, r=[bT], w=[bT])
        CP("dve", s0_kf, s0_ki, r=[bT], w=[bT])
        STT(s0_a2, s0_kf, -TWO_PI, s0_a2, ALU.mult, ALU.add, r=[bT], w=[bT])
        TS("dve", s0_m, s0_a2, float(np.pi), ALU.is_gt, r=[bT], w=[bT])
        STT(s0_a2, s0_m, -TWO_PI, s0_a2, ALU.mult, ALU.add, r=[bT], w=[bT])
        TS("dve", s0_m, s0_a2, -float(np.pi), ALU.is_lt, r=[bT], w=[bT])
        STT(s0_a2, s0_m, TWO_PI, s0_a2, ALU.mult, ALU.add, r=[bT], w=[bT])
        ACT(dst, s0_a2, AF.Sin, r=[bT], w=[bT])

    reduce_sin(s0_sin, s0_ang, 0.0)
    reduce_sin(s0_cos, s0_ang, float(np.pi / 2))
    CP("dve", cosd[:, :, 0:32], s0_cos, r=[bT], w=[bK])
    CP("dve", cosd[:, :, 32:64], s0_cos, r=[bT], w=[bK])
    TS("dve", sins[:, :, 0:32], s0_sin, -1.0, ALU.mult, r=[bT], w=[bK])
    CP("dve", sins[:, :, 32:64], s0_sin, r=[bT], w=[bK])

    s0_c = A([128, KT])
    s0_cb = A([128, KT], BF16)
    s0_slab = [A([128, KT, 512], BF16) for _ in range(2)]
    s0_bm = [A([1, 512]) for _ in range(2)]
    s0_row = [A([1, 512]) for _ in range(2)]
    b_c = Buf()
    b_sl = [Buf(), Buf()]
    b_bm = [Buf(), Buf()]
    b_row = [Buf(), Buf()]
    b_mod = Buf()
    DMA("sp", s0_c, c_d, w=[b_c])
    ACT(cbP, s0_c, AF.Silu, r=[b_c], w=[b_cb])

    def mod_slab(j, slab_ap, b_slab_, bm_ap, b_bm_, row_ap, b_row_, bank):
        load_w(slab_ap, w_mod, j * 512, 512, [b_slab_])
        DMA("sp", bm_ap, bmod_in[j * 512:(j + 1) * 512].unsqueeze(0), w=[b_bm_])
        for k in range(KT):
            MM(pf(bank)[0:1, :], cbP[:, k:k + 1], slab_ap[:, k, :], start=(k == 0), stop=(k == KT - 1),
               r=[b_cb, b_slab_], w=[pbuf[bank]], inc=(k == KT - 1))
        TT("dve", row_ap, pf(bank)[0:1, :], bm_ap, ALU.add, r=[pbuf[bank], b_bm_], w=[b_row_])
        DMA("sp", mod_d[j * 512:(j + 1) * 512].unsqueeze(0), row_ap, r=[b_row_], w=[b_mod])

    for j in range(8):
        i = j % 2
        mod_slab(j, s0_slab[i], b_sl[i], s0_bm[i], b_bm[i], s0_row[i], b_row[i], i)

    if stage_end():
        return nc

    P.stage = "stage_1"
    A.off = PERSIST
    hT = A([128, KT, TH], BF16)
    b_hT = [Buf() for _ in range(NT1)]
    HT_END = A.off
    bcA = A([128, D])
    bcB = A([128, D])
    bcC = A([128, D])
    b_bc = Buf()
    xt = [A([128, D]) for _ in range(2)]
    b_xt = [Buf(), Buf()]
    junk = A([128, D], BF16)
    b_junk = Buf()
    tmpf = A([128, D])
    b_tmpf = Buf()
    hb = [A([128, D], BF16) for _ in range(2)]
    b_hb = [Buf(), Buf()]
    DMA("sp", bcC, mod_d[D:2 * D].partition_broadcast(128), r=[b_mod], w=[b_bc])
    DMA("sp", bcA, npre1.partition_broadcast(128), w=[b_bc])
    DMA("sp", bcB, mod_d[0:D].partition_broadcast(128), r=[b_mod], w=[b_bc])
    STT(bcA, bcC, 1.0, bcA, ALU.add, ALU.mult, r=[b_bc], w=[b_bc])

    def norm_tile(src_ap, np_, wmod, shift, hb_ap, b_src, b_hb_i, sidx):
        ss = small[0:np_, sidx:sidx + 1]
        tm = small[0:np_, sidx + 1:sidx + 2]
        rs = small[0:np_, sidx + 2:sidx + 3]
        ACT(junk[0:np_], src_ap, AF.Square, r=[b_src], w=[b_junk, b_small], accum=ss)
        rms_scale(ss, tm, rs, r=[b_small], w=[b_small])
        STT(tmpf[0:np_], src_ap, rs, wmod[0:np_], ALU.mult, ALU.mult, r=[b_src, b_small, b_bc], w=[b_tmpf])
        TT("pool", hb_ap, tmpf[0:np_], shift[0:np_], ALU.add, r=[b_tmpf, b_bc], w=[b_hb_i])

    def transpose16(hb_ap, b_hb_i, dst_fn, b_dst, np_=128):
        for h in range(2):
            for kk in range(8):
                k = h * 8 + kk
                TR(pb(6 + h)[:, kk * 128:kk * 128 + np_], hb_ap[:, k * 128:(k + 1) * 128], ident[0:np_, 0:np_],
                   r=[b_hb_i, bK], w=[pbuf[6 + h]], inc=(kk == 7))
            src = pb(6 + h).rearrange("p (k c) -> p k c", k=8)[:, :, 0:np_]
            CP("act" if (h == 0 or np_ != 128) else "dve", dst_fn(h), src, r=[pbuf[6 + h]], w=b_dst)

    for ti in range(NT1):
        i = ti % 2
        src = xh_d if ti == NT else x_d[ti * 128:(ti + 1) * 128, :]
        DMA("sp", xt[i], src, w=[b_xt[i]])
        norm_tile(xt[i], 128, bcA, bcB, hb[i], b_xt[i], b_hb[i], 4 * i)
        c0 = T if ti == NT else ti * 128
        transpose16(hb[i], b_hb[i], lambda h, c0=c0: hT[:, h * 8:(h + 1) * 8, c0:c0 + 128], [b_hT[ti]])

    if stage_end():
        return nc

    P.stage = "stage_2a"
    A.off = HT_END
    kvslab = A([128, KT, 512], BF16)
    qslab = [kvslab[:, :, 0:256], kvslab[:, :, 256:512]]
    b_kvs = Buf()
    b_qs = [Buf(), Buf()]
    kT = A([64, 4, TH], BF16)
    b_kT = Buf()
    vA = A([128, NT1, 256], BF16)
    b_vA = Buf()
    qT = A([64, 4, T], BF16)
    b_qT = Buf()
    rA = A([128, 4, 64])
    rB = A([128, 4, 64])
    rX = A([128, 4, 64])
    b_rX = Buf()
    rR = [A([128, 4, 64], BF16) for _ in range(2)]
    b_rA, b_rB = Buf(), Buf()
    b_rR = [Buf(), Buf()]
    _sb = A([128, 4, 256])
    S_sb = [_sb, _sb]
    _bS = Buf()
    b_S = [_bS, _bS]
    _pe = A([128, 4, 256], BF16)
    Pe = [_pe, _pe]
    _bPe = Buf()
    b_Pe = [_bPe, _bPe]
    qpb = [Buf(), Buf()]
    Obuf = [Buf(), Buf()]
    Pn = [A([128, 4, 256], BF16) for _ in range(2)]
    b_Pn = [Buf(), Buf()]
    PTs = [A([128, 8, 128], BF16) for _ in range(2)]
    b_PT = [Buf(), Buf()]
    ost = [A([128, 2, 128], BF16) for _ in range(2)]
    b_ost = [Buf(), Buf()]
    sm = [A([128, 32]) for _ in range(2)]
    b_sm = [Buf(), Buf()]
    b_oaT = Buf()

    def rope(src3, nh, ti, dst, b_src, b_dst):
        cs = bc_mid(cosd[:, ti, :], nh)
        CP("act", rX[:, 0:nh, :], src3, r=[b_src], w=[b_rX])
        TT("dve", rA[:, 0:nh, :], rX[:, 0:nh, :], cs, ALU.mult, r=[b_rX, bK], w=[b_rA])
        TT("dve", rB[:, 0:nh, 0:32], rX[:, 0:nh, 32:64], bc_mid(sins[:, ti, 0:32], nh), ALU.mult,
           r=[b_rX, bK], w=[b_rB])
        TT("dve", rB[:, 0:nh, 32:64], rX[:, 0:nh, 0:32], bc_mid(sins[:, ti, 32:64], nh), ALU.mult,
           r=[b_rX, bK], w=[b_rB])
        TT("dve", dst, rA[:, 0:nh, :], rB[:, 0:nh, :], ALU.add, r=[b_rA, b_rB], w=[b_dst])

    def swa_gen():
        load_w(kvslab, w_in, 1024, 512, [b_kvs])
        for ti in range(NT1):
            i = ti % 2
            c0 = T if ti == NT else ti * 128
            for k in range(KT):
                MM(pf(0), hT[:, k, c0:c0 + 128], kvslab[:, k, :], start=(k == 0), stop=(k == KT - 1),
                   r=[b_hT[ti], b_kvs], w=[pbuf[0]], inc=(k == KT - 1))
            rope(pf(0, 256).rearrange("p (h d) -> p h d", h=4), 4, ti, rR[i], pbuf[0], b_rR[i])
            CP("act", vA[:, ti, :], pf(0, 256, 256), r=[pbuf[0]], w=[b_vA])
            for h in range(4):
                TR(pb(6 + i)[0:64, h * 128:(h + 1) * 128], rR[i][:, h, :], ident, r=[b_rR[i], bK], w=[pbuf[6 + i]],
                   inc=(h == 3))
            CP("act", kT[:, :, c0:c0 + 128], pb(6 + i, 512).rearrange("p (h c) -> p h c", h=4)[0:64],
               r=[pbuf[6 + i]], w=[b_kT])
            yield

        for hk in range(4):
            i2 = hk % 2
            load_w(qslab[i2], w_in, 256 * hk, 256, [b_qs[i2], b_kvs])
            for ti in range(NT):
                i = ti % 2
                for k in range(KT):
                    MM(pf(0, 256, i * 256), hT[:, k, ti * 128:(ti + 1) * 128], qslab[i2][:, k, :], start=(k == 0),
                       stop=(k == KT - 1), r=[b_hT[ti], b_qs[i2]],
                       w=([qpb[i], pbuf[0]] if (hk == 0 and ti < 2) else [qpb[i]]), inc=(k == KT - 1))
                rope(pf(0, 256, i * 256).rearrange("p (h d) -> p h d", h=4), 4, ti, rR[i], qpb[i], b_rR[i])
                for h in range(4):
                    TR(pb(6 + i)[0:64, h * 128:(h + 1) * 128], rR[i][:, h, :], ident, r=[b_rR[i], bK],
                       w=[pbuf[6 + i]], inc=(h == 3))
                CP("act", qT[:, :, ti * 128:(ti + 1) * 128], pb(6 + i, 512).rearrange("p (h c) -> p h c", h=4)[0:64],
                   r=[pbuf[6 + i]], w=[b_qT])
                yield
            def partA(n, hk=hk):
                i = n % 2
                cur = slice(n * 128, (n + 1) * 128)
                prv = slice(T, T + 128) if n == 0 else slice((n - 1) * 128, n * 128)
                Sps = PF[:, 1024:2048].rearrange("p (g k) -> p g k", g=4)
                for g in range(4):
                    MM(Sps[:, g, 0:128], qT[:, g, cur], kT[:, hk, prv], r=[b_qT, b_kT], w=[pbuf[2 + g // 2]])
                    MM(Sps[:, g, 128:256], qT[:, g, cur], kT[:, hk, cur], r=[b_qT, b_kT], w=[pbuf[2 + g // 2]],
                       inc=(g % 2 == 1))
                yield
                CP("act", S_sb[i], Sps, r=[pbuf[2], pbuf[3]], w=[b_S[i]])
                TT("dve", S_sb[i], S_sb[i], bc_mid(band0 if n == 0 else band, 4), ALU.add, r=[b_S[i], bK],
                   w=[b_S[i]])
                s_ = sm[i]
                rmax, mm_, negm, d2, rs, es, den, rden = [s_[:, 4 * j:4 * j + 4] for j in range(8)]
                P.add("dve", lambda e, o=rmax, a=S_sb[i]: e.tensor_reduce(out=o, in_=a, axis=AX.X, op=ALU.max),
                      [b_S[i]], [b_sm[i]])
                STT(mm_, rmax, 0.125, sinkbc[:, 4 * hk:4 * hk + 4], ALU.mult, ALU.max, r=[b_sm[i], bK], w=[b_sm[i]])
                TS("dve", negm, mm_, -1.0, ALU.mult, r=[b_sm[i]], w=[b_sm[i]])
                TT("dve", d2, sinkbc[:, 4 * hk:4 * hk + 4], negm, ALU.add, r=[b_sm[i], bK], w=[b_sm[i]])
                for g in range(4):
                    ACT(Pe[i][:, g, :], S_sb[i][:, g, :], AF.Exp, r=[b_S[i], b_sm[i]], w=[b_Pe[i], b_sm[i]],
                        bias=negm[:, g:g + 1], scale=0.125, accum=rs[:, g:g + 1])
                ACT(es, d2, AF.Exp, r=[b_sm[i]], w=[b_sm[i]])
                TT("dve", den, rs, es, ALU.add, r=[b_sm[i]], w=[b_sm[i]])
                RECIP(rden, den, r=[b_sm[i]], w=[b_sm[i]])
                TT("dve", Pn[i], Pe[i], bc_last(rden, 256), ALU.mult, r=[b_Pe[i], b_sm[i]], w=[b_Pn[i]])

            def partB(n, hk=hk):
                i = n % 2
                cur = slice(n * 128, (n + 1) * 128)
                vprev = NT if n == 0 else n - 1
                for g in range(4):
                    for kb in range(2):
                        j = g * 2 + kb
                        TR(pb(6 + i)[:, j * 128:(j + 1) * 128], Pn[i][:, g, kb * 128:(kb + 1) * 128], ident,
                           r=[b_Pn[i], bK], w=[pbuf[6 + i]], inc=(j == 7))
                CP("act", PTs[i], pb(6 + i).rearrange("p (j c) -> p j c", j=8), r=[pbuf[6 + i]], w=[b_PT[i]])
                yield
                Ops = pf(4, 256, i * 256).rearrange("p (a c) -> p a c", a=2)
                for g in range(4):
                    half = g % 2
                    for kb in range(2):
                        vt = vprev if kb == 0 else n
                        MM(Ops[64 * half:64 * half + 64, g // 2, :], vA[:, vt, hk * 64:(hk + 1) * 64],
                           PTs[i][:, g * 2 + kb, :], start=(kb == 0), stop=(kb == 1), r=[b_vA, b_PT[i]],
                           w=[Obuf[i]], inc=(g == 3 and kb == 1), tp=(0, 64 * half))
                CP("act", ost[i], Ops, r=[Obuf[i]], w=[b_ost[i]])
                DMA("sp", oaT_d[hk * 256:(hk + 1) * 256, cur].rearrange("(a p) c -> p a c", p=128), ost[i],
                    r=[b_ost[i]], w=[b_oaT])

            yield from partA(0)
            for n in range(NT):
                if n + 1 < NT:
                    yield from partA(n + 1)
                yield from partB(n)
                yield


    P.stage = "stage_2b"
    slab = [A([128, KT, 256], BF16) for _ in range(2)]
    banks2b = [1, 5]
    b_slab = [Buf(), Buf()]
    stg = [A([128, 512]) for _ in range(4)]
    b_stg = [Buf() for _ in range(4)]
    cnt = {"slab": 0, "stg": 0, "bank": 0}
    b_scr = {}

    def sbuf_for(name):
        if name not in b_scr:
            b_scr[name] = Buf()
        return b_scr[name]

    def gemm_fm(w, c0, ncols, act_T, act_bufs_fn, func, dst, dst_row0, out_dt, name):
        for s0 in range(0, ncols, 256):
            n = min(256, ncols - s0)
            si = cnt["slab"] % 2
            cnt["slab"] += 1
            kt = act_T.shape[1]
            load_w(slab[si][:, 0:kt, 0:n], w, c0 + s0, n, [b_slab[si]])
            for ct in range(0, n, 128):
                for g in range(NG):
                    bk = banks2b[cnt["bank"] % 2]
                    cnt["bank"] += 1
                    for k in range(kt):
                        MM(pf(bk), slab[si][:, k, ct:ct + 128], act_T[:, k, g * 512:(g + 1) * 512], start=(k == 0),
                           stop=(k == kt - 1), r=[b_slab[si]] + act_bufs_fn(g), w=[pbuf[bk]], inc=(k == kt - 1))
                    sj = cnt["stg"] % 4
                    cnt["stg"] += 1
                    o = stg[sj] if out_dt == F32 else stg[sj].bitcast(BF16)[:, 0:512]
                    if func == AF.Copy and (cnt["stg"] % 2 == 0):
                        CP("dve", o, pf(bk), r=[pbuf[bk]], w=[b_stg[sj]])
                    else:
                        ACT(o, pf(bk), func, r=[pbuf[bk]], w=[b_stg[sj]])
                    r0 = dst_row0 + s0 + ct
                    DMA("sp", dst[r0:r0 + 128, g * 512:(g + 1) * 512], o, r=[b_stg[sj]], w=[sbuf_for(name)])
                    yield

    def gemm_tm(w, c0, ncols, func, dst, dst_c0, name):
        for s0 in range(0, ncols, 256):
            n = min(256, ncols - s0)
            si = cnt["slab"] % 2
            cnt["slab"] += 1
            load_w(slab[si][:, :, 0:n], w, c0 + s0, n, [b_slab[si]])
            for ti in range(NT):
                bk = banks2b[cnt["bank"] % 2]
                cnt["bank"] += 1
                for k in range(KT):
                    MM(pf(bk, n), hT[:, k, ti * 128:(ti + 1) * 128], slab[si][:, k, 0:n], start=(k == 0),
                       stop=(k == KT - 1), r=[b_slab[si], b_hT[ti]], w=[pbuf[bk]], inc=(k == KT - 1))
                sj = cnt["stg"] % 4
                cnt["stg"] += 1
                o = stg[sj].bitcast(BF16)[:, 0:n]
                if func == AF.Copy and (cnt["stg"] % 2 == 0):
                    CP("dve", o, pf(bk, n), r=[pbuf[bk]], w=[b_stg[sj]])
                else:
                    ACT(o, pf(bk, n), func, r=[pbuf[bk]], w=[b_stg[sj]])
                DMA("sp", dst[ti * 128:(ti + 1) * 128, dst_c0 + s0:dst_c0 + s0 + n], o, r=[b_stg[sj]],
                    w=[sbuf_for(name)])
                yield

    def hT_bufs(g):
        return b_hT[g * 4:(g + 1) * 4]

    def g2b_gen():
        gslab = slab[0][:, :, 0:16]
        load_w(gslab, w_in, 5632, 16, [b_slab[0]])
        cnt["slab"] += 1
        for g in range(NG):
            bk = banks2b[cnt["bank"] % 2]
            cnt["bank"] += 1
            for k in range(KT):
                MM(pf(bk)[0:16, :], gslab[:, k, :], hT[:, k, g * 512:(g + 1) * 512], start=(k == 0), stop=(k == KT - 1),
                   r=[b_slab[0]] + hT_bufs(g), w=[pbuf[bk]], inc=(k == KT - 1))
            CP("act", gkT[0:16, g * 512:(g + 1) * 512], pf(bk)[0:16, :], r=[pbuf[bk]], w=[b_gkT])
            yield
        yield from gemm_fm(w_in, 1536, 1024, hT, hT_bufs, AF.Copy, qbT_d, 0, F32, "qbT")
        yield from gemm_fm(w_in, 2560, 1024, hT, hT_bufs, AF.Copy, kbT_d, 0, F32, "kbT")
        yield from gemm_tm(w_in, 3584, 2048, AF.Copy, vb_d, 0, "vb")
        yield from gemm_tm(w_in, 5648, 2048, AF.Silu, sog_d, 0, "sog")
        yield from gemm_fm(w_in, 7696, 2048, hT, hT_bufs, AF.Sigmoid, sgaT_d, 0, BF16, "sgaT")
        yield from gemm_fm(w_in, 9744, 2048, hT, hT_bufs, AF.Sigmoid, sgbT_d, 0, BF16, "sgbT")

    g1, g2 = swa_gen(), g2b_gen()
    a1 = a2 = True
    acc = 0.0
    RATIO = 1.65
    while a1 or a2:
        if a1:
            P.stage = "stage_2a"
            try:
                next(g1)
            except StopIteration:
                a1 = False
        acc += RATIO if a1 else 1e9
        P.stage = "stage_2b"
        while a2 and acc >= 1.0:
            try:
                next(g2)
            except StopIteration:
                a2 = False
            acc -= 1.0
        if not a2:
            acc = 0.0

    if stage_end():
        return nc

    P.stage = "stage_3"
    A.off = PERSIST
    Sst = A([128, 8, 512])
    Sbf = A([128, 8, 512], BF16)
    b_Sst, b_Sbf = Buf(), Buf()
    qcT = A([128, 8, T], BF16)
    b_qcT = Buf()
    cumB = A([128, 8])
    S3_KEEP = A.off
    ecum = A([128, 8])
    dec = [A([128, 8]) for _ in range(2)]
    b_cum = Buf()
    b_dec = [Buf(), Buf()]
    qf = [A([128, 8, 128]) for _ in range(2)]
    kf = [A([128, 8, 128]) for _ in range(2)]
    b_qf = [Buf(), Buf()]
    b_kf = [Buf(), Buf()]
    vch = [A([128, 2048], BF16) for _ in range(2)]
    b_vch = [Buf(), Buf()]
    G = [A([128, 8, 128]) for _ in range(6)]
    b_G = [Buf() for _ in range(6)]
    qi = [A([128, 8, 128], BF16) for _ in range(2)]
    ki = [A([128, 8, 128], BF16) for _ in range(2)]
    qn = [A([128, 8, 128], BF16) for _ in range(2)]
    ks = [A([128, 8, 128], BF16) for _ in range(2)]
    b_qi, b_ki, b_qn, b_ks = [[Buf(), Buf()] for _ in range(4)]
    ATs = A([128, 4, 128], BF16)
    b_AT = Buf()
    ATf = A([128, 4, 128])
    b_ATf = Buf()
    ksT = A([128, 1024], BF16)
    b_ksT = Buf()
    olst = [A([128, 1024]) for _ in range(2)]
    b_olst = [Buf(), Buf()]
    b_oloc = Buf()
    P.add("pool", lambda e: e.memset(Sst.rearrange("p a b -> p (a b)"), 0.0), w=[b_Sst])
    P.add("pool", lambda e: e.memset(Sbf.rearrange("p a b -> p (a b)"), 0.0), w=[b_Sbf])
    P.add("pool", lambda e: e.memset(cumB, 0.0), w=[b_cum])
    DKS = 256 ** -0.5

    def F2(a):
        return a.rearrange("p a b -> p (a b)")

    def gla_prep(c):
        i = c % 2
        cc = slice(c * 128, (c + 1) * 128)
        DMA("sp", qf[i], qbT_d[:, cc].rearrange("(j p) c -> p j c", p=128), r=[sbuf_for("qbT")], w=[b_qf[i]])
        DMA("sp", kf[i], kbT_d[:, cc].rearrange("(j p) c -> p j c", p=128), r=[sbuf_for("kbT")], w=[b_kf[i]])
        DMA("sp", vch[i], vb_d[cc, :], r=[sbuf_for("vb")], w=[b_vch[i]])
        zps = PF[:, 0:1024].rearrange("p (j c) -> p j c", j=8)
        for j in range(8):
            MM(zps[:, j, :], wgk[:, j * 128:(j + 1) * 128], gkT[:, cc], r=[bK, b_gkT], w=[pbuf[j // 4]],
               inc=(j % 4 == 3))
        zb = [pbuf[0], pbuf[1]]
        CP("act", G[0], zps, r=zb, w=[b_G[0]])
        ACT(F2(G[1]), F2(G[0]), AF.Abs, r=[b_G[0]], w=[b_G[1]])
        ACT(F2(G[1]), F2(G[1]), AF.Exp, r=[b_G[1]], w=[b_G[1]], scale=-1.0)
        ACT(F2(G[1]), F2(G[1]), AF.Ln, r=[b_G[1]], w=[b_G[1]], bias=1.0)
        TS("dve", F2(G[0]), F2(G[0]), 0.0, ALU.min, r=[b_G[0]], w=[b_G[0]])
        TT("pool", G[0].rearrange("p a b -> p (a b)"), G[0].rearrange("p a b -> p (a b)"), G[1].rearrange("p a b -> p (a b)"), ALU.subtract, r=[b_G[0], b_G[1]], w=[b_G[0]])
        P.add("dve", lambda e: e.tensor_tensor_scan(out=G[2].rearrange("p a b -> p (a b)"), data0=scanm,
                                                     data1=G[0].rearrange("p a b -> p (a b)"), initial=0.0,
                                                     op0=ALU.mult, op1=ALU.add), [b_G[0], bK], [b_G[2]])
        TT("dve", G[3], G[2], G[2][:, :, 64:65].broadcast_to([128, 8, 128]), ALU.subtract, r=[b_G[2]], w=[b_G[3]])
        TT("dve", G[4], G[2][:, :, 127:128].broadcast_to([128, 8, 128]), G[2], ALU.subtract, r=[b_G[2]],
           w=[b_G[4]])
        ACT(dec[i], G[2][:, :, 127], AF.Exp, r=[b_G[2]], w=[b_dec[i]], scale=1.0 / 16)
        ACT(ecum, cumB, AF.Exp, r=[b_cum], w=[b_cum], scale=1.0 / 16)
        ACT(F2(G[5]), F2(G[3]), AF.Exp, r=[b_G[3]], w=[b_G[5]], scale=1.0 / 16)
        STT(F2(qi[i]), F2(qf[i]), DKS, F2(G[5]), ALU.mult, ALU.mult, r=[b_qf[i], b_G[5]], w=[b_qi[i]])
        ACT(F2(G[5]), F2(G[3]), AF.Exp, r=[b_G[3]], w=[b_G[5]], scale=-1.0 / 16)
        TT("dve", F2(ki[i]), F2(kf[i]), F2(G[5]), ALU.mult, r=[b_kf[i], b_G[5]], w=[b_ki[i]])
        ACT(F2(G[3]), F2(G[2]), AF.Exp, r=[b_G[2]], w=[b_G[3]], scale=1.0 / 16)
        STT(F2(qn[i]), F2(qf[i]), DKS, F2(G[3]), ALU.mult, ALU.mult, r=[b_qf[i], b_G[3]], w=[b_qn[i]])
        ACT(F2(G[4]), F2(G[4]), AF.Exp, r=[b_G[4]], w=[b_G[4]], scale=1.0 / 16)
        TT("pool", ks[i].rearrange("p a b -> p (a b)"), kf[i].rearrange("p a b -> p (a b)"), G[4].rearrange("p a b -> p (a b)"), ALU.mult, r=[b_kf[i], b_G[4]], w=[b_ks[i]])
        TT("dve", qcT[:, :, cc], qn[i], bc_last(ecum, 128), ALU.mult, r=[b_qn[i], b_cum], w=[b_qcT])
        TT("dve", cumB, cumB, G[2][:, :, 127], ALU.add, r=[b_cum, b_G[2]], w=[b_cum])

    def gla_pe(c):
        i = c % 2
        ATp = pf(2).rearrange("p (h c) -> p h c", h=4)
        for h in range(4):
            for dt_ in range(2):
                j = h * 2 + dt_
                MM(ATp[:, h, :], ki[i][:, j, :], qi[i][:, j, :], start=(dt_ == 0), stop=(dt_ == 1),
                   r=[b_ki[i], b_qi[i]], w=[pbuf[2]], inc=(j == 7))
        CP("act", ATf, ATp, r=[pbuf[2]], w=[b_ATf])
        TT("dve", ATs, ATf, bc_mid(tri, 4), ALU.mult, r=[b_ATf, bK], w=[b_AT])
        for j in range(8):
            TR(pb(6)[:, j * 128:(j + 1) * 128], ks[i][:, j, :], ident, r=[b_ks[i], bK], w=[pbuf[6]], inc=(j == 7))
        CP("act", ksT, pb(6), r=[pbuf[6]], w=[b_ksT])
        for hp in range(2):
            for hh in range(2):
                h = hp * 2 + hh
                bk = 3 + hh
                MM(pf(bk), ATs[:, h, :], vch[i][:, h * 512:(h + 1) * 512], start=True, stop=False,
                   r=[b_AT, b_vch[i]], w=[pbuf[bk]])
                MM(pf(bk), qn[i][:, 2 * h, :], Sbf[:, 2 * h, :], start=False, stop=False, r=[b_qn[i], b_Sbf],
                   w=[pbuf[bk]])
                MM(pf(bk), qn[i][:, 2 * h + 1, :], Sbf[:, 2 * h + 1, :], start=False, stop=True,
                   r=[b_qn[i], b_Sbf], w=[pbuf[bk]], inc=True)
            CP("act", olst[hp], PF[:, 3 * 512:5 * 512], r=[pbuf[3], pbuf[4]], w=[b_olst[hp]])
            DMA("sp", oloc_d[c * 128:(c + 1) * 128, hp * 1024:(hp + 1) * 1024], olst[hp], r=[b_olst[hp]],
                w=[b_oloc])

    def gla_update(c):
        i = c % 2
        banks = [5, 0, 1]
        for j in range(8):
            h = j // 2
            bk = banks[j % 3]
            MM(pf(bk), ksT[:, j * 128:(j + 1) * 128], vch[i][:, h * 512:(h + 1) * 512], r=[b_ksT, b_vch[i]],
               w=[pbuf[bk]], inc=True)
            STT(Sst[:, j, :], Sst[:, j, :], dec[i][:, j:j + 1], pf(bk), ALU.mult, ALU.add,
                r=[b_Sst, b_dec[i], pbuf[bk]], w=[b_Sst])
        CP("pool", Sbf.rearrange("p a b -> p (a b)"), Sst.rearrange("p a b -> p (a b)"), r=[b_Sst], w=[b_Sbf])

    m_slab = A([128, KT, 512], BF16)
    m_bm = A([1, 512])
    m_row = A([1, 512])
    b_mslab, b_mbm, b_mrow = Buf(), Buf(), Buf()
    spc = (16 + NT - 1) // NT
    next_j = [8]
    gla_prep(0)
    for c in range(NT):
        gla_pe(c)
        if c + 1 < NT:
            gla_prep(c + 1)
        gla_update(c)
        for _ in range(spc):
            if next_j[0] < 24:
                mod_slab(next_j[0], m_slab, b_mslab, m_bm, b_mbm, m_row, b_mrow, 2)
                next_j[0] += 1
    while next_j[0] < 24:
        mod_slab(next_j[0], m_slab, b_mslab, m_bm, b_mbm, m_row, b_mrow, 2)
        next_j[0] += 1
    b_sloc = [Buf() for _ in range(4)]
    b_sall = [Buf() for _ in range(4)]
    for q in range(4):
        DMA("sp", sloc_q[q][:, 0:512].rearrange("(j p) e -> p j e", p=128), Sst[:, 2 * q:2 * q + 2, :], r=[b_Sst],
            w=[b_sloc[q]])
        DMA("sp", sloc_q[q][:, 512:513].rearrange("(j p) e -> p j e", p=128), cumB[:, 2 * q:2 * q + 2].unsqueeze(2),
            r=[b_cum], w=[b_sloc[q]], slow=True)
        if solo:
            for i3 in range(4):
                DMA("sp", sall_q[q][i3 * 256:(i3 + 1) * 256, :], sloc_q[q], r=[b_sloc[q]], w=[b_sall[q]])
        else:
            P.add("pool", lambda e, q=q: e.collective_compute("AllGather", ALU.bypass,
                                                              replica_groups=[[0, 1, 2, 3], [4, 5, 6, 7]],
                                                              ins=[sloc_q[q].opt()], outs=[sall_q[q].opt()]),
                  [b_sloc[q]], [b_sall[q]], cc=True)

    if stage_end():
        return nc
    A.off = S3_KEEP
    P.stage = "stage_3b"
    Sin = Sst
    sl = [A([128, 8, 520]) for _ in range(2)]
    b_sl = [Buf(), Buf()]
    De = A([128, 8])
    b_De = Buf()
    P.add("pool", lambda e: e.memset(Sin.rearrange("p a b -> p (a b)"), 0.0), r=[b_Sst], w=[b_Sst])
    for i3 in range(3):
        i = i3 % 2
        for q in range(4):
            DMA("sp", sl[i][:, 2 * q:2 * q + 2, :],
                sall_q[q][i3 * 256:(i3 + 1) * 256, :].rearrange("(j p) e -> p j e", p=128), r=[b_sall[q]],
                w=[b_sl[i]])
        ACT(De, sl[i][:, :, 512], AF.Exp, r=[b_sl[i], bK], w=[b_De], scale=cfg[:, 8 + i3:9 + i3])
        for j in range(8):
            eng = "dve"
            TS(eng, Sin[:, j, :], Sin[:, j, :], De[:, j:j + 1], ALU.mult, r=[b_Sst, b_De], w=[b_Sst])
            STT(Sin[:, j, :], sl[i][:, j, 0:512], cfg[:, 2 + i3:3 + i3], Sin[:, j, :], ALU.mult, ALU.add,
                r=[b_sl[i], b_Sst, bK], w=[b_Sst])
    CP("act", Sbf.rearrange("p a b -> p (a b)"), Sin.rearrange("p a b -> p (a b)"), r=[b_Sst], w=[b_Sbf])

    P.stage = "stage_3c"
    olt = [A([128, 2048]) for _ in range(2)]
    b_olt = [Buf(), Buf()]
    sogt = [A([128, 2048], BF16) for _ in range(2)]
    b_sogt = [Buf(), Buf()]
    gno = A([128, 4, 512])
    b_gno = Buf()
    osum = A([128, 4, 512])
    b_osum = Buf()
    junk3 = A([128, 512], BF16)
    b_junk3 = Buf()
    obb = [A([128, 2048], BF16) for _ in range(2)]
    b_obb = [Buf(), Buf()]
    obst = [A([128, KT, 128], BF16) for _ in range(2)]
    b_obst = [Buf(), Buf()]
    sm3 = [A([128, 16]) for _ in range(2)]
    b_sm3 = [Buf(), Buf()]
    b_obT = Buf()
    for c in range(NT):
        i = c % 2
        cc = slice(c * 128, (c + 1) * 128)
        DMA("sp", olt[i], oloc_d[cc, :], r=[b_oloc], w=[b_olt[i]])
        DMA("sp", sogt[i], sog_d[cc, :], r=[sbuf_for("sog")], w=[b_sogt[i]])
        for h in range(4):
            for dt_ in range(2):
                MM(pf(h), qcT[:, 2 * h + dt_, cc], Sbf[:, 2 * h + dt_, :], start=(dt_ == 0), stop=(dt_ == 1),
                   r=[b_qcT, b_Sbf], w=[pbuf[h]], inc=(dt_ == 1))
        TT("dve", osum.rearrange("p a b -> p (a b)"), PF[:, 0:2048], olt[i], ALU.add,
           r=[pbuf[0], pbuf[1], pbuf[2], pbuf[3], b_olt[i]], w=[b_osum])
        TT("dve", gno, sogt[i].rearrange("p (h e) -> p h e", h=4), bc_mid(glan, 4), ALU.mult,
           r=[b_sogt[i], bK], w=[b_gno])
        ssq4, tm4, rs4 = sm3[i][:, 0:4], sm3[i][:, 4:8], sm3[i][:, 8:12]
        for h in range(4):
            ACT(junk3, osum[:, h, :], AF.Square, r=[b_osum], w=[b_junk3, b_sm3[i]], accum=ssq4[:, h:h + 1])
        TS("dve", tm4, ssq4, 1.0 / 512, ALU.mult, EPS, ALU.add, r=[b_sm3[i]], w=[b_sm3[i]])
        ACT(tm4, tm4, AF.Sqrt, r=[b_sm3[i]], w=[b_sm3[i]])
        RECIP(rs4, tm4, r=[b_sm3[i]], w=[b_sm3[i]])
        for h in range(4):
            STT(obb[i][:, h * 512:(h + 1) * 512], osum[:, h, :], rs4[:, h:h + 1], gno[:, h, :], ALU.mult, ALU.mult,
                r=[b_osum, b_sm3[i], b_gno], w=[b_obb[i]])
        transpose16(obb[i], b_obb[i], lambda h, i=i: obst[i][:, h * 8:(h + 1) * 8, :], [b_obst[i]])
        DMA("sp", obT_d[:, cc].rearrange("(k p) c -> p k c", p=128), obst[i], r=[b_obst[i]], w=[b_obT])

    if stage_end():
        return nc

    P.stage = "stage_4a"
    A.off = PERSIST
    Wa = A([128, 8, 2048], BF16)
    Wb = A([128, 16, 2048], BF16)
    b_W = Buf()
    oag = [A([128, 8, 512], BF16) for _ in range(2)]
    obg = [A([128, 16, 512], BF16) for _ in range(2)]
    b_oag = [Buf(), Buf()]
    b_obg = [Buf(), Buf()]
    sga = [A([128, 512], BF16) for _ in range(2)]
    sgb = [A([128, 512], BF16) for _ in range(2)]
    b_sga = [Buf(), Buf()]
    b_sgb = [Buf(), Buf()]
    t1 = [A([128, 512]) for _ in range(2)]
    t2 = [A([128, 512]) for _ in range(2)]
    b_t1 = [Buf(), Buf()]
    b_t2 = [Buf(), Buf()]
    mst = [A([128, 512], BF16) for _ in range(2)]
    b_mst = [Buf(), Buf()]
    b_mgT = Buf()
    for q4 in range(4):
        load_w(Wa[:, :, q4 * 512:(q4 + 1) * 512], w_bra, q4 * 512, 512, [b_W])
        load_w(Wb[:, :, q4 * 512:(q4 + 1) * 512], w_brb, q4 * 512, 512, [b_W])
    it = 0
    for g in range(NG):
        gi = g % 2
        gs = slice(g * 512, (g + 1) * 512)
        DMA("sp", oag[gi], oaT_d[:, gs].rearrange("(k p) c -> p k c", p=128), r=[b_oaT], w=[b_oag[gi]])
        DMA("sp", obg[gi], obT_d[:, gs].rearrange("(k p) c -> p k c", p=128), r=[b_obT], w=[b_obg[gi]])
        for f in range(16):
            i = it % 2
            it += 1
            fs = slice(f * 128, (f + 1) * 128)
            DMA("sp", sga[i], sgaT_d[fs, gs], r=[sbuf_for("sgaT")], w=[b_sga[i]])
            DMA("sp", sgb[i], sgbT_d[fs, gs], r=[sbuf_for("sgbT")], w=[b_sgb[i]])
            ba, bb = (0, 1) if i == 0 else (2, 3)
            for k in range(8):
                MM(pf(ba), Wa[:, k, fs], oag[gi][:, k, :], start=(k == 0), stop=(k == 7), r=[b_W, b_oag[gi]],
                   w=[pbuf[ba]], inc=(k == 7))
            for k in range(16):
                MM(pf(bb), Wb[:, k, fs], obg[gi][:, k, :], start=(k == 0), stop=(k == 15), r=[b_W, b_obg[gi]],
                   w=[pbuf[bb]], inc=(k == 15))
            TT("dve", t1[i], pf(ba), sga[i], ALU.mult, r=[pbuf[ba], b_sga[i]], w=[b_t1[i]])
            TT("dve", t2[i], pf(bb), sgb[i], ALU.mult, r=[pbuf[bb], b_sgb[i]], w=[b_t2[i]])
            TT("pool", mst[i], t1[i], t2[i], ALU.add, r=[b_t1[i], b_t2[i]], w=[b_mst[i]])
            DMA("sp", mgT_d[fs, gs], mst[i], r=[b_mst[i]], w=[b_mgT])

    if stage_end():
        return nc

    P.stage = "stage_4b"
    A.off = PERSIST
    Wo = A([128, KT, 2048], BF16)
    b_Wo = Buf()
    bcA = A([128, D])
    bcB = A([128, D])
    bcC = A([128, D])
    tmpf = A([128, D])
    b_bc = Buf()
    b_tmpf = Buf()
    junk = A([128, D], BF16)
    b_junk = Buf()
    mgt = [A([128, KT, 128], BF16) for _ in range(2)]
    b_mgt = [Buf(), Buf()]
    xt = [A([128, D]) for _ in range(2)]
    b_xt = [Buf(), Buf()]
    x1t = A([128, D])
    b_x1t = Buf()
    hb = [A([128, D], BF16) for _ in range(1)]
    b_hb = [Buf()]
    h2st = [A([128, KT, 128], BF16) for _ in range(2)]
    b_h2st = [Buf(), Buf()]
    b_x1 = Buf()
    b_h2T = Buf()
    for q4 in range(4):
        load_w(Wo[:, :, q4 * 512:(q4 + 1) * 512], w_out, q4 * 512, 512, [b_Wo])
    DMA("sp", bcA, mod_d[2 * D:3 * D].partition_broadcast(128), r=[b_mod], w=[b_bc])
    DMA("sp", tmpf, npost1.partition_broadcast(128), w=[b_tmpf])
    TT("dve", bcA, bcA, tmpf, ALU.mult, r=[b_bc, b_tmpf], w=[b_bc])
    DMA("sp", bcB, npre2.partition_broadcast(128), w=[b_bc])
    DMA("sp", tmpf, mod_d[4 * D:5 * D].partition_broadcast(128), r=[b_mod, b_bc], w=[b_tmpf])
    STT(bcB, tmpf, 1.0, bcB, ALU.add, ALU.mult, r=[b_bc, b_tmpf], w=[b_bc])
    DMA("sp", bcC, mod_d[3 * D:4 * D].partition_broadcast(128), r=[b_mod], w=[b_bc])

    def norm2_and_T(src, np_, b_src, dst_fn, b_dst, sidx):
        norm_tile(src, np_, bcB, bcC, hb[0][0:np_], b_src, b_hb[0], sidx)
        transpose16(hb[0][0:np_], b_hb[0], dst_fn, b_dst, np_=np_)

    order = [NT - 1] + list(range(NT - 1))
    b_xl = Buf()
    b_xla = Buf()
    ysb = A([128, D])
    b_ysb = Buf()

    def mm4b(n_):
        ti = order[n_]
        i = n_ % 2
        cc = slice(ti * 128, (ti + 1) * 128)
        DMA("sp", mgt[i], mgT_d[:, cc].rearrange("(k p) c -> p k c", p=128), r=[b_mgT], w=[b_mgt[i]])
        DMA("sp", xt[i], x_d[cc, :], w=[b_xt[i]])
        for s4 in range(4):
            for k in range(KT):
                MM(pf(s4), mgt[i][:, k, :], Wo[:, k, s4 * 512:(s4 + 1) * 512], start=(k == 0), stop=(k == KT - 1),
                   r=[b_mgt[i], b_Wo], w=[pbuf[s4]], inc=(k == KT - 1))
        CP("act", ysb, PF[:, 0:2048], r=[pbuf[0], pbuf[1], pbuf[2], pbuf[3]], w=[b_ysb])

    mm4b(0)
    for n_, ti in enumerate(order):
        i = n_ % 2
        cc = slice(ti * 128, (ti + 1) * 128)
        ss, tm, rs = small[:, 16:17], small[:, 17:18], small[:, 18:19]
        ACT(junk, ysb, AF.Square, r=[b_ysb], w=[b_junk, b_small], accum=ss)
        rms_scale(ss, tm, rs, r=[b_small], w=[b_small])
        STT(tmpf, ysb, rs, bcA, ALU.mult, ALU.mult, r=[b_ysb, b_small, b_bc], w=[b_tmpf])
        if n_ + 1 < NT:
            mm4b(n_ + 1)
        TT("pool", x1t, tmpf, xt[i], ALU.add, r=[b_tmpf, b_xt[i]], w=[b_x1t])
        DMA("sp", x1_d[cc, :], x1t, r=[b_x1t], w=[b_x1])
        if n_ == 0:
            DMA("sp", xl_d, x1t[126:128, :], r=[b_x1t], w=[b_xl])
            if solo:
                for i3 in range(4):
                    DMA("sp", xla_d[2 * i3:2 * i3 + 2, :], xl_d, r=[b_xl], w=[b_xla])
            else:
                P.add("pool", lambda e: e.collective_compute("AllGather", ALU.bypass,
                                                             replica_groups=[[0, 1, 2, 3], [4, 5, 6, 7]],
                                                             ins=[xl_d.opt()], outs=[xla_d.opt()]), [b_xl], [b_xla],
                      cc=True)
        norm2_and_T(x1t, 128, b_x1t, lambda h, i=i: h2st[i][:, h * 8:(h + 1) * 8, :], [b_h2st[i]], 20)
        DMA("sp", h2T_d[:, cc].rearrange("(k p) c -> p k c", p=128), h2st[i], r=[b_h2st[i]], w=[b_h2T])
    xc = Wo.rearrange("p a b -> p (a b)")[0:2, :].bitcast(F32)[:, 0:3 * D].rearrange("p (a b) -> p a b", a=3)
    b_xc = Buf()
    xhh = x1t[0:2, :]
    DMA("sp", xc, xla_d[0:6, :].rearrange("(i r) d -> r i d", r=2), r=[b_xla], w=[b_xc, b_Wo])
    TS("dve", xhh, xc[:, 0, :], cfg[0:2, 5:6], ALU.mult, r=[b_xc, bK], w=[b_x1t])
    for i3 in (1, 2):
        STT(xhh, xc[:, i3, :], cfg[0:2, 5 + i3:6 + i3], xhh, ALU.mult, ALU.add, r=[b_xc, bK, b_x1t], w=[b_x1t])
    hst = A([128, KT, 2], BF16)
    b_hst = Buf()
    norm2_and_T(xhh, 2, b_x1t, lambda h: hst[:, h * 8:(h + 1) * 8, :], [b_hst], 24)
    TS("dve", h2Th.rearrange("p a b -> p (a b)"), hst.rearrange("p a b -> p (a b)"), cfg[:, 0:1], ALU.mult,
       r=[b_hst, bK], w=[b_h2Th])

    if stage_end():
        return nc

    P.stage = "stage_6"
    A.off = PERSIST
    h2T = A([128, KT, T], BF16)
    b_h2 = [Buf() for _ in range(NG)]
    for g in range(NG):
        DMA("sp", h2T[:, :, g * 512:(g + 1) * 512], h2T_d[:, g * 512:(g + 1) * 512].rearrange("(k p) c -> p k c",
                                                                                                p=128),
            r=[b_h2T], w=[b_h2[g]])
    CH = min(1024, T)
    NCH = T // CH
    us = [[A([128, KT, 256], BF16) for _ in range(2)] for _ in range(2)]
    b_us = [[Buf(), Buf()] for _ in range(2)]
    U = [[A([128, 2 + T]) for _ in range(2)] for _ in range(2)]
    b_U = [[Buf(), Buf()] for _ in range(2)]
    Cb = [[A([128, CH]) for _ in range(2)] for _ in range(2)]
    b_C = [[Buf(), Buf()] for _ in range(2)]
    Gs = [A([128, CH]) for _ in range(2)]
    b_Gs = [Buf(), Buf()]
    ast = [A([128, CH], BF16) for _ in range(2)]
    b_ast = [Buf(), Buf()]
    b_actT = Buf()
    bkc = 0
    def load_us(sx):
        sj = sx % 2
        load_w(us[sj][0], w_up, sx * 256, 256, [b_us[sj][0]])
        load_w(us[sj][1], w_up, DFF + sx * 256, 256, [b_us[sj][1]])
    load_us(0)
    for f in range(FT):
        sidx = f // 2
        if f % 2 == 0 and sidx + 1 < FT // 2:
            load_us(sidx + 1)
        si = sidx % 2
        fo = (f % 2) * 128
        ui = f % 2
        for hv in range(2):
            Ub = U[ui][hv]
            fcol = f + hv * FT
            bk = bkc % 6
            bkc += 1
            for k in range(KT):
                MM(pf(bk, 2), us[si][hv][:, k, fo:fo + 128], h2Th[:, k, :], start=(k == 0), stop=(k == KT - 1),
                   r=[b_us[si][hv], b_h2Th], w=[pbuf[bk]], inc=(k == KT - 1))
            CP("act", Ub[:, 0:2], pf(bk, 2), r=[pbuf[bk]], w=[b_U[ui][hv]])
            for g in range(NG):
                bk = bkc % 6
                bkc += 1
                for k in range(KT):
                    MM(pf(bk), us[si][hv][:, k, fo:fo + 128], h2T[:, k, g * 512:(g + 1) * 512], start=(k == 0),
                       stop=(k == KT - 1), r=[b_us[si][hv], b_h2[g]], w=[pbuf[bk]], inc=(k == KT - 1))
                CP("act", Ub[:, 2 + g * 512:2 + (g + 1) * 512], pf(bk), r=[pbuf[bk]], w=[b_U[ui][hv]])
        for ch in range(NCH):
            ci = (f * NCH + ch) % 2
            o0 = ch * CH
            for hv in range(2):
                Ub = U[ui][hv]
                fcol = f + hv * FT
                Cc = Cb[ci][hv]
                ACT(Cc, Ub[:, 2 + o0:2 + o0 + CH], AF.Identity, r=[b_U[ui][hv], bK], w=[b_C[ci][hv]],
                    bias=convp[:, 3, fcol:fcol + 1], scale=convp[:, 2, fcol:fcol + 1])
                STT(Cc, Ub[:, 1 + o0:1 + o0 + CH], convp[:, 1, fcol:fcol + 1], Cc, ALU.mult, ALU.add,
                    r=[b_U[ui][hv], bK, b_C[ci][hv]], w=[b_C[ci][hv]])
                STT(Cc, Ub[:, o0:o0 + CH], convp[:, 0, fcol:fcol + 1], Cc, ALU.mult, ALU.add,
                    r=[b_U[ui][hv], bK, b_C[ci][hv]], w=[b_C[ci][hv]])
            ACT(Gs[ci], Cb[ci][0], AF.Silu, r=[b_C[ci][0]], w=[b_Gs[ci]])
            TT("pool", ast[ci], Gs[ci], Cb[ci][1], ALU.mult, r=[b_Gs[ci], b_C[ci][1]], w=[b_ast[ci]])
            DMA("sp", actT_d[f * 128:(f + 1) * 128, o0:o0 + CH], ast[ci], r=[b_ast[ci]], w=[b_actT])

    if stage_end():
        return nc

    P.stage = "stage_7"
    A.off = PERSIST
    TG = min(1024, T)
    NTG = T // TG
    ssq7 = A([128, NT, 8])
    S7_KEEP = A.off
    ag = [A([128, FT, 512], BF16) for _ in range(TG // 512)]
    b_ag = [Buf() for _ in range(TG // 512)]
    ds = [A([128, FT, 256], BF16) for _ in range(2)]
    b_ds = [Buf(), Buf()]
    yst = [A([128, 256]) for _ in range(4)]
    b_yst = [Buf() for _ in range(4)]
    b_ssq7 = Buf()
    junk7 = A([128, 256], BF16)
    b_junk7 = Buf()
    b_y2 = Buf()
    it = 0
    for tg in range(NTG):
        for hh in range(TG // 512):
            c0 = tg * TG + hh * 512
            for k0 in range(0, FT, 11):
                DMA("sp", ag[hh][:, k0:k0 + 11, :], actT_d[k0 * 128:(k0 + 11) * 128, c0:c0 + 512].rearrange(
                    "(k p) c -> p k c", p=128), r=[b_actT], w=[b_ag[hh]])
        for s8 in range(8):
            si = (tg * 8 + s8) % 2
            for k0 in range(0, FT, 11):
                DMA("pool", ds[si][:, k0:k0 + 11, :], wview(w_down, s8 * 256, 256)[:, k0:k0 + 11, :], w=[b_ds[si]])
            for tt in range(TG // 128):
                ti = tg * (TG // 128) + tt
                hh, toff = tt // 4, (tt % 4) * 128
                bk = it % 6
                sj = it % 4
                it += 1
                for k in range(FT):
                    MM(pf(bk, 256), ag[hh][:, k, toff:toff + 128], ds[si][:, k, :], start=(k == 0), stop=(k == FT - 1),
                       r=[b_ag[hh], b_ds[si]], w=[pbuf[bk]], inc=(k == FT - 1))
                CP("act", yst[sj], pf(bk, 256), r=[pbuf[bk]], w=[b_yst[sj]])
                DMA("sp", y2_d[ti * 128:(ti + 1) * 128, s8 * 256:(s8 + 1) * 256], yst[sj], r=[b_yst[sj]], w=[b_y2])

    if stage_end():
        return nc
    A.off = S7_KEEP
    P.stage = "stage_8"
    gp2 = A([128, D])
    tm8 = A([128, D])
    b_gp2 = Buf()
    b_tm8 = Buf()
    ss8 = A([128, NT])
    rs8 = A([128, NT])
    b_ss8 = Buf()
    y2t = [A([128, D]) for _ in range(2)]
    x1b = [A([128, D]) for _ in range(2)]
    b_y2t = [Buf(), Buf()]
    b_x1b = [Buf(), Buf()]
    b_out = Buf()
    DMA("sp", gp2, mod_d[5 * D:6 * D].partition_broadcast(128), r=[b_mod], w=[b_gp2])
    DMA("sp", tm8, npost2.partition_broadcast(128), w=[b_tm8])
    TT("dve", gp2, gp2, tm8, ALU.mult, r=[b_gp2, b_tm8], w=[b_gp2])
    junk8 = A([128, D], BF16)
    b_junk8 = Buf()
    for ti in range(NT):
        i = ti % 2
        cc = slice(ti * 128, (ti + 1) * 128)
        DMA("sp", y2t[i], y2_d[cc, :], r=[b_y2], w=[b_y2t[i]])
        DMA("sp", x1b[i], x1_d[cc, :], r=[b_x1], w=[b_x1b[i]])
        ACT(junk8, y2t[i], AF.Square, r=[b_y2t[i]], w=[b_junk8, b_ss8], accum=ss8[:, ti:ti + 1])
        TS("dve", ss8[:, ti:ti + 1], ss8[:, ti:ti + 1], 1.0 / D, ALU.mult, EPS, ALU.add, r=[b_ss8], w=[b_ss8])
        ACT(ss8[:, ti:ti + 1], ss8[:, ti:ti + 1], AF.Sqrt, r=[b_ss8], w=[b_ss8])
        RECIP(rs8[:, ti:ti + 1], ss8[:, ti:ti + 1], r=[b_ss8], w=[b_ss8])
        STT(y2t[i], y2t[i], rs8[:, ti:ti + 1], gp2, ALU.mult, ALU.mult, r=[b_y2t[i], b_ss8, b_gp2], w=[b_y2t[i]])
        TT("pool", x1b[i], x1b[i], y2t[i], ALU.add, r=[b_x1b[i], b_y2t[i]], w=[b_x1b[i]])
        DMA("sp", out_d[cc, :], x1b[i], r=[b_x1b[i]], w=[b_out])
    P.add("sp", None, r=[b_out], inc=False)
    P.barrier()
    P.emit()
    st.close()
    return nc


def host_constants():
    i = np.arange(128)[:, None]
    j = np.arange(256)[None, :]
    dist = 128 + i - j
    band = np.where((dist >= 0) & (dist < 128), 0.0, NEG).astype(np.float32)
    jj = np.arange(128)[:, None]
    ii = np.arange(128)[None, :]
    tri = (jj <= ii).astype(np.float32)
    scanm = np.ones((128, 1024), np.float32)
    scanm[:, ::128] = 0.0
    invf = (10000.0 ** (-np.arange(0, 64, 2, dtype=np.float32) / 64)).astype(np.float32)
    invf = np.broadcast_to(invf[None, :], (128, 32)).copy()
    ident = np.eye(128, dtype=np.float32)
    return dict(band=band, tri=tri, scanm=scanm, invf=invf, ident=ident)


def make_in_maps(inputs, T):
    x = np.asarray(inputs["x"], np.float32)
    c = np.asarray(inputs["c"], np.float32)
    pos = np.asarray(inputs["positions"], np.int32)
    NT = T // 128
    consts = host_constants()
    g = lambda k: np.ascontiguousarray(np.asarray(inputs[k])[0])
    shared = dict(
        w_mod=g("w_mod"), b_mod=g("b_mod"), mix_norm_pre=g("mix_norm_pre"), mix_norm_post=g("mix_norm_post"),
        w_in=g("w_in"), attn_sinks=g("attn_sinks"),
        wgk=np.ascontiguousarray(np.concatenate([g("w_gk_up"), g("b_gk")[None, :]], 0)),
        gla_norm=g("gla_norm"), w_branch_attn=g("w_branch_attn"), w_branch_gla=g("w_branch_gla"),
        w_out=g("w_out"), ffn_norm_pre=g("ffn_norm_pre"), ffn_norm_post=g("ffn_norm_post"), w_up=g("w_up"),
        w_down=g("w_down"), **consts)
    cw = g("conv_w")
    cb = g("conv_b")
    convp = np.stack([cw[0].reshape(88, 128).T, cw[1].reshape(88, 128).T, cw[2].reshape(88, 128).T,
                      cb.reshape(88, 128).T], axis=1)
    shared["convp"] = np.ascontiguousarray(convp.astype(np.float32))
    maps = []
    for r in range(8):
        b, p = r // 4, r % 4
        t0 = p * T
        m = dict(shared)
        m["x"] = np.ascontiguousarray(x[b, t0:t0 + T])
        m["xh"] = np.ascontiguousarray(x[b, t0 - 128:t0]) if p > 0 else np.zeros((128, D), np.float32)
        m["c"] = np.ascontiguousarray(c[b].reshape(KT, 128).T)
        pp = np.zeros((128, NT + 1), np.int32)
        pp[:, :NT] = pos[b, t0:t0 + T].reshape(NT, 128).T
        if p > 0:
            pp[:, NT] = pos[b, t0 - 128:t0]
        m["pos"] = pp
        cfg = np.zeros((128, 16), np.float32)
        cfg[:, 0] = 1.0 if p > 0 else 0.0
        cfg[:, 1] = 0.0 if p > 0 else NEG
        for i in range(3):
            cfg[:, 2 + i] = 1.0 if i < p else 0.0
            cfg[:, 5 + i] = 1.0 if i == p - 1 else 0.0
            cfg[:, 8 + i] = (1.0 / 16) if i < p else 0.0
        m["cfg"] = cfg
        maps.append(m)
    return maps


_CACHE = {}


def kernel(**inputs):
    S = np.asarray(inputs["x"]).shape[1]
    T = S // 4
    if T not in _CACHE:
        _CACHE[T] = build(T)
    nc = _CACHE[T]
    maps = make_in_maps(inputs, T)
    res = run_bass_kernel_spmd(nc, maps, core_ids=list(range(8)))
    out = np.zeros((2, S, D), np.float32)
    for r in range(8):
        b, p = r // 4, r % 4
        out[b, p * T:(p + 1) * T] = res.results[r]["out"]
    return out
```

```python
import contextlib
import os
import numpy as np
import concourse.bass as bass
import concourse.mybir as mybir
from concourse.bass_utils import run_bass_kernel_spmd

F32 = mybir.dt.float32
BF16 = mybir.dt.bfloat16
I32 = mybir.dt.int32
AF = mybir.ActivationFunctionType
ALU = mybir.AluOpType
AX = mybir.AxisListType

D = 2048
KT = 16
DFF = 5632
FT = DFF // 128
EPS = 1e-6
NEG = -30000.0
TWO_PI = float(2 * np.pi)
COMPUTE = ("pe", "act", "dve", "pool")


class Buf:
    __slots__ = ("w", "r")

    def __init__(self):
        self.w = None
        self.r = []


class Op:
    __slots__ = ("eng", "fn", "deps", "dma", "sem", "val", "inc", "rank", "stage")


class Prog:
    def __init__(self, nc):
        self.nc = nc
        self.ops = {e: [] for e in ("pe", "act", "dve", "pool", "sp")}
        self.n_dma_sems = {"sp": 10, "act": 4, "pool": 8}
        self.dma_use = {}
        self.dma_rr = {q: 0 for q in self.n_dma_sems}
        self.dma_last = {}

    def add(self, eng, fn, r=(), w=(), inc=None, dma=False, extra=(), cc=False):
        op = Op()
        op.eng = eng
        op.stage = getattr(self, "stage", "s")
        op.fn = fn
        op.dma = dma
        op.sem = None
        op.val = 0
        op.rank = None
        if inc is None:
            inc = eng != "pe"
        op.inc = inc or dma
        deps = list(extra)
        for b in r:
            if b.w is not None:
                deps.append(b.w)
        for b in w:
            if b.w is not None:
                deps.append(b.w)
            deps.extend(b.r)
        if cc:
            op.dma = True
            op.inc = True
            self.n_cc = getattr(self, "n_cc", 0) + 1
            op.sem = ("cc", self.n_cc)
            op.val = 1
        elif dma:
            slot = self.dma_rr[eng]
            self.dma_rr[eng] = (slot + 1) % self.n_dma_sems[eng]
            key = (eng, slot)
            prev = self.dma_last.get(key)
            if prev is not None:
                deps.append(prev)
            self.dma_last[key] = op
            n = self.dma_use.get(key, 0) + 1
            self.dma_use[key] = n
            op.sem = key
            op.val = 16 * n
        op.deps = deps
        self.ops[eng].append(op)
        for b in r:
            if not op.dma:
                b.r = [o for o in b.r if o.dma or o.eng != eng]
            b.r.append(op)
        for b in w:
            b.w = op
            b.r = []
        return op

    def pe(self, fn, r=(), w=(), inc=False):
        return self.add("pe", fn, r, w, inc=inc)

    def barrier(self):
        last = []
        for e in self.ops:
            lst = [o for o in self.ops[e] if o.fn is not None and not o.dma]
            if lst:
                if e == "pe":
                    lst[-1].inc = True
                last.append(lst[-1])
        last.extend(self.dma_last.values())
        last.extend(o for o in self.ops["pool"] if o.dma and o.sem[0] == "cc")
        for e in self.ops:
            self.add(e, None, inc=False, extra=list(last))

    def emit(self):
        nc = self.nc
        lst = [o for o in self.ops["pe"] if o.fn is not None]
        if lst:
            lst[-1].inc = True
        for e in self.ops:
            rank = 0
            pend = []
            for op in self.ops[e]:
                if op.dma or op.fn is None:
                    continue
                if op.inc:
                    rank += 1
                    op.rank = rank
                    for p in pend:
                        p.rank = rank
                    pend = []
                else:
                    pend.append(op)
            assert not pend, (e, len(pend))
        with contextlib.ExitStack() as st:
            sems = {}
            for e in COMPUTE:
                sems[e] = st.enter_context(nc.semaphore("s_" + e))
            for q, n in self.n_dma_sems.items():
                for i in range(n):
                    sems[(q, i)] = st.enter_context(nc.semaphore(f"d_{q}{i}"))
            for i in range(getattr(self, "n_cc", 0)):
                sems[("cc", i + 1)] = st.enter_context(nc.semaphore(f"cc{i}"))
            block = st.enter_context(nc.Block())

            def ev(op):
                if op.dma:
                    return op.sem, op.val
                return op.eng, op.rank

            prof = bool(os.environ.get("KPROF"))

            def replay(ename, eobj):
                known = {}
                cur = [None, None]
                for op in self.ops[ename]:
                    if prof and op.stage != cur[0]:
                        if cur[1] is not None:
                            cur[1].__exit__(None, None, None)
                        cur[0] = op.stage
                        cur[1] = nc.named_scope(op.stage)
                        cur[1].__enter__()
                    _replay_one(ename, eobj, op, known)
                if cur[1] is not None:
                    cur[1].__exit__(None, None, None)

            def _replay_one(ename, eobj, op, known):
                if True:
                    need = {}
                    for d in op.deps:
                        if d.fn is None:
                            continue
                        if d.eng == ename and not d.dma and ename in ("pe", "sp"):
                            continue
                        k, v = ev(d)
                        if known.get(k, 0) < v and need.get(k, 0) < v:
                            need[k] = v
                    for k, v in need.items():
                        eobj.wait_ge(sems[k], v)
                        known[k] = v
                    if op.fn is None:
                        return
                    ins = op.fn(eobj)
                    if op.dma and op.sem[0] == "cc":
                        ins.then_inc(sems[op.sem])
                    elif op.dma:
                        ins.then_inc(sems[op.sem], 16)
                    elif op.inc:
                        ins.then_inc(sems[ename], 1)

            @block.tensor
            def _(e):
                replay("pe", e)

            @block.scalar
            def _(e):
                replay("act", e)

            @block.vector
            def _(e):
                replay("dve", e)

            @block.gpsimd
            def _(e):
                replay("pool", e)

            @block.sync
            def _(e):
                replay("sp", e)


def build(T, dbg=False, stop_at=0, solo=False):
    NT = T // 128
    NG = T // 512
    NT1 = NT + 1
    TH = T + 128
    nc = bass.Bass("TRN2", target_bir_lowering=False)

    def din(name, shape, dt=F32):
        return nc.dram_tensor(name, list(shape), dt, kind="ExternalInput").ap()

    def dscr(name, shape, dt=F32, internal=False):
        if dbg and not internal:
            return nc.dram_tensor(name, list(shape), dt, kind="ExternalOutput").ap()
        return nc.dram_tensor(name, list(shape), dt).ap()

    x_d = din("x", [T, D])
    xh_d = din("xh", [128, D])
    c_d = din("c", [128, KT])
    pos_d = din("pos", [128, NT1], I32)
    cfg_d = din("cfg", [128, 16])
    invf_d = din("invf", [128, 32])
    ident_d = din("ident", [128, 128])
    band_d = din("band", [128, 256])
    tri_d = din("tri", [128, 128])
    scanm_d = din("scanm", [128, 1024])
    w_mod = din("w_mod", [D, 6 * D])
    bmod_in = din("b_mod", [6 * D])
    npre1 = din("mix_norm_pre", [D])
    npost1 = din("mix_norm_post", [D])
    w_in = din("w_in", [D, 11792])
    sinks_d = din("attn_sinks", [16])
    wgk_d = din("wgk", [17, 1024])
    glan_d = din("gla_norm", [512])
    w_bra = din("w_branch_attn", [1024, D])
    w_brb = din("w_branch_gla", [2048, D])
    w_out = din("w_out", [D, D])
    npre2 = din("ffn_norm_pre", [D])
    npost2 = din("ffn_norm_post", [D])
    w_up = din("w_up", [D, 2 * DFF])
    convp_d = din("convp", [128, 4, 2 * FT])
    w_down = din("w_down", [DFF, D])
    out_d = nc.dram_tensor("out", [T, D], F32, kind="ExternalOutput").ap()

    mod_d = dscr("mod_s", [6 * D])
    oaT_d = dscr("oaT_s", [1024, T], BF16)
    qbT_d = dscr("qbT_s", [1024, T])
    kbT_d = dscr("kbT_s", [1024, T])
    vb_d = dscr("vb_s", [T, 2048], BF16)
    sog_d = dscr("sog_s", [T, 2048], BF16)
    sgaT_d = dscr("sgaT_s", [2048, T], BF16)
    sgbT_d = dscr("sgbT_s", [2048, T], BF16)
    oloc_d = dscr("oloc_s", [T, 2048])
    sloc_q = [dscr(f"sloc{q}_s", [256, 520], internal=True) for q in range(4)]
    sall_q = [dscr(f"sall{q}_s", [4 * 256, 520], internal=True) for q in range(4)]
    obT_d = dscr("obT_s", [2048, T], BF16)
    mgT_d = dscr("mgT_s", [2048, T], BF16)
    x1_d = dscr("x1_s", [T, D])
    h2T_d = dscr("h2T_s", [2048, T], BF16)
    xl_d = dscr("xl_s", [2, D], internal=True)
    xla_d = dscr("xla_s", [8, D], internal=True)
    actT_d = dscr("actT_s", [DFF, T], BF16)
    y2_d = dscr("y2_s", [T, D])

    P = Prog(nc)
    st = contextlib.ExitStack()
    nstage = [0]

    def stage_end():
        nstage[0] += 1
        P.barrier()
        if nstage[0] == stop_at:
            P.emit()
            st.close()
            return True
        return False

    SB_BYTES = 206 * 1024
    big = st.enter_context(nc.sbuf_tensor("big", [128, SB_BYTES // 4], F32))
    PF = st.enter_context(nc.psum_tensor("PF", [128, 3072], F32))
    PB = st.enter_context(nc.psum_tensor("PB", [128, 2048], BF16))
    pbuf = [Buf() for _ in range(8)]

    def pf(bank, n=512, off=0):
        return PF[:, bank * 512 + off: bank * 512 + off + n]

    def pb(bank, n=1024, off=0):
        return PB[:, (bank - 6) * 1024 + off:(bank - 6) * 1024 + off + n]

    class Alloc:
        def __init__(self):
            self.off = 0

        def __call__(self, shape, dt=F32):
            esz = 4 if dt in (F32, I32) else 2
            n = int(np.prod(shape[1:])) * esz
            n = (n + 31) // 32 * 32
            assert self.off + n <= SB_BYTES, ("SBUF overflow", self.off + n)
            a = big[:, self.off // 4:(self.off + n) // 4]
            self.off += n
            if dt != F32:
                a = a.bitcast(dt)
            a = a[:, 0:int(np.prod(shape[1:]))]
            if len(shape) == 3:
                a = a.rearrange("p (a b) -> p a b", a=shape[1])
            elif len(shape) == 4:
                a = a.rearrange("p (a b c) -> p a b c", a=shape[1], b=shape[2])
            if shape[0] != 128:
                a = a[0:shape[0]]
            return a

    A = Alloc()

    def MM(out, lhsT, rhs, start=True, stop=True, r=(), w=(), inc=False, tp=None):
        kw = {} if tp is None else {"tile_position": tp}
        P.pe(lambda e: e.matmul(out, lhsT=lhsT, rhs=rhs, start=start, stop=stop, **kw), r=r, w=w, inc=inc)

    def TR(out, in_, idn, r=(), w=(), inc=False):
        P.pe(lambda e: e.transpose(out=out, in_=in_, identity=idn), r=r, w=w, inc=inc)

    def ACT(out, in_, func, r=(), w=(), bias=None, scale=None, accum=None):
        kw = {}
        if bias is not None:
            kw["bias"] = bias
        if scale is not None:
            kw["scale"] = scale
        if accum is not None:
            kw["accum_out"] = accum
        P.add("act", lambda e: e.activation(out=out, in_=in_, func=func, **kw), r, w)

    def TT(eng, out, in0, in1, op, r=(), w=()):
        P.add(eng, lambda e: e.tensor_tensor(out=out, in0=in0, in1=in1, op=op), r, w)

    def TS(eng, out, in0, s1, op0, s2=None, op1=None, r=(), w=()):
        if op1 is None:
            P.add(eng, lambda e: e.tensor_scalar(out=out, in0=in0, scalar1=s1, scalar2=None, op0=op0), r, w)
        else:
            P.add(eng, lambda e: e.tensor_scalar(out=out, in0=in0, scalar1=s1, scalar2=s2, op0=op0, op1=op1), r, w)

    def STT(out, in0, scalar, in1, op0, op1, r=(), w=()):
        P.add("dve", lambda e: e.scalar_tensor_tensor(out=out, in0=in0, scalar=scalar, in1=in1, op0=op0, op1=op1), r, w)

    def CP(eng, out, in_, r=(), w=()):
        if eng == "act":
            P.add("act", lambda e: e.activation(out=out, in_=in_, func=AF.Copy), r, w)
        else:
            P.add(eng, lambda e: e.tensor_copy(out=out, in_=in_), r, w)

    def DMA(q, out, in_, r=(), w=(), slow=False):
        if slow:
            P.add(q, lambda e: e.dma_start(out=out, in_=in_, allow_slow_non_contiguous=True), r, w, dma=True)
        else:
            P.add(q, lambda e: e.dma_start(out=out, in_=in_), r, w, dma=True)

    def RECIP(out, in_, r=(), w=()):
        P.add("dve", lambda e: e.reciprocal(out=out, in_=in_), r, w)

    def bc_mid(ap, n):
        return ap.unsqueeze(1).broadcast_to([ap.shape[0], n, ap.shape[1]])

    def bc_last(ap, n):
        return ap.unsqueeze(2).broadcast_to([ap.shape[0], ap.shape[1], n])

    def wview(w, c0, n):
        return w[:, c0:c0 + n].rearrange("(k p) n -> p k n", p=128)

    def load_w(dst, w, c0, n, bufs, nk=None):
        kt = dst.shape[1]
        step = max(1, 1024 // 128) if n >= 256 else kt
        step = min(step, kt)
        v = wview(w, c0, n)
        for k0 in range(0, kt, step):
            k1 = min(kt, k0 + step)
            DMA("pool", dst[:, k0:k1, :], v[:, k0:k1, :], w=bufs)

    def rms_scale(ssq, tmp, rstd, r, w):
        TS("dve", tmp, ssq, 1.0 / D, ALU.mult, EPS, ALU.add, r=r, w=w)
        ACT(tmp, tmp, AF.Sqrt, r=w, w=w)
        RECIP(rstd, tmp, r=w, w=w)

    ident_f = A([128, 128])
    ident = A([128, 128], BF16)
    cfg = A([128, 16])
    convp = A([128, 4, 2 * FT])
    sinkbc = A([128, 16])
    band = A([128, 256])
    band0 = A([128, 256])
    tri = A([128, 128])
    scanm = A([128, 1024])
    glan = A([128, 512])
    wgk = A([17, 1024])
    cosd = A([128, NT1, 64])
    sins = A([128, NT1, 64])
    gkT = A([17, T])
    h2Th = A([128, KT, 2], BF16)
    small = A([128, 64])
    cbP = A([128, KT], BF16)
    b_cb = Buf()
    bK = Buf()
    b_gkT = Buf()
    b_h2Th = Buf()
    b_small = Buf()
    PERSIST = A.off

    DMA("sp", ident_f, ident_d, w=[bK])
    DMA("sp", cfg, cfg_d, w=[bK])
    DMA("sp", convp, convp_d, w=[bK])
    DMA("sp", sinkbc, sinks_d.partition_broadcast(128), w=[bK])
    DMA("sp", band, band_d, w=[bK])
    DMA("sp", tri, tri_d, w=[bK])
    DMA("sp", scanm, scanm_d, w=[bK])
    DMA("sp", glan, glan_d.partition_broadcast(128), w=[bK])
    DMA("sp", wgk, wgk_d, w=[bK])
    CP("dve", ident, ident_f, r=[bK], w=[bK])
    CP("dve", band0, band, r=[bK], w=[bK])
    TS("dve", band0[:, 0:128], band[:, 0:128], cfg[:, 1:2], ALU.add, r=[bK], w=[bK])
    P.add("pool", lambda e: e.memset(gkT[0:17, :], 1.0), w=[b_gkT])

    P.stage = "stage_0"
    A.off = PERSIST
    s0_pos_i = A([128, NT1], I32)
    s0_pos_f = A([128, NT1])
    s0_invf = A([128, 32])
    s0_ang = A([128, NT1, 32])
    s0_a2 = A([128, NT1, 32])
    s0_ki = A([128, NT1, 32], I32)
    s0_kf = A([128, NT1, 32])
    s0_m = A([128, NT1, 32])
    s0_sin = A([128, NT1, 32])
    s0_cos = A([128, NT1, 32])
    bT = Buf()
    DMA("sp", s0_pos_i, pos_d, w=[bT])
    DMA("sp", s0_invf, invf_d, w=[bT])
    CP("dve", s0_pos_f, s0_pos_i, r=[bT], w=[bT])
    TT("dve", s0_ang, bc_mid(s0_invf, NT1), bc_last(s0_pos_f, 32), ALU.mult, r=[bT], w=[bT])

    def reduce_sin(dst, src, phase):
        TS("dve", s0_a2, src, phase, ALU.add, r=[bT], w=[bT])
        TS("dve", s0_ki, s0_a2, 1.0 / TWO_PI, ALU.mult, r=[bT], w=[bT])
        CP("dve", s0_kf, s0_ki, r=[bT], w=[bT])
        STT(s0_a2, s0_kf, -TWO_PI, s0_a2, ALU.mult, ALU.add, r=[bT], w=[bT])
        TS("dve", s0_m, s0_a2, float(np.pi), ALU.is_gt, r=[bT], w=[bT])
        STT(s0_a2, s0_m, -TWO_PI, s0_a2, ALU.mult, ALU.add, r=[bT], w=[bT])
        TS("dve", s0_m, s0_a2, -float(np.pi), ALU.is_lt, r=[bT], w=[bT])
        STT(s0_a2, s0_m, TWO_PI, s0_a2, ALU.mult, ALU.add, r=[bT], w=[bT])
        ACT(dst, s0_a2, AF.Sin, r=[bT], w=[bT])

    reduce_sin(s0_sin, s0_ang, 0.0)
    reduce_sin(s0_cos, s0_ang, float(np.pi / 2))
    CP("dve", cosd[:, :, 0:32], s0_cos, r=[bT], w=[bK])
    CP("dve", cosd[:, :, 32:64], s0_cos, r=[bT], w=[bK])
    TS("dve", sins[:, :, 0:32], s0_sin, -1.0, ALU.mult, r=[bT], w=[bK])
    CP("dve", sins[:, :, 32:64], s0_sin, r=[bT], w=[bK])

    s0_c = A([128, KT])
    s0_cb = A([128, KT], BF16)
    s0_slab = [A([128, KT, 512], BF16) for _ in range(2)]
    s0_bm = [A([1, 512]) for _ in range(2)]
    s0_row = [A([1, 512]) for _ in range(2)]
    b_c = Buf()
    b_sl = [Buf(), Buf()]
    b_bm = [Buf(), Buf()]
    b_row = [Buf(), Buf()]
    b_mod = Buf()
    DMA("sp", s0_c, c_d, w=[b_c])
    ACT(cbP, s0_c, AF.Silu, r=[b_c], w=[b_cb])

    def mod_load(j, slab_ap, b_slab_, bm_ap, b_bm_):
        load_w(slab_ap, w_mod, j * 512, 512, [b_slab_])
        DMA("sp", bm_ap, bmod_in[j * 512:(j + 1) * 512].unsqueeze(0), w=[b_bm_])

    def mod_slab(j, slab_ap, b_slab_, bm_ap, b_bm_, row_ap, b_row_, bank, load=True):
        if load:
            mod_load(j, slab_ap, b_slab_, bm_ap, b_bm_)
        for k in range(KT):
            MM(pf(bank)[0:1, :], cbP[:, k:k + 1], slab_ap[:, k, :], start=(k == 0), stop=(k == KT - 1),
               r=[b_cb, b_slab_], w=[pbuf[bank]], inc=(k == KT - 1))
        TT("dve", row_ap, pf(bank)[0:1, :], bm_ap, ALU.add, r=[pbuf[bank], b_bm_], w=[b_row_])
        DMA("sp", mod_d[j * 512:(j + 1) * 512].unsqueeze(0), row_ap, r=[b_row_], w=[b_mod])

    for j in range(8):
        i = j % 2
        mod_slab(j, s0_slab[i], b_sl[i], s0_bm[i], b_bm[i], s0_row[i], b_row[i], i)

    if stage_end():
        return nc

    P.stage = "stage_1"
    A.off = PERSIST
    hT = A([128, KT, TH], BF16)
    b_hT = [Buf() for _ in range(NT1)]
    HT_END = A.off
    bcA = A([128, D])
    bcB = A([128, D])
    bcC = A([128, D])
    b_bc = Buf()
    xt = [A([128, D]) for _ in range(2)]
    b_xt = [Buf(), Buf()]
    junk = A([128, D], BF16)
    b_junk = Buf()
    tmpf = A([128, D])
    b_tmpf = Buf()
    hb = [A([128, D], BF16) for _ in range(2)]
    b_hb = [Buf(), Buf()]
    DMA("sp", bcC, mod_d[D:2 * D].partition_broadcast(128), r=[b_mod], w=[b_bc])
    DMA("sp", bcA, npre1.partition_broadcast(128), w=[b_bc])
    DMA("sp", bcB, mod_d[0:D].partition_broadcast(128), r=[b_mod], w=[b_bc])
    STT(bcA, bcC, 1.0, bcA, ALU.add, ALU.mult, r=[b_bc], w=[b_bc])

    def norm_tile(src_ap, np_, wmod, shift, hb_ap, b_src, b_hb_i, sidx):
        ss = small[0:np_, sidx:sidx + 1]
        tm = small[0:np_, sidx + 1:sidx + 2]
        rs = small[0:np_, sidx + 2:sidx + 3]
        ACT(junk[0:np_], src_ap, AF.Square, r=[b_src], w=[b_junk, b_small], accum=ss)
        rms_scale(ss, tm, rs, r=[b_small], w=[b_small])
        STT(tmpf[0:np_], src_ap, rs, wmod[0:np_], ALU.mult, ALU.mult, r=[b_src, b_small, b_bc], w=[b_tmpf])
        TT("pool", hb_ap, tmpf[0:np_], shift[0:np_], ALU.add, r=[b_tmpf, b_bc], w=[b_hb_i])

    def transpose16(hb_ap, b_hb_i, dst_fn, b_dst, np_=128):
        for h in range(2):
            for kk in range(8):
                k = h * 8 + kk
                TR(pb(6 + h)[:, kk * 128:kk * 128 + np_], hb_ap[:, k * 128:(k + 1) * 128], ident[0:np_, 0:np_],
                   r=[b_hb_i, bK], w=[pbuf[6 + h]], inc=(kk == 7))
            src = pb(6 + h).rearrange("p (k c) -> p k c", k=8)[:, :, 0:np_]
            CP("act" if (h == 0 or np_ != 128) else "dve", dst_fn(h), src, r=[pbuf[6 + h]], w=b_dst)

    for ti in range(NT1):
        i = ti % 2
        src = xh_d if ti == NT else x_d[ti * 128:(ti + 1) * 128, :]
        DMA("sp", xt[i], src, w=[b_xt[i]])
        norm_tile(xt[i], 128, bcA, bcB, hb[i], b_xt[i], b_hb[i], 4 * i)
        c0 = T if ti == NT else ti * 128
        transpose16(hb[i], b_hb[i], lambda h, c0=c0: hT[:, h * 8:(h + 1) * 8, c0:c0 + 128], [b_hT[ti]])

    if stage_end():
        return nc

    P.stage = "stage_2a"
    A.off = HT_END
    kvslab = A([128, KT, 512], BF16)
    qslab = [kvslab[:, :, 0:256], kvslab[:, :, 256:512]]
    b_kvs = Buf()
    b_qs = [Buf(), Buf()]
    kT = A([64, 4, TH], BF16)
    b_kT = Buf()
    vA = A([128, NT1, 256], BF16)
    b_vA = Buf()
    qT = A([64, 4, T], BF16)
    b_qT = Buf()
    rA = A([128, 4, 64])
    rB = A([128, 4, 64])
    rX = A([128, 4, 64])
    b_rX = Buf()
    rR = [A([128, 4, 64], BF16) for _ in range(2)]
    b_rA, b_rB = Buf(), Buf()
    b_rR = [Buf(), Buf()]
    _sb = A([128, 4, 256])
    S_sb = [_sb, _sb]
    _bS = Buf()
    b_S = [_bS, _bS]
    _pe = A([128, 4, 256], BF16)
    Pe = [_pe, _pe]
    _bPe = Buf()
    b_Pe = [_bPe, _bPe]
    qpb = [Buf(), Buf()]
    Obuf = [Buf(), Buf()]
    Pn = [A([128, 4, 256], BF16) for _ in range(2)]
    b_Pn = [Buf(), Buf()]
    PTs = [A([128, 8, 128], BF16) for _ in range(2)]
    b_PT = [Buf(), Buf()]
    ost = [A([128, 2, 128], BF16) for _ in range(2)]
    b_ost = [Buf(), Buf()]
    sm = [A([128, 32]) for _ in range(2)]
    b_sm = [Buf(), Buf()]
    b_oaT = Buf()

    def rope(src3, nh, ti, dst, b_src, b_dst):
        cs = bc_mid(cosd[:, ti, :], nh)
        CP("act", rX[:, 0:nh, :], src3, r=[b_src], w=[b_rX])
        TT("dve", rA[:, 0:nh, :], rX[:, 0:nh, :], cs, ALU.mult, r=[b_rX, bK], w=[b_rA])
        TT("dve", rB[:, 0:nh, 0:32], rX[:, 0:nh, 32:64], bc_mid(sins[:, ti, 0:32], nh), ALU.mult,
           r=[b_rX, bK], w=[b_rB])
        TT("dve", rB[:, 0:nh, 32:64], rX[:, 0:nh, 0:32], bc_mid(sins[:, ti, 32:64], nh), ALU.mult,
           r=[b_rX, bK], w=[b_rB])
        TT("dve", dst, rA[:, 0:nh, :], rB[:, 0:nh, :], ALU.add, r=[b_rA, b_rB], w=[b_dst])

    def swa_gen():
        load_w(kvslab, w_in, 1024, 512, [b_kvs])
        for ti in range(NT1):
            i = ti % 2
            c0 = T if ti == NT else ti * 128
            for k in range(KT):
                MM(pf(0), hT[:, k, c0:c0 + 128], kvslab[:, k, :], start=(k == 0), stop=(k == KT - 1),
                   r=[b_hT[ti], b_kvs], w=[pbuf[0]], inc=(k == KT - 1))
            rope(pf(0, 256).rearrange("p (h d) -> p h d", h=4), 4, ti, rR[i], pbuf[0], b_rR[i])
            CP("act", vA[:, ti, :], pf(0, 256, 256), r=[pbuf[0]], w=[b_vA])
            for h in range(4):
                TR(pb(6 + i)[0:64, h * 128:(h + 1) * 128], rR[i][:, h, :], ident, r=[b_rR[i], bK], w=[pbuf[6 + i]],
                   inc=(h == 3))
            CP("act", kT[:, :, c0:c0 + 128], pb(6 + i, 512).rearrange("p (h c) -> p h c", h=4)[0:64],
               r=[pbuf[6 + i]], w=[b_kT])
            yield

        for hk in range(4):
            i2 = hk % 2
            load_w(qslab[i2], w_in, 256 * hk, 256, [b_qs[i2], b_kvs])
            for ti in range(NT):
                i = ti % 2
                for k in range(KT):
                    MM(pf(0, 256, i * 256), hT[:, k, ti * 128:(ti + 1) * 128], qslab[i2][:, k, :], start=(k == 0),
                       stop=(k == KT - 1), r=[b_hT[ti], b_qs[i2]],
                       w=([qpb[i], pbuf[0]] if (hk == 0 and ti < 2) else [qpb[i]]), inc=(k == KT - 1))
                rope(pf(0, 256, i * 256).rearrange("p (h d) -> p h d", h=4), 4, ti, rR[i], qpb[i], b_rR[i])
                for h in range(4):
                    TR(pb(6 + i)[0:64, h * 128:(h + 1) * 128], rR[i][:, h, :], ident, r=[b_rR[i], bK],
                       w=[pbuf[6 + i]], inc=(h == 3))
                CP("act", qT[:, :, ti * 128:(ti + 1) * 128], pb(6 + i, 512).rearrange("p (h c) -> p h c", h=4)[0:64],
                   r=[pbuf[6 + i]], w=[b_qT])
                yield
            def partA(n, hk=hk):
                i = n % 2
                cur = slice(n * 128, (n + 1) * 128)
                prv = slice(T, T + 128) if n == 0 else slice((n - 1) * 128, n * 128)
                Sps = PF[:, 1024:2048].rearrange("p (g k) -> p g k", g=4)
                for g in range(4):
                    MM(Sps[:, g, 0:128], qT[:, g, cur], kT[:, hk, prv], r=[b_qT, b_kT], w=[pbuf[2 + g // 2]])
                    MM(Sps[:, g, 128:256], qT[:, g, cur], kT[:, hk, cur], r=[b_qT, b_kT], w=[pbuf[2 + g // 2]],
                       inc=(g % 2 == 1))
                yield
                CP("act", S_sb[i], Sps, r=[pbuf[2], pbuf[3]], w=[b_S[i]])
                TT("dve", S_sb[i], S_sb[i], bc_mid(band0 if n == 0 else band, 4), ALU.add, r=[b_S[i], bK],
                   w=[b_S[i]])
                s_ = sm[i]
                rmax, mm_, negm, d2, rs, es, den, rden = [s_[:, 4 * j:4 * j + 4] for j in range(8)]
                P.add("dve", lambda e, o=rmax, a=S_sb[i]: e.tensor_reduce(out=o, in_=a, axis=AX.X, op=ALU.max),
                      [b_S[i]], [b_sm[i]])
                STT(mm_, rmax, 0.125, sinkbc[:, 4 * hk:4 * hk + 4], ALU.mult, ALU.max, r=[b_sm[i], bK], w=[b_sm[i]])
                TS("dve", negm, mm_, -1.0, ALU.mult, r=[b_sm[i]], w=[b_sm[i]])
                TT("dve", d2, sinkbc[:, 4 * hk:4 * hk + 4], negm, ALU.add, r=[b_sm[i], bK], w=[b_sm[i]])
                for g in range(4):
                    ACT(Pe[i][:, g, :], S_sb[i][:, g, :], AF.Exp, r=[b_S[i], b_sm[i]], w=[b_Pe[i], b_sm[i]],
                        bias=negm[:, g:g + 1], scale=0.125, accum=rs[:, g:g + 1])
                ACT(es, d2, AF.Exp, r=[b_sm[i]], w=[b_sm[i]])
                TT("dve", den, rs, es, ALU.add, r=[b_sm[i]], w=[b_sm[i]])
                RECIP(rden, den, r=[b_sm[i]], w=[b_sm[i]])
                TT("dve", Pn[i], Pe[i], bc_last(rden, 256), ALU.mult, r=[b_Pe[i], b_sm[i]], w=[b_Pn[i]])

            def partB(n, hk=hk):
                i = n % 2
                cur = slice(n * 128, (n + 1) * 128)
                vprev = NT if n == 0 else n - 1
                for g in range(4):
                    for kb in range(2):
                        j = g * 2 + kb
                        TR(pb(6 + i)[:, j * 128:(j + 1) * 128], Pn[i][:, g, kb * 128:(kb + 1) * 128], ident,
                           r=[b_Pn[i], bK], w=[pbuf[6 + i]], inc=(j == 7))
                CP("act", PTs[i], pb(6 + i).rearrange("p (j c) -> p j c", j=8), r=[pbuf[6 + i]], w=[b_PT[i]])
                yield
                Ops = pf(4, 256, i * 256).rearrange("p (a c) -> p a c", a=2)
                for g in range(4):
                    half = g % 2
                    for kb in range(2):
                        vt = vprev if kb == 0 else n
                        MM(Ops[64 * half:64 * half + 64, g // 2, :], vA[:, vt, hk * 64:(hk + 1) * 64],
                           PTs[i][:, g * 2 + kb, :], start=(kb == 0), stop=(kb == 1), r=[b_vA, b_PT[i]],
                           w=[Obuf[i]], inc=(g == 3 and kb == 1), tp=(0, 64 * half))
                CP("act", ost[i], Ops, r=[Obuf[i]], w=[b_ost[i]])
                DMA("sp", oaT_d[hk * 256:(hk + 1) * 256, cur].rearrange("(a p) c -> p a c", p=128), ost[i],
                    r=[b_ost[i]], w=[b_oaT])

            yield from partA(0)
            for n in range(NT):
                if n + 1 < NT:
                    yield from partA(n + 1)
                yield from partB(n)
                yield


    P.stage = "stage_2b"
    slab = [A([128, KT, 256], BF16) for _ in range(2)]
    banks2b = [1, 5]
    b_slab = [Buf(), Buf()]
    stg = [A([128, 512]) for _ in range(4)]
    b_stg = [Buf() for _ in range(4)]
    cnt = {"slab": 0, "stg": 0, "bank": 0}
    b_scr = {}

    def sbuf_for(name):
        if name not in b_scr:
            b_scr[name] = Buf()
        return b_scr[name]

    def gemm_fm(w, c0, ncols, act_T, act_bufs_fn, func, dst, dst_row0, out_dt, name):
        for s0 in range(0, ncols, 256):
            n = min(256, ncols - s0)
            si = cnt["slab"] % 2
            cnt["slab"] += 1
            kt = act_T.shape[1]
            load_w(slab[si][:, 0:kt, 0:n], w, c0 + s0, n, [b_slab[si]])
            for ct in range(0, n, 128):
                for g in range(NG):
                    bk = banks2b[cnt["bank"] % 2]
                    cnt["bank"] += 1
                    for k in range(kt):
                        MM(pf(bk), slab[si][:, k, ct:ct + 128], act_T[:, k, g * 512:(g + 1) * 512], start=(k == 0),
                           stop=(k == kt - 1), r=[b_slab[si]] + act_bufs_fn(g), w=[pbuf[bk]], inc=(k == kt - 1))
                    sj = cnt["stg"] % 4
                    cnt["stg"] += 1
                    o = stg[sj] if out_dt == F32 else stg[sj].bitcast(BF16)[:, 0:512]
                    if func == AF.Copy and (cnt["stg"] % 2 == 0):
                        CP("dve", o, pf(bk), r=[pbuf[bk]], w=[b_stg[sj]])
                    else:
                        ACT(o, pf(bk), func, r=[pbuf[bk]], w=[b_stg[sj]])
                    r0 = dst_row0 + s0 + ct
                    DMA("sp", dst[r0:r0 + 128, g * 512:(g + 1) * 512], o, r=[b_stg[sj]], w=[sbuf_for(name)])
                    yield

    def gemm_tm(w, c0, ncols, func, dst, dst_c0, name):
        for s0 in range(0, ncols, 256):
            n = min(256, ncols - s0)
            si = cnt["slab"] % 2
            cnt["slab"] += 1
            load_w(slab[si][:, :, 0:n], w, c0 + s0, n, [b_slab[si]])
            for ti in range(NT):
                bk = banks2b[cnt["bank"] % 2]
                cnt["bank"] += 1
                for k in range(KT):
                    MM(pf(bk, n), hT[:, k, ti * 128:(ti + 1) * 128], slab[si][:, k, 0:n], start=(k == 0),
                       stop=(k == KT - 1), r=[b_slab[si], b_hT[ti]], w=[pbuf[bk]], inc=(k == KT - 1))
                sj = cnt["stg"] % 4
                cnt["stg"] += 1
                o = stg[sj].bitcast(BF16)[:, 0:n]
                if func == AF.Copy and (cnt["stg"] % 2 == 0):
                    CP("dve", o, pf(bk, n), r=[pbuf[bk]], w=[b_stg[sj]])
                else:
                    ACT(o, pf(bk, n), func, r=[pbuf[bk]], w=[b_stg[sj]])
                DMA("sp", dst[ti * 128:(ti + 1) * 128, dst_c0 + s0:dst_c0 + s0 + n], o, r=[b_stg[sj]],
                    w=[sbuf_for(name)])
                yield

    def hT_bufs(g):
        return b_hT[g * 4:(g + 1) * 4]

    def g2b_gen():
        gslab = slab[0][:, :, 0:16]
        load_w(gslab, w_in, 5632, 16, [b_slab[0]])
        cnt["slab"] += 1
        for g in range(NG):
            bk = banks2b[cnt["bank"] % 2]
            cnt["bank"] += 1
            for k in range(KT):
                MM(pf(bk)[0:16, :], gslab[:, k, :], hT[:, k, g * 512:(g + 1) * 512], start=(k == 0), stop=(k == KT - 1),
                   r=[b_slab[0]] + hT_bufs(g), w=[pbuf[bk]], inc=(k == KT - 1))
            CP("act", gkT[0:16, g * 512:(g + 1) * 512], pf(bk)[0:16, :], r=[pbuf[bk]], w=[b_gkT])
            yield
        yield from gemm_fm(w_in, 1536, 1024, hT, hT_bufs, AF.Copy, qbT_d, 0, F32, "qbT")
        yield from gemm_fm(w_in, 2560, 1024, hT, hT_bufs, AF.Copy, kbT_d, 0, F32, "kbT")
        yield from gemm_tm(w_in, 3584, 2048, AF.Copy, vb_d, 0, "vb")
        yield from gemm_tm(w_in, 5648, 2048, AF.Silu, sog_d, 0, "sog")
        yield from gemm_fm(w_in, 7696, 2048, hT, hT_bufs, AF.Sigmoid, sgaT_d, 0, BF16, "sgaT")
        yield from gemm_fm(w_in, 9744, 2048, hT, hT_bufs, AF.Sigmoid, sgbT_d, 0, BF16, "sgbT")

    g1, g2 = swa_gen(), g2b_gen()
    a1 = a2 = True
    acc = 0.0
    RATIO = 1.65
    while a1 or a2:
        if a1:
            P.stage = "stage_2a"
            try:
                next(g1)
            except StopIteration:
                a1 = False
        acc += RATIO if a1 else 1e9
        P.stage = "stage_2b"
        while a2 and acc >= 1.0:
            try:
                next(g2)
            except StopIteration:
                a2 = False
            acc -= 1.0
        if not a2:
            acc = 0.0

    if stage_end():
        return nc

    P.stage = "stage_3"
    A.off = PERSIST
    Sst = A([128, 8, 512])
    Sbf = A([128, 8, 512], BF16)
    b_Sst, b_Sbf = Buf(), Buf()
    qcT = A([128, 8, T], BF16)
    b_qcT = Buf()
    cumB = A([128, 8])
    S3_KEEP = A.off
    ecum = A([128, 8])
    dec = [A([128, 8]) for _ in range(2)]
    b_cum = Buf()
    b_dec = [Buf(), Buf()]
    qf = [A([128, 8, 128]) for _ in range(2)]
    kf = [A([128, 8, 128]) for _ in range(2)]
    b_qf = [Buf(), Buf()]
    b_kf = [Buf(), Buf()]
    vch = [A([128, 2048], BF16) for _ in range(2)]
    b_vch = [Buf(), Buf()]
    G = [A([128, 8, 128]) for _ in range(6)]
    b_G = [Buf() for _ in range(6)]
    qi = [A([128, 8, 128], BF16) for _ in range(2)]
    ki = [A([128, 8, 128], BF16) for _ in range(2)]
    qn = [A([128, 8, 128], BF16) for _ in range(2)]
    ks = [A([128, 8, 128], BF16) for _ in range(2)]
    b_qi, b_ki, b_qn, b_ks = [[Buf(), Buf()] for _ in range(4)]
    ATs = A([128, 4, 128], BF16)
    b_AT = Buf()
    ATf = A([128, 4, 128])
    b_ATf = Buf()
    ksT = A([128, 1024], BF16)
    b_ksT = Buf()
    olst = [A([128, 1024]) for _ in range(2)]
    b_olst = [Buf(), Buf()]
    b_oloc = Buf()
    P.add("pool", lambda e: e.memset(Sst.rearrange("p a b -> p (a b)"), 0.0), w=[b_Sst])
    P.add("pool", lambda e: e.memset(Sbf.rearrange("p a b -> p (a b)"), 0.0), w=[b_Sbf])
    P.add("pool", lambda e: e.memset(cumB, 0.0), w=[b_cum])
    DKS = 256 ** -0.5

    def F2(a):
        return a.rearrange("p a b -> p (a b)")

    def gla_prep(c):
        i = c % 2
        cc = slice(c * 128, (c + 1) * 128)
        DMA("sp", qf[i], qbT_d[:, cc].rearrange("(j p) c -> p j c", p=128), r=[sbuf_for("qbT")], w=[b_qf[i]])
        DMA("sp", kf[i], kbT_d[:, cc].rearrange("(j p) c -> p j c", p=128), r=[sbuf_for("kbT")], w=[b_kf[i]])
        DMA("sp", vch[i], vb_d[cc, :], r=[sbuf_for("vb")], w=[b_vch[i]])
        zps = PF[:, 0:1024].rearrange("p (j c) -> p j c", j=8)
        for j in range(8):
            MM(zps[:, j, :], wgk[:, j * 128:(j + 1) * 128], gkT[:, cc], r=[bK, b_gkT], w=[pbuf[j // 4]],
               inc=(j % 4 == 3))
        zb = [pbuf[0], pbuf[1]]
        CP("act", G[0], zps, r=zb, w=[b_G[0]])
        ACT(F2(G[1]), F2(G[0]), AF.Abs, r=[b_G[0]], w=[b_G[1]])
        ACT(F2(G[1]), F2(G[1]), AF.Exp, r=[b_G[1]], w=[b_G[1]], scale=-1.0)
        ACT(F2(G[1]), F2(G[1]), AF.Ln, r=[b_G[1]], w=[b_G[1]], bias=1.0)
        TS("dve", F2(G[0]), F2(G[0]), 0.0, ALU.min, r=[b_G[0]], w=[b_G[0]])
        TT("pool", G[0].rearrange("p a b -> p (a b)"), G[0].rearrange("p a b -> p (a b)"), G[1].rearrange("p a b -> p (a b)"), ALU.subtract, r=[b_G[0], b_G[1]], w=[b_G[0]])
        P.add("dve", lambda e: e.tensor_tensor_scan(out=G[2].rearrange("p a b -> p (a b)"), data0=scanm,
                                                     data1=G[0].rearrange("p a b -> p (a b)"), initial=0.0,
                                                     op0=ALU.mult, op1=ALU.add), [b_G[0], bK], [b_G[2]])
        TT("dve", G[3], G[2], G[2][:, :, 64:65].broadcast_to([128, 8, 128]), ALU.subtract, r=[b_G[2]], w=[b_G[3]])
        TT("dve", G[4], G[2][:, :, 127:128].broadcast_to([128, 8, 128]), G[2], ALU.subtract, r=[b_G[2]],
           w=[b_G[4]])
        ACT(dec[i], G[2][:, :, 127], AF.Exp, r=[b_G[2]], w=[b_dec[i]], scale=1.0 / 16)
        ACT(ecum, cumB, AF.Exp, r=[b_cum], w=[b_cum], scale=1.0 / 16)
        ACT(F2(G[5]), F2(G[3]), AF.Exp, r=[b_G[3]], w=[b_G[5]], scale=1.0 / 16)
        STT(F2(qi[i]), F2(qf[i]), DKS, F2(G[5]), ALU.mult, ALU.mult, r=[b_qf[i], b_G[5]], w=[b_qi[i]])
        ACT(F2(G[5]), F2(G[3]), AF.Exp, r=[b_G[3]], w=[b_G[5]], scale=-1.0 / 16)
        TT("dve", F2(ki[i]), F2(kf[i]), F2(G[5]), ALU.mult, r=[b_kf[i], b_G[5]], w=[b_ki[i]])
        ACT(F2(G[3]), F2(G[2]), AF.Exp, r=[b_G[2]], w=[b_G[3]], scale=1.0 / 16)
        STT(F2(qn[i]), F2(qf[i]), DKS, F2(G[3]), ALU.mult, ALU.mult, r=[b_qf[i], b_G[3]], w=[b_qn[i]])
        ACT(F2(G[4]), F2(G[4]), AF.Exp, r=[b_G[4]], w=[b_G[4]], scale=1.0 / 16)
        TT("pool", ks[i].rearrange("p a b -> p (a b)"), kf[i].rearrange("p a b -> p (a b)"), G[4].rearrange("p a b -> p (a b)"), ALU.mult, r=[b_kf[i], b_G[4]], w=[b_ks[i]])
        TT("dve", qcT[:, :, cc], qn[i], bc_last(ecum, 128), ALU.mult, r=[b_qn[i], b_cum], w=[b_qcT])
        TT("dve", cumB, cumB, G[2][:, :, 127], ALU.add, r=[b_cum, b_G[2]], w=[b_cum])

    def gla_pe(c):
        i = c % 2
        ATp = pf(2).rearrange("p (h c) -> p h c", h=4)
        for h in range(4):
            for dt_ in range(2):
                j = h * 2 + dt_
                MM(ATp[:, h, :], ki[i][:, j, :], qi[i][:, j, :], start=(dt_ == 0), stop=(dt_ == 1),
                   r=[b_ki[i], b_qi[i]], w=[pbuf[2]], inc=(j == 7))
        CP("act", ATf, ATp, r=[pbuf[2]], w=[b_ATf])
        TT("dve", ATs, ATf, bc_mid(tri, 4), ALU.mult, r=[b_ATf, bK], w=[b_AT])
        for j in range(8):
            TR(pb(6)[:, j * 128:(j + 1) * 128], ks[i][:, j, :], ident, r=[b_ks[i], bK], w=[pbuf[6]], inc=(j == 7))
        CP("act", ksT, pb(6), r=[pbuf[6]], w=[b_ksT])
        for hp in range(2):
            for hh in range(2):
                h = hp * 2 + hh
                bk = 3 + hh
                MM(pf(bk), ATs[:, h, :], vch[i][:, h * 512:(h + 1) * 512], start=True, stop=False,
                   r=[b_AT, b_vch[i]], w=[pbuf[bk]])
                MM(pf(bk), qn[i][:, 2 * h, :], Sbf[:, 2 * h, :], start=False, stop=False, r=[b_qn[i], b_Sbf],
                   w=[pbuf[bk]])
                MM(pf(bk), qn[i][:, 2 * h + 1, :], Sbf[:, 2 * h + 1, :], start=False, stop=True,
                   r=[b_qn[i], b_Sbf], w=[pbuf[bk]], inc=True)
            CP("act", olst[hp], PF[:, 3 * 512:5 * 512], r=[pbuf[3], pbuf[4]], w=[b_olst[hp]])
            DMA("sp", oloc_d[c * 128:(c + 1) * 128, hp * 1024:(hp + 1) * 1024], olst[hp], r=[b_olst[hp]],
                w=[b_oloc])

    def gla_update(c):
        i = c % 2
        banks = [5, 0, 1]
        for j in range(8):
            h = j // 2
            bk = banks[j % 3]
            MM(pf(bk), ksT[:, j * 128:(j + 1) * 128], vch[i][:, h * 512:(h + 1) * 512], r=[b_ksT, b_vch[i]],
               w=[pbuf[bk]], inc=True)
            STT(Sst[:, j, :], Sst[:, j, :], dec[i][:, j:j + 1], pf(bk), ALU.mult, ALU.add,
                r=[b_Sst, b_dec[i], pbuf[bk]], w=[b_Sst])
        CP("pool", Sbf.rearrange("p a b -> p (a b)"), Sst.rearrange("p a b -> p (a b)"), r=[b_Sst], w=[b_Sbf])

    m_slab = [A([128, KT, 512], BF16) for _ in range(2)]
    m_bm = [A([1, 512]) for _ in range(2)]
    m_row = [A([1, 512]) for _ in range(2)]
    b_mslab, b_mbm, b_mrow = [Buf(), Buf()], [Buf(), Buf()], [Buf(), Buf()]
    spc = (16 + NT - 1) // NT

    def m_load(j):
        if j < 24:
            mod_load(j, m_slab[j % 2], b_mslab[j % 2], m_bm[j % 2], b_mbm[j % 2])

    def m_comp(j):
        if j < 24:
            mod_slab(j, m_slab[j % 2], b_mslab[j % 2], m_bm[j % 2], b_mbm[j % 2], m_row[j % 2], b_mrow[j % 2], 2,
                     load=False)
    next_j = 8
    m_load(next_j)
    gla_prep(0)
    for c in range(NT):
        gla_pe(c)
        if c + 1 < NT:
            gla_prep(c + 1)
        gla_update(c)
        for _ in range(spc):
            m_load(next_j + 1)
            m_comp(next_j)
            next_j += 1
    while next_j < 24:
        m_load(next_j + 1)
        m_comp(next_j)
        next_j += 1
    b_sloc = [Buf() for _ in range(4)]
    b_sall = [Buf() for _ in range(4)]
    for q in range(4):
        DMA("sp", sloc_q[q][:, 0:512].rearrange("(j p) e -> p j e", p=128), Sst[:, 2 * q:2 * q + 2, :], r=[b_Sst],
            w=[b_sloc[q]])
        DMA("sp", sloc_q[q][:, 512:513].rearrange("(j p) e -> p j e", p=128), cumB[:, 2 * q:2 * q + 2].unsqueeze(2),
            r=[b_cum], w=[b_sloc[q]], slow=True)
        if solo:
            for i3 in range(4):
                DMA("sp", sall_q[q][i3 * 256:(i3 + 1) * 256, :], sloc_q[q], r=[b_sloc[q]], w=[b_sall[q]])
        else:
            P.add("pool", lambda e, q=q: e.collective_compute("AllGather", ALU.bypass,
                                                              replica_groups=[[0, 1, 2, 3], [4, 5, 6, 7]],
                                                              ins=[sloc_q[q].opt()], outs=[sall_q[q].opt()]),
                  [b_sloc[q]], [b_sall[q]], cc=True)

    if stage_end():
        return nc
    A.off = S3_KEEP
    P.stage = "stage_3b"
    Sin = Sst
    sl = [A([128, 8, 520]) for _ in range(2)]
    b_sl = [Buf(), Buf()]
    De = A([128, 8])
    b_De = Buf()
    P.add("pool", lambda e: e.memset(Sin.rearrange("p a b -> p (a b)"), 0.0), r=[b_Sst], w=[b_Sst])
    for i3 in range(3):
        i = i3 % 2
        for q in range(4):
            DMA("sp", sl[i][:, 2 * q:2 * q + 2, :],
                sall_q[q][i3 * 256:(i3 + 1) * 256, :].rearrange("(j p) e -> p j e", p=128), r=[b_sall[q]],
                w=[b_sl[i]])
        ACT(De, sl[i][:, :, 512], AF.Exp, r=[b_sl[i], bK], w=[b_De], scale=cfg[:, 8 + i3:9 + i3])
        for j in range(8):
            eng = "dve"
            TS(eng, Sin[:, j, :], Sin[:, j, :], De[:, j:j + 1], ALU.mult, r=[b_Sst, b_De], w=[b_Sst])
            STT(Sin[:, j, :], sl[i][:, j, 0:512], cfg[:, 2 + i3:3 + i3], Sin[:, j, :], ALU.mult, ALU.add,
                r=[b_sl[i], b_Sst, bK], w=[b_Sst])
    CP("act", Sbf.rearrange("p a b -> p (a b)"), Sin.rearrange("p a b -> p (a b)"), r=[b_Sst], w=[b_Sbf])

    P.stage = "stage_3c"
    olt = [A([128, 2048]) for _ in range(2)]
    b_olt = [Buf(), Buf()]
    sogt = [A([128, 2048], BF16) for _ in range(2)]
    b_sogt = [Buf(), Buf()]
    gno = A([128, 4, 512])
    b_gno = Buf()
    osum = A([128, 4, 512])
    b_osum = Buf()
    junk3 = A([128, 512], BF16)
    b_junk3 = Buf()
    obb = [A([128, 2048], BF16) for _ in range(2)]
    b_obb = [Buf(), Buf()]
    obst = [A([128, KT, 128], BF16) for _ in range(2)]
    b_obst = [Buf(), Buf()]
    sm3 = [A([128, 16]) for _ in range(2)]
    b_sm3 = [Buf(), Buf()]
    b_obT = Buf()
    for c in range(NT):
        i = c % 2
        cc = slice(c * 128, (c + 1) * 128)
        DMA("sp", olt[i], oloc_d[cc, :], r=[b_oloc], w=[b_olt[i]])
        DMA("sp", sogt[i], sog_d[cc, :], r=[sbuf_for("sog")], w=[b_sogt[i]])
        for h in range(4):
            for dt_ in range(2):
                MM(pf(h), qcT[:, 2 * h + dt_, cc], Sbf[:, 2 * h + dt_, :], start=(dt_ == 0), stop=(dt_ == 1),
                   r=[b_qcT, b_Sbf], w=[pbuf[h]], inc=(dt_ == 1))
        TT("dve", osum.rearrange("p a b -> p (a b)"), PF[:, 0:2048], olt[i], ALU.add,
           r=[pbuf[0], pbuf[1], pbuf[2], pbuf[3], b_olt[i]], w=[b_osum])
        TT("dve", gno, sogt[i].rearrange("p (h e) -> p h e", h=4), bc_mid(glan, 4), ALU.mult,
           r=[b_sogt[i], bK], w=[b_gno])
        ssq4, tm4, rs4 = sm3[i][:, 0:4], sm3[i][:, 4:8], sm3[i][:, 8:12]
        for h in range(4):
            ACT(junk3, osum[:, h, :], AF.Square, r=[b_osum], w=[b_junk3, b_sm3[i]], accum=ssq4[:, h:h + 1])
        TS("dve", tm4, ssq4, 1.0 / 512, ALU.mult, EPS, ALU.add, r=[b_sm3[i]], w=[b_sm3[i]])
        ACT(tm4, tm4, AF.Sqrt, r=[b_sm3[i]], w=[b_sm3[i]])
        RECIP(rs4, tm4, r=[b_sm3[i]], w=[b_sm3[i]])
        for h in range(4):
            STT(obb[i][:, h * 512:(h + 1) * 512], osum[:, h, :], rs4[:, h:h + 1], gno[:, h, :], ALU.mult, ALU.mult,
                r=[b_osum, b_sm3[i], b_gno], w=[b_obb[i]])
        transpose16(obb[i], b_obb[i], lambda h, i=i: obst[i][:, h * 8:(h + 1) * 8, :], [b_obst[i]])
        DMA("sp", obT_d[:, cc].rearrange("(k p) c -> p k c", p=128), obst[i], r=[b_obst[i]], w=[b_obT])

    if stage_end():
        return nc

    P.stage = "stage_4a"
    A.off = PERSIST
    Wa = A([128, 8, 2048], BF16)
    Wb = A([128, 16, 2048], BF16)
    b_W = Buf()
    oag = [A([128, 8, 512], BF16) for _ in range(2)]
    obg = [A([128, 16, 512], BF16) for _ in range(2)]
    b_oag = [Buf(), Buf()]
    b_obg = [Buf(), Buf()]
    sga = [A([128, 512], BF16) for _ in range(2)]
    sgb = [A([128, 512], BF16) for _ in range(2)]
    b_sga = [Buf(), Buf()]
    b_sgb = [Buf(), Buf()]
    t1 = [A([128, 512]) for _ in range(2)]
    t2 = [A([128, 512]) for _ in range(2)]
    b_t1 = [Buf(), Buf()]
    b_t2 = [Buf(), Buf()]
    mst = [A([128, 512], BF16) for _ in range(2)]
    b_mst = [Buf(), Buf()]
    b_mgT = Buf()
    for q4 in range(4):
        load_w(Wa[:, :, q4 * 512:(q4 + 1) * 512], w_bra, q4 * 512, 512, [b_W])
        load_w(Wb[:, :, q4 * 512:(q4 + 1) * 512], w_brb, q4 * 512, 512, [b_W])
    it = 0
    for g in range(NG):
        gi = g % 2
        gs = slice(g * 512, (g + 1) * 512)
        DMA("sp", oag[gi], oaT_d[:, gs].rearrange("(k p) c -> p k c", p=128), r=[b_oaT], w=[b_oag[gi]])
        DMA("sp", obg[gi], obT_d[:, gs].rearrange("(k p) c -> p k c", p=128), r=[b_obT], w=[b_obg[gi]])
        for f in range(16):
            i = it % 2
            it += 1
            fs = slice(f * 128, (f + 1) * 128)
            DMA("sp", sga[i], sgaT_d[fs, gs], r=[sbuf_for("sgaT")], w=[b_sga[i]])
            DMA("sp", sgb[i], sgbT_d[fs, gs], r=[sbuf_for("sgbT")], w=[b_sgb[i]])
            ba, bb = (0, 1) if i == 0 else (2, 3)
            for k in range(8):
                MM(pf(ba), Wa[:, k, fs], oag[gi][:, k, :], start=(k == 0), stop=(k == 7), r=[b_W, b_oag[gi]],
                   w=[pbuf[ba]], inc=(k == 7))
            for k in range(16):
                MM(pf(bb), Wb[:, k, fs], obg[gi][:, k, :], start=(k == 0), stop=(k == 15), r=[b_W, b_obg[gi]],
                   w=[pbuf[bb]], inc=(k == 15))
            TT("dve", t1[i], pf(ba), sga[i], ALU.mult, r=[pbuf[ba], b_sga[i]], w=[b_t1[i]])
            TT("dve", t2[i], pf(bb), sgb[i], ALU.mult, r=[pbuf[bb], b_sgb[i]], w=[b_t2[i]])
            TT("pool", mst[i], t1[i], t2[i], ALU.add, r=[b_t1[i], b_t2[i]], w=[b_mst[i]])
            DMA("sp", mgT_d[fs, gs], mst[i], r=[b_mst[i]], w=[b_mgT])

    if stage_end():
        return nc

    P.stage = "stage_4b"
    A.off = PERSIST
    Wo = A([128, KT, 2048], BF16)
    b_Wo = Buf()
    bcA = A([128, D])
    bcB = A([128, D])
    bcC = A([128, D])
    tmpf = A([128, D])
    b_bc = Buf()
    b_tmpf = Buf()
    junk = A([128, D], BF16)
    b_junk = Buf()
    mgt = [A([128, KT, 128], BF16) for _ in range(2)]
    b_mgt = [Buf(), Buf()]
    xt = [A([128, D]) for _ in range(2)]
    b_xt = [Buf(), Buf()]
    x1t = A([128, D])
    b_x1t = Buf()
    hb = [A([128, D], BF16) for _ in range(1)]
    b_hb = [Buf()]
    h2st = [A([128, KT, 128], BF16) for _ in range(2)]
    b_h2st = [Buf(), Buf()]
    b_x1 = Buf()
    b_h2T = Buf()
    for q4 in range(4):
        load_w(Wo[:, :, q4 * 512:(q4 + 1) * 512], w_out, q4 * 512, 512, [b_Wo])
    DMA("sp", bcA, mod_d[2 * D:3 * D].partition_broadcast(128), r=[b_mod], w=[b_bc])
    DMA("sp", tmpf, npost1.partition_broadcast(128), w=[b_tmpf])
    TT("dve", bcA, bcA, tmpf, ALU.mult, r=[b_bc, b_tmpf], w=[b_bc])
    DMA("sp", bcB, npre2.partition_broadcast(128), w=[b_bc])
    DMA("sp", tmpf, mod_d[4 * D:5 * D].partition_broadcast(128), r=[b_mod, b_bc], w=[b_tmpf])
    STT(bcB, tmpf, 1.0, bcB, ALU.add, ALU.mult, r=[b_bc, b_tmpf], w=[b_bc])
    DMA("sp", bcC, mod_d[3 * D:4 * D].partition_broadcast(128), r=[b_mod], w=[b_bc])

    def norm2_and_T(src, np_, b_src, dst_fn, b_dst, sidx):
        norm_tile(src, np_, bcB, bcC, hb[0][0:np_], b_src, b_hb[0], sidx)
        transpose16(hb[0][0:np_], b_hb[0], dst_fn, b_dst, np_=np_)

    order = [NT - 1] + list(range(NT - 1))
    b_xl = Buf()
    b_xla = Buf()
    ysb = A([128, D])
    b_ysb = Buf()

    def mm4b(n_):
        ti = order[n_]
        i = n_ % 2
        cc = slice(ti * 128, (ti + 1) * 128)
        DMA("sp", mgt[i], mgT_d[:, cc].rearrange("(k p) c -> p k c", p=128), r=[b_mgT], w=[b_mgt[i]])
        DMA("sp", xt[i], x_d[cc, :], w=[b_xt[i]])
        for s4 in range(4):
            for k in range(KT):
                MM(pf(s4), mgt[i][:, k, :], Wo[:, k, s4 * 512:(s4 + 1) * 512], start=(k == 0), stop=(k == KT - 1),
                   r=[b_mgt[i], b_Wo], w=[pbuf[s4]], inc=(k == KT - 1))
        CP("act", ysb, PF[:, 0:2048], r=[pbuf[0], pbuf[1], pbuf[2], pbuf[3]], w=[b_ysb])

    mm4b(0)
    for n_, ti in enumerate(order):
        i = n_ % 2
        cc = slice(ti * 128, (ti + 1) * 128)
        ss, tm, rs = small[:, 16:17], small[:, 17:18], small[:, 18:19]
        ACT(junk, ysb, AF.Square, r=[b_ysb], w=[b_junk, b_small], accum=ss)
        rms_scale(ss, tm, rs, r=[b_small], w=[b_small])
        STT(tmpf, ysb, rs, bcA, ALU.mult, ALU.mult, r=[b_ysb, b_small, b_bc], w=[b_tmpf])
        if n_ + 1 < NT:
            mm4b(n_ + 1)
        TT("pool", x1t, tmpf, xt[i], ALU.add, r=[b_tmpf, b_xt[i]], w=[b_x1t])
        DMA("sp", x1_d[cc, :], x1t, r=[b_x1t], w=[b_x1])
        if n_ == 0:
            DMA("sp", xl_d, x1t[126:128, :], r=[b_x1t], w=[b_xl])
            if solo:
                for i3 in range(4):
                    DMA("sp", xla_d[2 * i3:2 * i3 + 2, :], xl_d, r=[b_xl], w=[b_xla])
            else:
                P.add("pool", lambda e: e.collective_compute("AllGather", ALU.bypass,
                                                             replica_groups=[[0, 1, 2, 3], [4, 5, 6, 7]],
                                                             ins=[xl_d.opt()], outs=[xla_d.opt()]), [b_xl], [b_xla],
                      cc=True)
        norm2_and_T(x1t, 128, b_x1t, lambda h, i=i: h2st[i][:, h * 8:(h + 1) * 8, :], [b_h2st[i]], 20)
        DMA("sp", h2T_d[:, cc].rearrange("(k p) c -> p k c", p=128), h2st[i], r=[b_h2st[i]], w=[b_h2T])
    xc = Wo.rearrange("p a b -> p (a b)")[0:2, :].bitcast(F32)[:, 0:3 * D].rearrange("p (a b) -> p a b", a=3)
    b_xc = Buf()
    xhh = x1t[0:2, :]
    DMA("sp", xc, xla_d[0:6, :].rearrange("(i r) d -> r i d", r=2), r=[b_xla], w=[b_xc, b_Wo])
    TS("dve", xhh, xc[:, 0, :], cfg[0:2, 5:6], ALU.mult, r=[b_xc, bK], w=[b_x1t])
    for i3 in (1, 2):
        STT(xhh, xc[:, i3, :], cfg[0:2, 5 + i3:6 + i3], xhh, ALU.mult, ALU.add, r=[b_xc, bK, b_x1t], w=[b_x1t])
    hst = A([128, KT, 2], BF16)
    b_hst = Buf()
    norm2_and_T(xhh, 2, b_x1t, lambda h: hst[:, h * 8:(h + 1) * 8, :], [b_hst], 24)
    TS("dve", h2Th.rearrange("p a b -> p (a b)"), hst.rearrange("p a b -> p (a b)"), cfg[:, 0:1], ALU.mult,
       r=[b_hst, bK], w=[b_h2Th])

    if stage_end():
        return nc

    P.stage = "stage_6"
    A.off = PERSIST
    h2T = A([128, KT, T], BF16)
    b_h2 = [Buf() for _ in range(NG)]
    for g in range(NG):
        DMA("sp", h2T[:, :, g * 512:(g + 1) * 512], h2T_d[:, g * 512:(g + 1) * 512].rearrange("(k p) c -> p k c",
                                                                                                p=128),
            r=[b_h2T], w=[b_h2[g]])
    CH = min(1024, T)
    NCH = T // CH
    us = [[A([128, KT, 256], BF16) for _ in range(2)] for _ in range(2)]
    b_us = [[Buf(), Buf()] for _ in range(2)]
    U = [[A([128, 2 + T]) for _ in range(2)] for _ in range(2)]
    b_U = [[Buf(), Buf()] for _ in range(2)]
    Cb = [[A([128, CH]) for _ in range(2)] for _ in range(2)]
    b_C = [[Buf(), Buf()] for _ in range(2)]
    Gs = [A([128, CH]) for _ in range(2)]
    b_Gs = [Buf(), Buf()]
    ast = [A([128, CH], BF16) for _ in range(2)]
    b_ast = [Buf(), Buf()]
    b_actT = Buf()
    bkc = 0
    def load_us(sx):
        sj = sx % 2
        load_w(us[sj][0], w_up, sx * 256, 256, [b_us[sj][0]])
        load_w(us[sj][1], w_up, DFF + sx * 256, 256, [b_us[sj][1]])
    load_us(0)
    for f in range(FT):
        sidx = f // 2
        if f % 2 == 0 and sidx + 1 < FT // 2:
            load_us(sidx + 1)
        si = sidx % 2
        fo = (f % 2) * 128
        ui = f % 2
        for hv in range(2):
            Ub = U[ui][hv]
            fcol = f + hv * FT
            bk = bkc % 6
            bkc += 1
            for k in range(KT):
                MM(pf(bk, 2), us[si][hv][:, k, fo:fo + 128], h2Th[:, k, :], start=(k == 0), stop=(k == KT - 1),
                   r=[b_us[si][hv], b_h2Th], w=[pbuf[bk]], inc=(k == KT - 1))
            CP("act", Ub[:, 0:2], pf(bk, 2), r=[pbuf[bk]], w=[b_U[ui][hv]])
            for g in range(NG):
                bk = bkc % 6
                bkc += 1
                for k in range(KT):
                    MM(pf(bk), us[si][hv][:, k, fo:fo + 128], h2T[:, k, g * 512:(g + 1) * 512], start=(k == 0),
                       stop=(k == KT - 1), r=[b_us[si][hv], b_h2[g]], w=[pbuf[bk]], inc=(k == KT - 1))
                CP("act", Ub[:, 2 + g * 512:2 + (g + 1) * 512], pf(bk), r=[pbuf[bk]], w=[b_U[ui][hv]])
        for ch in range(NCH):
            ci = (f * NCH + ch) % 2
            o0 = ch * CH
            for hv in range(2):
                Ub = U[ui][hv]
                fcol = f + hv * FT
                Cc = Cb[ci][hv]
                ACT(Cc, Ub[:, 2 + o0:2 + o0 + CH], AF.Identity, r=[b_U[ui][hv], bK], w=[b_C[ci][hv]],
                    bias=convp[:, 3, fcol:fcol + 1], scale=convp[:, 2, fcol:fcol + 1])
                STT(Cc, Ub[:, 1 + o0:1 + o0 + CH], convp[:, 1, fcol:fcol + 1], Cc, ALU.mult, ALU.add,
                    r=[b_U[ui][hv], bK, b_C[ci][hv]], w=[b_C[ci][hv]])
                STT(Cc, Ub[:, o0:o0 + CH], convp[:, 0, fcol:fcol + 1], Cc, ALU.mult, ALU.add,
                    r=[b_U[ui][hv], bK, b_C[ci][hv]], w=[b_C[ci][hv]])
            ACT(Gs[ci], Cb[ci][0], AF.Silu, r=[b_C[ci][0]], w=[b_Gs[ci]])
            TT("pool", ast[ci], Gs[ci], Cb[ci][1], ALU.mult, r=[b_Gs[ci], b_C[ci][1]], w=[b_ast[ci]])
            DMA("sp", actT_d[f * 128:(f + 1) * 128, o0:o0 + CH], ast[ci], r=[b_ast[ci]], w=[b_actT])

    if stage_end():
        return nc

    P.stage = "stage_7"
    A.off = PERSIST
    TG = min(1024, T)
    NTG = T // TG
    ssq7 = A([128, NT, 8])
    S7_KEEP = A.off
    ag = [A([128, FT, 512], BF16) for _ in range(TG // 512)]
    b_ag = [Buf() for _ in range(TG // 512)]
    ds = [A([128, FT, 256], BF16) for _ in range(2)]
    b_ds = [Buf(), Buf()]
    yst = [A([128, 256]) for _ in range(4)]
    b_yst = [Buf() for _ in range(4)]
    b_ssq7 = Buf()
    junk7 = A([128, 256], BF16)
    b_junk7 = Buf()
    b_y2 = Buf()
    it = 0
    for tg in range(NTG):
        for hh in range(TG // 512):
            c0 = tg * TG + hh * 512
            for k0 in range(0, FT, 11):
                DMA("sp", ag[hh][:, k0:k0 + 11, :], actT_d[k0 * 128:(k0 + 11) * 128, c0:c0 + 512].rearrange(
                    "(k p) c -> p k c", p=128), r=[b_actT], w=[b_ag[hh]])
        for s8 in range(8):
            si = (tg * 8 + s8) % 2
            for k0 in range(0, FT, 11):
                DMA("pool", ds[si][:, k0:k0 + 11, :], wview(w_down, s8 * 256, 256)[:, k0:k0 + 11, :], w=[b_ds[si]])
            for tt in range(TG // 128):
                ti = tg * (TG // 128) + tt
                hh, toff = tt // 4, (tt % 4) * 128
                bk = it % 6
                sj = it % 4
                it += 1
                for k in range(FT):
                    MM(pf(bk, 256), ag[hh][:, k, toff:toff + 128], ds[si][:, k, :], start=(k == 0), stop=(k == FT - 1),
                       r=[b_ag[hh], b_ds[si]], w=[pbuf[bk]], inc=(k == FT - 1))
                CP("act", yst[sj], pf(bk, 256), r=[pbuf[bk]], w=[b_yst[sj]])
                DMA("sp", y2_d[ti * 128:(ti + 1) * 128, s8 * 256:(s8 + 1) * 256], yst[sj], r=[b_yst[sj]], w=[b_y2])

    if stage_end():
        return nc
    A.off = S7_KEEP
    P.stage = "stage_8"
    gp2 = A([128, D])
    tm8 = A([128, D])
    b_gp2 = Buf()
    b_tm8 = Buf()
    ss8 = A([128, NT])
    rs8 = A([128, NT])
    b_ss8 = Buf()
    y2t = [A([128, D]) for _ in range(2)]
    x1b = [A([128, D]) for _ in range(2)]
    b_y2t = [Buf(), Buf()]
    b_x1b = [Buf(), Buf()]
    b_out = Buf()
    DMA("sp", gp2, mod_d[5 * D:6 * D].partition_broadcast(128), r=[b_mod], w=[b_gp2])
    DMA("sp", tm8, npost2.partition_broadcast(128), w=[b_tm8])
    TT("dve", gp2, gp2, tm8, ALU.mult, r=[b_gp2, b_tm8], w=[b_gp2])
    junk8 = A([128, D], BF16)
    b_junk8 = Buf()
    for ti in range(NT):
        i = ti % 2
        cc = slice(ti * 128, (ti + 1) * 128)
        DMA("sp", y2t[i], y2_d[cc, :], r=[b_y2], w=[b_y2t[i]])
        DMA("sp", x1b[i], x1_d[cc, :], r=[b_x1], w=[b_x1b[i]])
        ACT(junk8, y2t[i], AF.Square, r=[b_y2t[i]], w=[b_junk8, b_ss8], accum=ss8[:, ti:ti + 1])
        TS("dve", ss8[:, ti:ti + 1], ss8[:, ti:ti + 1], 1.0 / D, ALU.mult, EPS, ALU.add, r=[b_ss8], w=[b_ss8])
        ACT(ss8[:, ti:ti + 1], ss8[:, ti:ti + 1], AF.Sqrt, r=[b_ss8], w=[b_ss8])
        RECIP(rs8[:, ti:ti + 1], ss8[:, ti:ti + 1], r=[b_ss8], w=[b_ss8])
        STT(y2t[i], y2t[i], rs8[:, ti:ti + 1], gp2, ALU.mult, ALU.mult, r=[b_y2t[i], b_ss8, b_gp2], w=[b_y2t[i]])
        TT("pool", x1b[i], x1b[i], y2t[i], ALU.add, r=[b_x1b[i], b_y2t[i]], w=[b_x1b[i]])
        DMA("sp", out_d[cc, :], x1b[i], r=[b_x1b[i]], w=[b_out])
    P.add("sp", None, r=[b_out], inc=False)
    P.barrier()
    P.emit()
    st.close()
    return nc


def host_constants():
    i = np.arange(128)[:, None]
    j = np.arange(256)[None, :]
    dist = 128 + i - j
    band = np.where((dist >= 0) & (dist < 128), 0.0, NEG).astype(np.float32)
    jj = np.arange(128)[:, None]
    ii = np.arange(128)[None, :]
    tri = (jj <= ii).astype(np.float32)
    scanm = np.ones((128, 1024), np.float32)
    scanm[:, ::128] = 0.0
    invf = (10000.0 ** (-np.arange(0, 64, 2, dtype=np.float32) / 64)).astype(np.float32)
    invf = np.broadcast_to(invf[None, :], (128, 32)).copy()
    ident = np.eye(128, dtype=np.float32)
    return dict(band=band, tri=tri, scanm=scanm, invf=invf, ident=ident)


def make_in_maps(inputs, T):
    x = np.asarray(inputs["x"], np.float32)
    c = np.asarray(inputs["c"], np.float32)
    pos = np.asarray(inputs["positions"], np.int32)
    NT = T // 128
    consts = host_constants()
    g = lambda k: np.ascontiguousarray(np.asarray(inputs[k])[0])
    shared = dict(
        w_mod=g("w_mod"), b_mod=g("b_mod"), mix_norm_pre=g("mix_norm_pre"), mix_norm_post=g("mix_norm_post"),
        w_in=g("w_in"), attn_sinks=g("attn_sinks"),
        wgk=np.ascontiguousarray(np.concatenate([g("w_gk_up"), g("b_gk")[None, :]], 0)),
        gla_norm=g("gla_norm"), w_branch_attn=g("w_branch_attn"), w_branch_gla=g("w_branch_gla"),
        w_out=g("w_out"), ffn_norm_pre=g("ffn_norm_pre"), ffn_norm_post=g("ffn_norm_post"), w_up=g("w_up"),
        w_down=g("w_down"), **consts)
    cw = g("conv_w")
    cb = g("conv_b")
    convp = np.stack([cw[0].reshape(88, 128).T, cw[1].reshape(88, 128).T, cw[2].reshape(88, 128).T,
                      cb.reshape(88, 128).T], axis=1)
    shared["convp"] = np.ascontiguousarray(convp.astype(np.float32))
    maps = []
    for r in range(8):
        b, p = r // 4, r % 4
        t0 = p * T
        m = dict(shared)
        m["x"] = np.ascontiguousarray(x[b, t0:t0 + T])
        m["xh"] = np.ascontiguousarray(x[b, t0 - 128:t0]) if p > 0 else np.zeros((128, D), np.float32)
        m["c"] = np.ascontiguousarray(c[b].reshape(KT, 128).T)
        pp = np.zeros((128, NT + 1), np.int32)
        pp[:, :NT] = pos[b, t0:t0 + T].reshape(NT, 128).T
        if p > 0:
            pp[:, NT] = pos[b, t0 - 128:t0]
        m["pos"] = pp
        cfg = np.zeros((128, 16), np.float32)
        cfg[:, 0] = 1.0 if p > 0 else 0.0
        cfg[:, 1] = 0.0 if p > 0 else NEG
        for i in range(3):
            cfg[:, 2 + i] = 1.0 if i < p else 0.0
            cfg[:, 5 + i] = 1.0 if i == p - 1 else 0.0
            cfg[:, 8 + i] = (1.0 / 16) if i < p else 0.0
        m["cfg"] = cfg
        maps.append(m)
    return maps


_CACHE = {}


def kernel(**inputs):
    S = np.asarray(inputs["x"]).shape[1]
    T = S // 4
    if T not in _CACHE:
        _CACHE[T] = build(T)
    nc = _CACHE[T]
    maps = make_in_maps(inputs, T)
    res = run_bass_kernel_spmd(nc, maps, core_ids=list(range(8)))
    out = np.zeros((2, S, D), np.float32)
    for r in range(8):
        b, p = r // 4, r % 4
        out[b, p * T:(p + 1) * T] = res.results[r]["out"]
    return out
```

```python
import contextlib
import os
import numpy as np
import concourse.bass as bass
import concourse.mybir as mybir
from concourse.bass_utils import run_bass_kernel_spmd

F32 = mybir.dt.float32
BF16 = mybir.dt.bfloat16
I32 = mybir.dt.int32
AF = mybir.ActivationFunctionType
ALU = mybir.AluOpType
AX = mybir.AxisListType

D = 2048
KT = 16
DFF = 5632
FT = DFF // 128
EPS = 1e-6
NEG = -30000.0
TWO_PI = float(2 * np.pi)
COMPUTE = ("pe", "act", "dve", "pool")


class Buf:
    __slots__ = ("w", "r")

    def __init__(self):
        self.w = None
        self.r = []


class Op:
    __slots__ = ("eng", "fn", "deps", "dma", "sem", "val", "inc", "rank", "stage")


class Prog:
    def __init__(self, nc):
        self.nc = nc
        self.ops = {e: [] for e in ("pe", "act", "dve", "pool", "sp")}
        self.n_dma_sems = {"sp": 10, "act": 4, "pool": 8}
        self.dma_use = {}
        self.dma_rr = {q: 0 for q in self.n_dma_sems}
        self.dma_last = {}

    def add(self, eng, fn, r=(), w=(), inc=None, dma=False, extra=(), cc=False):
        op = Op()
        op.eng = eng
        op.stage = getattr(self, "stage", "s")
        op.fn = fn
        op.dma = dma
        op.sem = None
        op.val = 0
        op.rank = None
        if inc is None:
            inc = eng != "pe"
        op.inc = inc or dma
        deps = list(extra)
        for b in r:
            if b.w is not None:
                deps.append(b.w)
        for b in w:
            if b.w is not None:
                deps.append(b.w)
            deps.extend(b.r)
        if cc:
            op.dma = True
            op.inc = True
            self.n_cc = getattr(self, "n_cc", 0) + 1
            op.sem = ("cc", self.n_cc)
            op.val = 1
        elif dma:
            slot = self.dma_rr[eng]
            self.dma_rr[eng] = (slot + 1) % self.n_dma_sems[eng]
            key = (eng, slot)
            prev = self.dma_last.get(key)
            if prev is not None:
                deps.append(prev)
            self.dma_last[key] = op
            n = self.dma_use.get(key, 0) + 1
            self.dma_use[key] = n
            op.sem = key
            op.val = 16 * n
        op.deps = deps
        self.ops[eng].append(op)
        for b in r:
            if not op.dma:
                b.r = [o for o in b.r if o.dma or o.eng != eng]
            b.r.append(op)
        for b in w:
            b.w = op
            b.r = []
        return op

    def pe(self, fn, r=(), w=(), inc=False):
        return self.add("pe", fn, r, w, inc=inc)

    def barrier(self):
        last = []
        for e in self.ops:
            lst = [o for o in self.ops[e] if o.fn is not None and not o.dma]
            if lst:
                if e == "pe":
                    lst[-1].inc = True
                last.append(lst[-1])
        last.extend(self.dma_last.values())
        last.extend(o for o in self.ops["pool"] if o.dma and o.sem[0] == "cc")
        for e in self.ops:
            self.add(e, None, inc=False, extra=list(last))

    def emit(self):
        nc = self.nc
        lst = [o for o in self.ops["pe"] if o.fn is not None]
        if lst:
            lst[-1].inc = True
        for e in self.ops:
            rank = 0
            pend = []
            for op in self.ops[e]:
                if op.dma or op.fn is None:
                    continue
                if op.inc:
                    rank += 1
                    op.rank = rank
                    for p in pend:
                        p.rank = rank
                    pend = []
                else:
                    pend.append(op)
            assert not pend, (e, len(pend))
        with contextlib.ExitStack() as st:
            sems = {}
            for e in COMPUTE:
                sems[e] = st.enter_context(nc.semaphore("s_" + e))
            for q, n in self.n_dma_sems.items():
                for i in range(n):
                    sems[(q, i)] = st.enter_context(nc.semaphore(f"d_{q}{i}"))
            for i in range(getattr(self, "n_cc", 0)):
                sems[("cc", i + 1)] = st.enter_context(nc.semaphore(f"cc{i}"))
            block = st.enter_context(nc.Block())

            def ev(op):
                if op.dma:
                    return op.sem, op.val
                return op.eng, op.rank

            prof = bool(os.environ.get("KPROF"))

            def replay(ename, eobj):
                known = {}
                cur = [None, None]
                for op in self.ops[ename]:
                    if prof and op.stage != cur[0]:
                        if cur[1] is not None:
                            cur[1].__exit__(None, None, None)
                        cur[0] = op.stage
                        cur[1] = nc.named_scope(op.stage)
                        cur[1].__enter__()
                    _replay_one(ename, eobj, op, known)
                if cur[1] is not None:
                    cur[1].__exit__(None, None, None)

            def _replay_one(ename, eobj, op, known):
                if True:
                    need = {}
                    for d in op.deps:
                        if d.fn is None:
                            continue
                        if d.eng == ename and not d.dma and ename in ("pe", "sp"):
                            continue
                        k, v = ev(d)
                        if known.get(k, 0) < v and need.get(k, 0) < v:
                            need[k] = v
                    for k, v in need.items():
                        eobj.wait_ge(sems[k], v)
                        known[k] = v
                    if op.fn is None:
                        return
                    ins = op.fn(eobj)
                    if op.dma and op.sem[0] == "cc":
                        ins.then_inc(sems[op.sem])
                    elif op.dma:
                        ins.then_inc(sems[op.sem], 16)
                    elif op.inc:
                        ins.then_inc(sems[ename], 1)

            @block.tensor
            def _(e):
                replay("pe", e)

            @block.scalar
            def _(e):
                replay("act", e)

            @block.vector
            def _(e):
                replay("dve", e)

            @block.gpsimd
            def _(e):
                replay("pool", e)

            @block.sync
            def _(e):
                replay("sp", e)


def build(T, dbg=False, stop_at=0, solo=False):
    NT = T // 128
    NG = T // 512
    NT1 = NT + 1
    TH = T + 128
    nc = bass.Bass("TRN2", target_bir_lowering=False)

    def din(name, shape, dt=F32):
        return nc.dram_tensor(name, list(shape), dt, kind="ExternalInput").ap()

    def dscr(name, shape, dt=F32, internal=False):
        if dbg and not internal:
            return nc.dram_tensor(name, list(shape), dt, kind="ExternalOutput").ap()
        return nc.dram_tensor(name, list(shape), dt).ap()

    x_d = din("x", [T, D])
    xh_d = din("xh", [128, D])
    c_d = din("c", [128, KT])
    pos_d = din("pos", [128, NT1], I32)
    cfg_d = din("cfg", [128, 16])
    invf_d = din("invf", [128, 32])
    ident_d = din("ident", [128, 128])
    band_d = din("band", [128, 256])
    tri_d = din("tri", [128, 128])
    scanm_d = din("scanm", [128, 1024])
    w_mod = din("w_mod", [D, 6 * D])
    bmod_in = din("b_mod", [6 * D])
    npre1 = din("mix_norm_pre", [D])
    npost1 = din("mix_norm_post", [D])
    w_in = din("w_in", [D, 11792])
    sinks_d = din("attn_sinks", [16])
    wgk_d = din("wgk", [17, 1024])
    glan_d = din("gla_norm", [512])
    w_bra = din("w_branch_attn", [1024, D])
    w_brb = din("w_branch_gla", [2048, D])
    w_out = din("w_out", [D, D])
    npre2 = din("ffn_norm_pre", [D])
    npost2 = din("ffn_norm_post", [D])
    w_up = din("w_up", [D, 2 * DFF])
    convp_d = din("convp", [128, 4, 2 * FT])
    w_down = din("w_down", [DFF, D])
    out_d = nc.dram_tensor("out", [T, D], F32, kind="ExternalOutput").ap()

    mod_d = dscr("mod_s", [6 * D])
    oaT_d = dscr("oaT_s", [1024, T], BF16)
    qbT_d = dscr("qbT_s", [1024, T])
    kbT_d = dscr("kbT_s", [1024, T])
    vb_d = dscr("vb_s", [T, 2048], BF16)
    sog_d = dscr("sog_s", [T, 2048], BF16)
    sgaT_d = dscr("sgaT_s", [2048, T], BF16)
    sgbT_d = dscr("sgbT_s", [2048, T], BF16)
    oloc_d = dscr("oloc_s", [T, 2048])
    sloc_q = [dscr(f"sloc{q}_s", [256, 520], internal=True) for q in range(4)]
    sall_q = [dscr(f"sall{q}_s", [4 * 256, 520], internal=True) for q in range(4)]
    obT_d = dscr("obT_s", [2048, T], BF16)
    mgT_d = dscr("mgT_s", [2048, T], BF16)
    x1_d = dscr("x1_s", [T, D])
    h2T_d = dscr("h2T_s", [2048, T], BF16)
    xl_d = dscr("xl_s", [2, D], internal=True)
    xla_d = dscr("xla_s", [8, D], internal=True)
    actT_d = dscr("actT_s", [DFF, T], BF16)
    y2_d = dscr("y2_s", [T, D])

    P = Prog(nc)
    st = contextlib.ExitStack()
    nstage = [0]

    def stage_end():
        nstage[0] += 1
        P.barrier()
        if nstage[0] == stop_at:
            P.emit()
            st.close()
            return True
        return False

    SB_BYTES = 206 * 1024
    big = st.enter_context(nc.sbuf_tensor("big", [128, SB_BYTES // 4], F32))
    PF = st.enter_context(nc.psum_tensor("PF", [128, 3072], F32))
    PB = st.enter_context(nc.psum_tensor("PB", [128, 2048], BF16))
    pbuf = [Buf() for _ in range(8)]

    def pf(bank, n=512, off=0):
        return PF[:, bank * 512 + off: bank * 512 + off + n]

    def pb(bank, n=1024, off=0):
        return PB[:, (bank - 6) * 1024 + off:(bank - 6) * 1024 + off + n]

    class Alloc:
        def __init__(self):
            self.off = 0

        def __call__(self, shape, dt=F32):
            esz = 4 if dt in (F32, I32) else 2
            n = int(np.prod(shape[1:])) * esz
            n = (n + 31) // 32 * 32
            assert self.off + n <= SB_BYTES, ("SBUF overflow", self.off + n)
            a = big[:, self.off // 4:(self.off + n) // 4]
            self.off += n
            if dt != F32:
                a = a.bitcast(dt)
            a = a[:, 0:int(np.prod(shape[1:]))]
            if len(shape) == 3:
                a = a.rearrange("p (a b) -> p a b", a=shape[1])
            elif len(shape) == 4:
                a = a.rearrange("p (a b c) -> p a b c", a=shape[1], b=shape[2])
            if shape[0] != 128:
                a = a[0:shape[0]]
            return a

    A = Alloc()

    def MM(out, lhsT, rhs, start=True, stop=True, r=(), w=(), inc=False, tp=None):
        kw = {} if tp is None else {"tile_position": tp}
        P.pe(lambda e: e.matmul(out, lhsT=lhsT, rhs=rhs, start=start, stop=stop, **kw), r=r, w=w, inc=inc)

    def TR(out, in_, idn, r=(), w=(), inc=False):
        P.pe(lambda e: e.transpose(out=out, in_=in_, identity=idn), r=r, w=w, inc=inc)

    def ACT(out, in_, func, r=(), w=(), bias=None, scale=None, accum=None):
        kw = {}
        if bias is not None:
            kw["bias"] = bias
        if scale is not None:
            kw["scale"] = scale
        if accum is not None:
            kw["accum_out"] = accum
        P.add("act", lambda e: e.activation(out=out, in_=in_, func=func, **kw), r, w)

    def TT(eng, out, in0, in1, op, r=(), w=()):
        P.add(eng, lambda e: e.tensor_tensor(out=out, in0=in0, in1=in1, op=op), r, w)

    def TS(eng, out, in0, s1, op0, s2=None, op1=None, r=(), w=()):
        if op1 is None:
            P.add(eng, lambda e: e.tensor_scalar(out=out, in0=in0, scalar1=s1, scalar2=None, op0=op0), r, w)
        else:
            P.add(eng, lambda e: e.tensor_scalar(out=out, in0=in0, scalar1=s1, scalar2=s2, op0=op0, op1=op1), r, w)

    def STT(out, in0, scalar, in1, op0, op1, r=(), w=()):
        P.add("dve", lambda e: e.scalar_tensor_tensor(out=out, in0=in0, scalar=scalar, in1=in1, op0=op0, op1=op1), r, w)

    def CP(eng, out, in_, r=(), w=()):
        if eng == "act":
            P.add("act", lambda e: e.activation(out=out, in_=in_, func=AF.Copy), r, w)
        else:
            P.add(eng, lambda e: e.tensor_copy(out=out, in_=in_), r, w)

    def DMA(q, out, in_, r=(), w=(), slow=False):
        if slow:
            P.add(q, lambda e: e.dma_start(out=out, in_=in_, allow_slow_non_contiguous=True), r, w, dma=True)
        else:
            P.add(q, lambda e: e.dma_start(out=out, in_=in_), r, w, dma=True)

    def RECIP(out, in_, r=(), w=()):
        P.add("dve", lambda e: e.reciprocal(out=out, in_=in_), r, w)

    def bc_mid(ap, n):
        return ap.unsqueeze(1).broadcast_to([ap.shape[0], n, ap.shape[1]])

    def bc_last(ap, n):
        return ap.unsqueeze(2).broadcast_to([ap.shape[0], ap.shape[1], n])

    def wview(w, c0, n):
        return w[:, c0:c0 + n].rearrange("(k p) n -> p k n", p=128)

    def load_w(dst, w, c0, n, bufs, nk=None):
        kt = dst.shape[1]
        step = max(1, 1024 // 128) if n >= 256 else kt
        step = min(step, kt)
        v = wview(w, c0, n)
        for k0 in range(0, kt, step):
            k1 = min(kt, k0 + step)
            DMA("pool", dst[:, k0:k1, :], v[:, k0:k1, :], w=bufs)

    def rms_scale(ssq, tmp, rstd, r, w):
        TS("dve", tmp, ssq, 1.0 / D, ALU.mult, EPS, ALU.add, r=r, w=w)
        ACT(tmp, tmp, AF.Sqrt, r=w, w=w)
        RECIP(rstd, tmp, r=w, w=w)

    ident_f = A([128, 128])
    ident = A([128, 128], BF16)
    cfg = A([128, 16])
    convp = A([128, 4, 2 * FT])
    sinkbc = A([128, 16])
    band = A([128, 256])
    band0 = A([128, 256])
    tri = A([128, 128])
    scanm = A([128, 1024])
    glan = A([128, 512])
    wgk = A([17, 1024])
    cosd = A([128, NT1, 64])
    sins = A([128, NT1, 64])
    gkT = A([17, T])
    h2Th = A([128, KT, 2], BF16)
    small = A([128, 64])
    cbP = A([128, KT], BF16)
    b_cb = Buf()
    bK = Buf()
    b_gkT = Buf()
    b_h2Th = Buf()
    b_small = Buf()
    PERSIST = A.off

    DMA("sp", ident_f, ident_d, w=[bK])
    DMA("sp", cfg, cfg_d, w=[bK])
    DMA("sp", convp, convp_d, w=[bK])
    DMA("sp", sinkbc, sinks_d.partition_broadcast(128), w=[bK])
    DMA("sp", band, band_d, w=[bK])
    DMA("sp", tri, tri_d, w=[bK])
    DMA("sp", scanm, scanm_d, w=[bK])
    DMA("sp", glan, glan_d.partition_broadcast(128), w=[bK])
    DMA("sp", wgk, wgk_d, w=[bK])
    CP("dve", ident, ident_f, r=[bK], w=[bK])
    CP("dve", band0, band, r=[bK], w=[bK])
    TS("dve", band0[:, 0:128], band[:, 0:128], cfg[:, 1:2], ALU.add, r=[bK], w=[bK])
    P.add("pool", lambda e: e.memset(gkT[0:17, :], 1.0), w=[b_gkT])

    P.stage = "stage_0"
    A.off = PERSIST
    s0_pos_i = A([128, NT1], I32)
    s0_pos_f = A([128, NT1])
    s0_invf = A([128, 32])
    s0_ang = A([128, NT1, 32])
    s0_a2 = A([128, NT1, 32])
    s0_ki = A([128, NT1, 32], I32)
    s0_kf = A([128, NT1, 32])
    s0_m = A([128, NT1, 32])
    s0_sin = A([128, NT1, 32])
    s0_cos = A([128, NT1, 32])
    bT = Buf()
    DMA("sp", s0_pos_i, pos_d, w=[bT])
    DMA("sp", s0_invf, invf_d, w=[bT])
    CP("dve", s0_pos_f, s0_pos_i, r=[bT], w=[bT])
    TT("dve", s0_ang, bc_mid(s0_invf, NT1), bc_last(s0_pos_f, 32), ALU.mult, r=[bT], w=[bT])

    def reduce_sin(dst, src, phase):
        TS("dve", s0_a2, src, phase, ALU.add, r=[bT], w=[bT])
        TS("dve", s0_ki, s0_a2, 1.0 / TWO_PI, ALU.mult, r=[bT], w=[bT])
        CP("dve", s0_kf, s0_ki, r=[bT], w=[bT])
        STT(s0_a2, s0_kf, -TWO_PI, s0_a2, ALU.mult, ALU.add, r=[bT], w=[bT])
        TS("dve", s0_m, s0_a2, float(np.pi), ALU.is_gt, r=[bT], w=[bT])
        STT(s0_a2, s0_m, -TWO_PI, s0_a2, ALU.mult, ALU.add, r=[bT], w=[bT])
        TS("dve", s0_m, s0_a2, -float(np.pi), ALU.is_lt, r=[bT], w=[bT])
        STT(s0_a2, s0_m, TWO_PI, s0_a2, ALU.mult, ALU.add, r=[bT], w=[bT])
        ACT(dst, s0_a2, AF.Sin, r=[bT], w=[bT])

    reduce_sin(s0_sin, s0_ang, 0.0)
    reduce_sin(s0_cos, s0_ang, float(np.pi / 2))
    CP("dve", cosd[:, :, 0:32], s0_cos, r=[bT], w=[bK])
    CP("dve", cosd[:, :, 32:64], s0_cos, r=[bT], w=[bK])
    TS("dve", sins[:, :, 0:32], s0_sin, -1.0, ALU.mult, r=[bT], w=[bK])
    CP("dve", sins[:, :, 32:64], s0_sin, r=[bT], w=[bK])

    s0_c = A([128, KT])
    s0_cb = A([128, KT], BF16)
    s0_slab = [A([128, KT, 512], BF16) for _ in range(2)]
    s0_bm = [A([1, 512]) for _ in range(2)]
    s0_row = [A([1, 512]) for _ in range(2)]
    b_c = Buf()
    b_sl = [Buf(), Buf()]
    b_bm = [Buf(), Buf()]
    b_row = [Buf(), Buf()]
    b_mod = Buf()
    DMA("sp", s0_c, c_d, w=[b_c])
    ACT(cbP, s0_c, AF.Silu, r=[b_c], w=[b_cb])

    def mod_load(j, slab_ap, b_slab_, bm_ap, b_bm_):
        load_w(slab_ap, w_mod, j * 512, 512, [b_slab_])
        DMA("sp", bm_ap, bmod_in[j * 512:(j + 1) * 512].unsqueeze(0), w=[b_bm_])

    def mod_slab(j, slab_ap, b_slab_, bm_ap, b_bm_, row_ap, b_row_, bank, load=True):
        if load:
            mod_load(j, slab_ap, b_slab_, bm_ap, b_bm_)
        for k in range(KT):
            MM(pf(bank)[0:1, :], cbP[:, k:k + 1], slab_ap[:, k, :], start=(k == 0), stop=(k == KT - 1),
               r=[b_cb, b_slab_], w=[pbuf[bank]], inc=(k == KT - 1))
        TT("dve", row_ap, pf(bank)[0:1, :], bm_ap, ALU.add, r=[pbuf[bank], b_bm_], w=[b_row_])
        DMA("sp", mod_d[j * 512:(j + 1) * 512].unsqueeze(0), row_ap, r=[b_row_], w=[b_mod])

    for j in range(8):
        i = j % 2
        mod_slab(j, s0_slab[i], b_sl[i], s0_bm[i], b_bm[i], s0_row[i], b_row[i], i)

    if stage_end():
        return nc

    P.stage = "stage_1"
    A.off = PERSIST
    hT = A([128, KT, TH], BF16)
    b_hT = [Buf() for _ in range(NT1)]
    HT_END = A.off
    bcA = A([128, D])
    bcB = A([128, D])
    bcC = A([128, D])
    b_bc = Buf()
    xt = [A([128, D]) for _ in range(2)]
    b_xt = [Buf(), Buf()]
    junk = A([128, D], BF16)
    b_junk = Buf()
    tmpf = A([128, D])
    b_tmpf = Buf()
    hb = [A([128, D], BF16) for _ in range(2)]
    b_hb = [Buf(), Buf()]
    DMA("sp", bcC, mod_d[D:2 * D].partition_broadcast(128), r=[b_mod], w=[b_bc])
    DMA("sp", bcA, npre1.partition_broadcast(128), w=[b_bc])
    DMA("sp", bcB, mod_d[0:D].partition_broadcast(128), r=[b_mod], w=[b_bc])
    STT(bcA, bcC, 1.0, bcA, ALU.add, ALU.mult, r=[b_bc], w=[b_bc])

    def norm_tile(src_ap, np_, wmod, shift, hb_ap, b_src, b_hb_i, sidx):
        ss = small[0:np_, sidx:sidx + 1]
        tm = small[0:np_, sidx + 1:sidx + 2]
        rs = small[0:np_, sidx + 2:sidx + 3]
        ACT(junk[0:np_], src_ap, AF.Square, r=[b_src], w=[b_junk, b_small], accum=ss)
        rms_scale(ss, tm, rs, r=[b_small], w=[b_small])
        STT(tmpf[0:np_], src_ap, rs, wmod[0:np_], ALU.mult, ALU.mult, r=[b_src, b_small, b_bc], w=[b_tmpf])
        TT("pool", hb_ap, tmpf[0:np_], shift[0:np_], ALU.add, r=[b_tmpf, b_bc], w=[b_hb_i])

    def transpose16(hb_ap, b_hb_i, dst_fn, b_dst, np_=128):
        for h in range(2):
            for kk in range(8):
                k = h * 8 + kk
                TR(pb(6 + h)[:, kk * 128:kk * 128 + np_], hb_ap[:, k * 128:(k + 1) * 128], ident[0:np_, 0:np_],
                   r=[b_hb_i, bK], w=[pbuf[6 + h]], inc=(kk == 7))
            src = pb(6 + h).rearrange("p (k c) -> p k c", k=8)[:, :, 0:np_]
            CP("act" if (h == 0 or np_ != 128) else "dve", dst_fn(h), src, r=[pbuf[6 + h]], w=b_dst)

    for ti in range(NT1):
        i = ti % 2
        src = xh_d if ti == NT else x_d[ti * 128:(ti + 1) * 128, :]
        DMA("sp", xt[i], src, w=[b_xt[i]])
        norm_tile(xt[i], 128, bcA, bcB, hb[i], b_xt[i], b_hb[i], 4 * i)
        c0 = T if ti == NT else ti * 128
        transpose16(hb[i], b_hb[i], lambda h, c0=c0: hT[:, h * 8:(h + 1) * 8, c0:c0 + 128], [b_hT[ti]])

    if stage_end():
        return nc

    P.stage = "stage_2a"
    A.off = HT_END
    kvslab = A([128, KT, 512], BF16)
    qslab = [kvslab[:, :, 0:256], kvslab[:, :, 256:512]]
    b_kvs = Buf()
    b_qs = [Buf(), Buf()]
    kT = A([64, 4, TH], BF16)
    b_kT = Buf()
    vA = A([128, NT1, 256], BF16)
    b_vA = Buf()
    qT = A([64, 4, T], BF16)
    b_qT = Buf()
    rA = A([128, 4, 64])
    rB = A([128, 4, 64])
    rX = A([128, 4, 64])
    b_rX = Buf()
    rR = [A([128, 4, 64], BF16) for _ in range(2)]
    b_rA, b_rB = Buf(), Buf()
    b_rR = [Buf(), Buf()]
    _sb = A([128, 4, 256])
    S_sb = [_sb, _sb]
    _bS = Buf()
    b_S = [_bS, _bS]
    _pe = A([128, 4, 256], BF16)
    Pe = [_pe, _pe]
    _bPe = Buf()
    b_Pe = [_bPe, _bPe]
    qpb = [Buf(), Buf()]
    Obuf = [Buf(), Buf()]
    Pn = [A([128, 4, 256], BF16) for _ in range(2)]
    b_Pn = [Buf(), Buf()]
    PTs = [A([128, 8, 128], BF16) for _ in range(2)]
    b_PT = [Buf(), Buf()]
    ost = [A([128, 2, 128], BF16) for _ in range(2)]
    b_ost = [Buf(), Buf()]
    sm = [A([128, 32]) for _ in range(2)]
    b_sm = [Buf(), Buf()]
    b_oaT = Buf()

    def rope(src3, nh, ti, dst, b_src, b_dst):
        cs = bc_mid(cosd[:, ti, :], nh)
        CP("act", rX[:, 0:nh, :], src3, r=[b_src], w=[b_rX])
        TT("dve", rA[:, 0:nh, :], rX[:, 0:nh, :], cs, ALU.mult, r=[b_rX, bK], w=[b_rA])
        TT("dve", rB[:, 0:nh, 0:32], rX[:, 0:nh, 32:64], bc_mid(sins[:, ti, 0:32], nh), ALU.mult,
           r=[b_rX, bK], w=[b_rB])
        TT("dve", rB[:, 0:nh, 32:64], rX[:, 0:nh, 0:32], bc_mid(sins[:, ti, 32:64], nh), ALU.mult,
           r=[b_rX, bK], w=[b_rB])
        TT("dve", dst, rA[:, 0:nh, :], rB[:, 0:nh, :], ALU.add, r=[b_rA, b_rB], w=[b_dst])

    def swa_gen():
        load_w(kvslab, w_in, 1024, 512, [b_kvs])
        for ti in range(NT1):
            i = ti % 2
            c0 = T if ti == NT else ti * 128
            for k in range(KT):
                MM(pf(0), hT[:, k, c0:c0 + 128], kvslab[:, k, :], start=(k == 0), stop=(k == KT - 1),
                   r=[b_hT[ti], b_kvs], w=[pbuf[0]], inc=(k == KT - 1))
            rope(pf(0, 256).rearrange("p (h d) -> p h d", h=4), 4, ti, rR[i], pbuf[0], b_rR[i])
            CP("act", vA[:, ti, :], pf(0, 256, 256), r=[pbuf[0]], w=[b_vA])
            for h in range(4):
                TR(pb(6 + i)[0:64, h * 128:(h + 1) * 128], rR[i][:, h, :], ident, r=[b_rR[i], bK], w=[pbuf[6 + i]],
                   inc=(h == 3))
            CP("act", kT[:, :, c0:c0 + 128], pb(6 + i, 512).rearrange("p (h c) -> p h c", h=4)[0:64],
               r=[pbuf[6 + i]], w=[b_kT])
            yield

        for hk in range(4):
            i2 = hk % 2
            load_w(qslab[i2], w_in, 256 * hk, 256, [b_qs[i2], b_kvs])
            for ti in range(NT):
                i = ti % 2
                for k in range(KT):
                    MM(pf(0, 256, i * 256), hT[:, k, ti * 128:(ti + 1) * 128], qslab[i2][:, k, :], start=(k == 0),
                       stop=(k == KT - 1), r=[b_hT[ti], b_qs[i2]],
                       w=([qpb[i], pbuf[0]] if (hk == 0 and ti < 2) else [qpb[i]]), inc=(k == KT - 1))
                rope(pf(0, 256, i * 256).rearrange("p (h d) -> p h d", h=4), 4, ti, rR[i], qpb[i], b_rR[i])
                for h in range(4):
                    TR(pb(6 + i)[0:64, h * 128:(h + 1) * 128], rR[i][:, h, :], ident, r=[b_rR[i], bK],
                       w=[pbuf[6 + i]], inc=(h == 3))
                CP("act", qT[:, :, ti * 128:(ti + 1) * 128], pb(6 + i, 512).rearrange("p (h c) -> p h c", h=4)[0:64],
                   r=[pbuf[6 + i]], w=[b_qT])
                yield
            def partA(n, hk=hk):
                i = n % 2
                cur = slice(n * 128, (n + 1) * 128)
                prv = slice(T, T + 128) if n == 0 else slice((n - 1) * 128, n * 128)
                Sps = PF[:, 1024:2048].rearrange("p (g k) -> p g k", g=4)
                for g in range(4):
                    MM(Sps[:, g, 0:128], qT[:, g, cur], kT[:, hk, prv], r=[b_qT, b_kT], w=[pbuf[2 + g // 2]])
                    MM(Sps[:, g, 128:256], qT[:, g, cur], kT[:, hk, cur], r=[b_qT, b_kT], w=[pbuf[2 + g // 2]],
                       inc=(g % 2 == 1))
                yield
                CP("act", S_sb[i], Sps, r=[pbuf[2], pbuf[3]], w=[b_S[i]])
                TT("dve", S_sb[i], S_sb[i], bc_mid(band0 if n == 0 else band, 4), ALU.add, r=[b_S[i], bK],
                   w=[b_S[i]])
                s_ = sm[i]
                rmax, mm_, negm, d2, rs, es, den, rden = [s_[:, 4 * j:4 * j + 4] for j in range(8)]
                P.add("dve", lambda e, o=rmax, a=S_sb[i]: e.tensor_reduce(out=o, in_=a, axis=AX.X, op=ALU.max),
                      [b_S[i]], [b_sm[i]])
                STT(mm_, rmax, 0.125, sinkbc[:, 4 * hk:4 * hk + 4], ALU.mult, ALU.max, r=[b_sm[i], bK], w=[b_sm[i]])
                TS("dve", negm, mm_, -1.0, ALU.mult, r=[b_sm[i]], w=[b_sm[i]])
                TT("dve", d2, sinkbc[:, 4 * hk:4 * hk + 4], negm, ALU.add, r=[b_sm[i], bK], w=[b_sm[i]])
                for g in range(4):
                    ACT(Pe[i][:, g, :], S_sb[i][:, g, :], AF.Exp, r=[b_S[i], b_sm[i]], w=[b_Pe[i], b_sm[i]],
                        bias=negm[:, g:g + 1], scale=0.125, accum=rs[:, g:g + 1])
                ACT(es, d2, AF.Exp, r=[b_sm[i]], w=[b_sm[i]])
                TT("dve", den, rs, es, ALU.add, r=[b_sm[i]], w=[b_sm[i]])
                RECIP(rden, den, r=[b_sm[i]], w=[b_sm[i]])
                TT("dve", Pn[i], Pe[i], bc_last(rden, 256), ALU.mult, r=[b_Pe[i], b_sm[i]], w=[b_Pn[i]])

            def partB(n, hk=hk):
                i = n % 2
                cur = slice(n * 128, (n + 1) * 128)
                vprev = NT if n == 0 else n - 1
                for g in range(4):
                    for kb in range(2):
                        j = g * 2 + kb
                        TR(pb(6 + i)[:, j * 128:(j + 1) * 128], Pn[i][:, g, kb * 128:(kb + 1) * 128], ident,
                           r=[b_Pn[i], bK], w=[pbuf[6 + i]], inc=(j == 7))
                CP("act", PTs[i], pb(6 + i).rearrange("p (j c) -> p j c", j=8), r=[pbuf[6 + i]], w=[b_PT[i]])
                yield
                Ops = pf(4, 256, i * 256).rearrange("p (a c) -> p a c", a=2)
                for g in range(4):
                    half = g % 2
                    for kb in range(2):
                        vt = vprev if kb == 0 else n
                        MM(Ops[64 * half:64 * half + 64, g // 2, :], vA[:, vt, hk * 64:(hk + 1) * 64],
                           PTs[i][:, g * 2 + kb, :], start=(kb == 0), stop=(kb == 1), r=[b_vA, b_PT[i]],
                           w=[Obuf[i]], inc=(g == 3 and kb == 1), tp=(0, 64 * half))
                CP("act", ost[i], Ops, r=[Obuf[i]], w=[b_ost[i]])
                DMA("sp", oaT_d[hk * 256:(hk + 1) * 256, cur].rearrange("(a p) c -> p a c", p=128), ost[i],
                    r=[b_ost[i]], w=[b_oaT])

            yield from partA(0)
            for n in range(NT):
                if n + 1 < NT:
                    yield from partA(n + 1)
                yield from partB(n)
                yield


    P.stage = "stage_2b"
    slab = [A([128, KT, 256], BF16) for _ in range(2)]
    banks2b = [1, 5]
    b_slab = [Buf(), Buf()]
    stg = [A([128, 512]) for _ in range(4)]
    b_stg = [Buf() for _ in range(4)]
    cnt = {"slab": 0, "stg": 0, "bank": 0}
    b_scr = {}

    def sbuf_for(name):
        if name not in b_scr:
            b_scr[name] = Buf()
        return b_scr[name]

    def gemm_fm(w, c0, ncols, act_T, act_bufs_fn, func, dst, dst_row0, out_dt, name):
        for s0 in range(0, ncols, 256):
            n = min(256, ncols - s0)
            si = cnt["slab"] % 2
            cnt["slab"] += 1
            kt = act_T.shape[1]
            load_w(slab[si][:, 0:kt, 0:n], w, c0 + s0, n, [b_slab[si]])
            for ct in range(0, n, 128):
                for g in range(NG):
                    bk = banks2b[cnt["bank"] % 2]
                    cnt["bank"] += 1
                    for k in range(kt):
                        MM(pf(bk), slab[si][:, k, ct:ct + 128], act_T[:, k, g * 512:(g + 1) * 512], start=(k == 0),
                           stop=(k == kt - 1), r=[b_slab[si]] + act_bufs_fn(g), w=[pbuf[bk]], inc=(k == kt - 1))
                    sj = cnt["stg"] % 4
                    cnt["stg"] += 1
                    o = stg[sj] if out_dt == F32 else stg[sj].bitcast(BF16)[:, 0:512]
                    if func == AF.Copy and (cnt["stg"] % 2 == 0):
                        CP("dve", o, pf(bk), r=[pbuf[bk]], w=[b_stg[sj]])
                    else:
                        ACT(o, pf(bk), func, r=[pbuf[bk]], w=[b_stg[sj]])
                    r0 = dst_row0 + s0 + ct
                    DMA("sp", dst[r0:r0 + 128, g * 512:(g + 1) * 512], o, r=[b_stg[sj]], w=[sbuf_for(name)])
                    yield

    def gemm_tm(w, c0, ncols, func, dst, dst_c0, name):
        for s0 in range(0, ncols, 256):
            n = min(256, ncols - s0)
            si = cnt["slab"] % 2
            cnt["slab"] += 1
            load_w(slab[si][:, :, 0:n], w, c0 + s0, n, [b_slab[si]])
            for ti in range(NT):
                bk = banks2b[cnt["bank"] % 2]
                cnt["bank"] += 1
                for k in range(KT):
                    MM(pf(bk, n), hT[:, k, ti * 128:(ti + 1) * 128], slab[si][:, k, 0:n], start=(k == 0),
                       stop=(k == KT - 1), r=[b_slab[si], b_hT[ti]], w=[pbuf[bk]], inc=(k == KT - 1))
                sj = cnt["stg"] % 4
                cnt["stg"] += 1
                o = stg[sj].bitcast(BF16)[:, 0:n]
                if func == AF.Copy and (cnt["stg"] % 2 == 0):
                    CP("dve", o, pf(bk, n), r=[pbuf[bk]], w=[b_stg[sj]])
                else:
                    ACT(o, pf(bk, n), func, r=[pbuf[bk]], w=[b_stg[sj]])
                DMA("sp", dst[ti * 128:(ti + 1) * 128, dst_c0 + s0:dst_c0 + s0 + n], o, r=[b_stg[sj]],
                    w=[sbuf_for(name)])
                yield

    def hT_bufs(g):
        return b_hT[g * 4:(g + 1) * 4]

    def g2b_gen():
        gslab = slab[0][:, :, 0:16]
        load_w(gslab, w_in, 5632, 16, [b_slab[0]])
        cnt["slab"] += 1
        for g in range(NG):
            bk = banks2b[cnt["bank"] % 2]
            cnt["bank"] += 1
            for k in range(KT):
                MM(pf(bk)[0:16, :], gslab[:, k, :], hT[:, k, g * 512:(g + 1) * 512], start=(k == 0), stop=(k == KT - 1),
                   r=[b_slab[0]] + hT_bufs(g), w=[pbuf[bk]], inc=(k == KT - 1))
            CP("act", gkT[0:16, g * 512:(g + 1) * 512], pf(bk)[0:16, :], r=[pbuf[bk]], w=[b_gkT])
            yield
        yield from gemm_fm(w_in, 1536, 1024, hT, hT_bufs, AF.Copy, qbT_d, 0, F32, "qbT")
        yield from gemm_fm(w_in, 2560, 1024, hT, hT_bufs, AF.Copy, kbT_d, 0, F32, "kbT")
        yield from gemm_tm(w_in, 3584, 2048, AF.Copy, vb_d, 0, "vb")
        yield from gemm_tm(w_in, 5648, 2048, AF.Silu, sog_d, 0, "sog")
        yield from gemm_fm(w_in, 7696, 2048, hT, hT_bufs, AF.Sigmoid, sgaT_d, 0, BF16, "sgaT")
        yield from gemm_fm(w_in, 9744, 2048, hT, hT_bufs, AF.Sigmoid, sgbT_d, 0, BF16, "sgbT")

    g1, g2 = swa_gen(), g2b_gen()
    a1 = a2 = True
    acc = 0.0
    RATIO = 1.65
    while a1 or a2:
        if a1:
            P.stage = "stage_2a"
            try:
                next(g1)
            except StopIteration:
                a1 = False
        acc += RATIO if a1 else 1e9
        P.stage = "stage_2b"
        while a2 and acc >= 1.0:
            try:
                next(g2)
            except StopIteration:
                a2 = False
            acc -= 1.0
        if not a2:
            acc = 0.0

    if stage_end():
        return nc

    P.stage = "stage_3"
    A.off = PERSIST
    Sst = A([128, 8, 512])
    Sbf = A([128, 8, 512], BF16)
    b_Sst, b_Sbf = Buf(), Buf()
    qcT = A([128, 8, T], BF16)
    b_qcT = Buf()
    cumB = A([128, 8])
    S3_KEEP = A.off
    ecum = A([128, 8])
    dec = [A([128, 8]) for _ in range(2)]
    b_cum = Buf()
    b_dec = [Buf(), Buf()]
    qf = [A([128, 8, 128]) for _ in range(2)]
    kf = [A([128, 8, 128]) for _ in range(2)]
    b_qf = [Buf(), Buf()]
    b_kf = [Buf(), Buf()]
    vch = [A([128, 2048], BF16) for _ in range(2)]
    b_vch = [Buf(), Buf()]
    G = [A([128, 8, 128]) for _ in range(6)]
    b_G = [Buf() for _ in range(6)]
    qi = [A([128, 8, 128], BF16) for _ in range(2)]
    ki = [A([128, 8, 128], BF16) for _ in range(2)]
    qn = [A([128, 8, 128], BF16) for _ in range(2)]
    ks = [A([128, 8, 128], BF16) for _ in range(2)]
    b_qi, b_ki, b_qn, b_ks = [[Buf(), Buf()] for _ in range(4)]
    ATs = A([128, 4, 128], BF16)
    b_AT = Buf()
    ATf = A([128, 4, 128])
    b_ATf = Buf()
    ksT = A([128, 1024], BF16)
    b_ksT = Buf()
    olst = [A([128, 1024]) for _ in range(2)]
    b_olst = [Buf(), Buf()]
    b_oloc = Buf()
    P.add("pool", lambda e: e.memset(Sst.rearrange("p a b -> p (a b)"), 0.0), w=[b_Sst])
    P.add("pool", lambda e: e.memset(Sbf.rearrange("p a b -> p (a b)"), 0.0), w=[b_Sbf])
    P.add("pool", lambda e: e.memset(cumB, 0.0), w=[b_cum])
    DKS = 256 ** -0.5

    def F2(a):
        return a.rearrange("p a b -> p (a b)")

    def gla_load(c):
        i = c % 2
        cc = slice(c * 128, (c + 1) * 128)
        DMA("sp", qf[i], qbT_d[:, cc].rearrange("(j p) c -> p j c", p=128), r=[sbuf_for("qbT")], w=[b_qf[i]])
        DMA("sp", kf[i], kbT_d[:, cc].rearrange("(j p) c -> p j c", p=128), r=[sbuf_for("kbT")], w=[b_kf[i]])
        DMA("sp", vch[i], vb_d[cc, :], r=[sbuf_for("vb")], w=[b_vch[i]])

    def gla_prep(c):
        i = c % 2
        cc = slice(c * 128, (c + 1) * 128)
        zps = PF[:, 0:1024].rearrange("p (j c) -> p j c", j=8)
        for j in range(8):
            MM(zps[:, j, :], wgk[:, j * 128:(j + 1) * 128], gkT[:, cc], r=[bK, b_gkT], w=[pbuf[j // 4]],
               inc=(j % 4 == 3))
        zb = [pbuf[0], pbuf[1]]
        CP("act", G[0], zps, r=zb, w=[b_G[0]])
        ACT(F2(G[1]), F2(G[0]), AF.Abs, r=[b_G[0]], w=[b_G[1]])
        ACT(F2(G[1]), F2(G[1]), AF.Exp, r=[b_G[1]], w=[b_G[1]], scale=-1.0)
        ACT(F2(G[1]), F2(G[1]), AF.Ln, r=[b_G[1]], w=[b_G[1]], bias=1.0)
        TS("dve", F2(G[0]), F2(G[0]), 0.0, ALU.min, r=[b_G[0]], w=[b_G[0]])
        TT("pool", G[0].rearrange("p a b -> p (a b)"), G[0].rearrange("p a b -> p (a b)"), G[1].rearrange("p a b -> p (a b)"), ALU.subtract, r=[b_G[0], b_G[1]], w=[b_G[0]])
        P.add("dve", lambda e: e.tensor_tensor_scan(out=G[2].rearrange("p a b -> p (a b)"), data0=scanm,
                                                     data1=G[0].rearrange("p a b -> p (a b)"), initial=0.0,
                                                     op0=ALU.mult, op1=ALU.add), [b_G[0], bK], [b_G[2]])
        TT("dve", G[3], G[2], G[2][:, :, 64:65].broadcast_to([128, 8, 128]), ALU.subtract, r=[b_G[2]], w=[b_G[3]])
        TT("dve", G[4], G[2][:, :, 127:128].broadcast_to([128, 8, 128]), G[2], ALU.subtract, r=[b_G[2]],
           w=[b_G[4]])
        ACT(dec[i], G[2][:, :, 127], AF.Exp, r=[b_G[2]], w=[b_dec[i]], scale=1.0 / 16)
        ACT(ecum, cumB, AF.Exp, r=[b_cum], w=[b_cum], scale=1.0 / 16)
        ACT(F2(G[5]), F2(G[3]), AF.Exp, r=[b_G[3]], w=[b_G[5]], scale=1.0 / 16)
        STT(F2(qi[i]), F2(qf[i]), DKS, F2(G[5]), ALU.mult, ALU.mult, r=[b_qf[i], b_G[5]], w=[b_qi[i]])
        ACT(F2(G[5]), F2(G[3]), AF.Exp, r=[b_G[3]], w=[b_G[5]], scale=-1.0 / 16)
        TT("dve", F2(ki[i]), F2(kf[i]), F2(G[5]), ALU.mult, r=[b_kf[i], b_G[5]], w=[b_ki[i]])
        ACT(F2(G[3]), F2(G[2]), AF.Exp, r=[b_G[2]], w=[b_G[3]], scale=1.0 / 16)
        STT(F2(qn[i]), F2(qf[i]), DKS, F2(G[3]), ALU.mult, ALU.mult, r=[b_qf[i], b_G[3]], w=[b_qn[i]])
        ACT(F2(G[4]), F2(G[4]), AF.Exp, r=[b_G[4]], w=[b_G[4]], scale=1.0 / 16)
        TT("pool", ks[i].rearrange("p a b -> p (a b)"), kf[i].rearrange("p a b -> p (a b)"), G[4].rearrange("p a b -> p (a b)"), ALU.mult, r=[b_kf[i], b_G[4]], w=[b_ks[i]])
        TT("dve", qcT[:, :, cc], qn[i], bc_last(ecum, 128), ALU.mult, r=[b_qn[i], b_cum], w=[b_qcT])
        TT("dve", cumB, cumB, G[2][:, :, 127], ALU.add, r=[b_cum, b_G[2]], w=[b_cum])

    def gla_pe(c):
        i = c % 2
        ATp = pf(2).rearrange("p (h c) -> p h c", h=4)
        for h in range(4):
            for dt_ in range(2):
                j = h * 2 + dt_
                MM(ATp[:, h, :], ki[i][:, j, :], qi[i][:, j, :], start=(dt_ == 0), stop=(dt_ == 1),
                   r=[b_ki[i], b_qi[i]], w=[pbuf[2]], inc=(j == 7))
        CP("act", ATf, ATp, r=[pbuf[2]], w=[b_ATf])
        TT("dve", ATs, ATf, bc_mid(tri, 4), ALU.mult, r=[b_ATf, bK], w=[b_AT])
        for j in range(8):
            TR(pb(6)[:, j * 128:(j + 1) * 128], ks[i][:, j, :], ident, r=[b_ks[i], bK], w=[pbuf[6]], inc=(j == 7))
        CP("act", ksT, pb(6), r=[pbuf[6]], w=[b_ksT])
        for hp in range(2):
            for hh in range(2):
                h = hp * 2 + hh
                bk = 3 + hh
                MM(pf(bk), ATs[:, h, :], vch[i][:, h * 512:(h + 1) * 512], start=True, stop=False,
                   r=[b_AT, b_vch[i]], w=[pbuf[bk]])
                MM(pf(bk), qn[i][:, 2 * h, :], Sbf[:, 2 * h, :], start=False, stop=False, r=[b_qn[i], b_Sbf],
                   w=[pbuf[bk]])
                MM(pf(bk), qn[i][:, 2 * h + 1, :], Sbf[:, 2 * h + 1, :], start=False, stop=True,
                   r=[b_qn[i], b_Sbf], w=[pbuf[bk]], inc=True)
            CP("act", olst[hp], PF[:, 3 * 512:5 * 512], r=[pbuf[3], pbuf[4]], w=[b_olst[hp]])
            DMA("sp", oloc_d[c * 128:(c + 1) * 128, hp * 1024:(hp + 1) * 1024], olst[hp], r=[b_olst[hp]],
                w=[b_oloc])

    def gla_update(c):
        i = c % 2
        banks = [5, 0, 1]
        for j in range(8):
            h = j // 2
            bk = banks[j % 3]
            MM(pf(bk), ksT[:, j * 128:(j + 1) * 128], vch[i][:, h * 512:(h + 1) * 512], r=[b_ksT, b_vch[i]],
               w=[pbuf[bk]], inc=True)
            STT(Sst[:, j, :], Sst[:, j, :], dec[i][:, j:j + 1], pf(bk), ALU.mult, ALU.add,
                r=[b_Sst, b_dec[i], pbuf[bk]], w=[b_Sst])
        CP("pool", Sbf.rearrange("p a b -> p (a b)"), Sst.rearrange("p a b -> p (a b)"), r=[b_Sst], w=[b_Sbf])

    m_slab = [A([128, KT, 512], BF16) for _ in range(2)]
    m_bm = [A([1, 512]) for _ in range(2)]
    m_row = [A([1, 512]) for _ in range(2)]
    b_mslab, b_mbm, b_mrow = [Buf(), Buf()], [Buf(), Buf()], [Buf(), Buf()]
    spc = (16 + NT - 1) // NT

    def m_load(j):
        if j < 24:
            mod_load(j, m_slab[j % 2], b_mslab[j % 2], m_bm[j % 2], b_mbm[j % 2])

    def m_comp(j):
        if j < 24:
            mod_slab(j, m_slab[j % 2], b_mslab[j % 2], m_bm[j % 2], b_mbm[j % 2], m_row[j % 2], b_mrow[j % 2], 2,
                     load=False)
    next_j = 8
    m_load(next_j)
    gla_load(0)
    gla_prep(0)
    for c in range(NT):
        if c + 1 < NT:
            gla_load(c + 1)
        gla_pe(c)
        if c + 1 < NT:
            gla_prep(c + 1)
        gla_update(c)
        for _ in range(spc):
            m_load(next_j + 1)
            m_comp(next_j)
            next_j += 1
    while next_j < 24:
        m_load(next_j + 1)
        m_comp(next_j)
        next_j += 1
    b_sloc = [Buf() for _ in range(4)]
    b_sall = [Buf() for _ in range(4)]
    for q in range(4):
        DMA("sp", sloc_q[q][:, 0:512].rearrange("(j p) e -> p j e", p=128), Sst[:, 2 * q:2 * q + 2, :], r=[b_Sst],
            w=[b_sloc[q]])
        DMA("sp", sloc_q[q][:, 512:513].rearrange("(j p) e -> p j e", p=128), cumB[:, 2 * q:2 * q + 2].unsqueeze(2),
            r=[b_cum], w=[b_sloc[q]], slow=True)
        if solo:
            for i3 in range(4):
                DMA("sp", sall_q[q][i3 * 256:(i3 + 1) * 256, :], sloc_q[q], r=[b_sloc[q]], w=[b_sall[q]])
        else:
            P.add("pool", lambda e, q=q: e.collective_compute("AllGather", ALU.bypass,
                                                              replica_groups=[[0, 1, 2, 3], [4, 5, 6, 7]],
                                                              ins=[sloc_q[q].opt()], outs=[sall_q[q].opt()]),
                  [b_sloc[q]], [b_sall[q]], cc=True)

    if stage_end():
        return nc
    A.off = S3_KEEP
    P.stage = "stage_3b"
    Sin = Sst
    sl = [A([128, 8, 520]) for _ in range(2)]
    b_sl = [Buf(), Buf()]
    De = A([128, 8])
    b_De = Buf()
    P.add("pool", lambda e: e.memset(Sin.rearrange("p a b -> p (a b)"), 0.0), r=[b_Sst], w=[b_Sst])
    for i3 in range(3):
        i = i3 % 2
        for q in range(4):
            DMA("sp", sl[i][:, 2 * q:2 * q + 2, :],
                sall_q[q][i3 * 256:(i3 + 1) * 256, :].rearrange("(j p) e -> p j e", p=128), r=[b_sall[q]],
                w=[b_sl[i]])
        ACT(De, sl[i][:, :, 512], AF.Exp, r=[b_sl[i], bK], w=[b_De], scale=cfg[:, 8 + i3:9 + i3])
        for j in range(8):
            eng = "dve"
            TS(eng, Sin[:, j, :], Sin[:, j, :], De[:, j:j + 1], ALU.mult, r=[b_Sst, b_De], w=[b_Sst])
            STT(Sin[:, j, :], sl[i][:, j, 0:512], cfg[:, 2 + i3:3 + i3], Sin[:, j, :], ALU.mult, ALU.add,
                r=[b_sl[i], b_Sst, bK], w=[b_Sst])
    CP("act", Sbf.rearrange("p a b -> p (a b)"), Sin.rearrange("p a b -> p (a b)"), r=[b_Sst], w=[b_Sbf])

    P.stage = "stage_3c"
    olt = [A([128, 2048]) for _ in range(2)]
    b_olt = [Buf(), Buf()]
    sogt = [A([128, 2048], BF16) for _ in range(2)]
    b_sogt = [Buf(), Buf()]
    gno = A([128, 4, 512])
    b_gno = Buf()
    osum = A([128, 4, 512])
    b_osum = Buf()
    junk3 = A([128, 512], BF16)
    b_junk3 = Buf()
    obb = [A([128, 2048], BF16) for _ in range(2)]
    b_obb = [Buf(), Buf()]
    obst = [A([128, KT, 128], BF16) for _ in range(2)]
    b_obst = [Buf(), Buf()]
    sm3 = [A([128, 16]) for _ in range(2)]
    b_sm3 = [Buf(), Buf()]
    b_obT = Buf()
    def load3c(c):
        i = c % 2
        cc = slice(c * 128, (c + 1) * 128)
        DMA("sp", olt[i], oloc_d[cc, :], r=[b_oloc], w=[b_olt[i]])
        DMA("sp", sogt[i], sog_d[cc, :], r=[sbuf_for("sog")], w=[b_sogt[i]])

    load3c(0)
    for c in range(NT):
        i = c % 2
        cc = slice(c * 128, (c + 1) * 128)
        if c + 1 < NT:
            load3c(c + 1)
        for h in range(4):
            for dt_ in range(2):
                MM(pf(h), qcT[:, 2 * h + dt_, cc], Sbf[:, 2 * h + dt_, :], start=(dt_ == 0), stop=(dt_ == 1),
                   r=[b_qcT, b_Sbf], w=[pbuf[h]], inc=(dt_ == 1))
        TT("dve", osum.rearrange("p a b -> p (a b)"), PF[:, 0:2048], olt[i], ALU.add,
           r=[pbuf[0], pbuf[1], pbuf[2], pbuf[3], b_olt[i]], w=[b_osum])
        TT("dve", gno, sogt[i].rearrange("p (h e) -> p h e", h=4), bc_mid(glan, 4), ALU.mult,
           r=[b_sogt[i], bK], w=[b_gno])
        ssq4, tm4, rs4 = sm3[i][:, 0:4], sm3[i][:, 4:8], sm3[i][:, 8:12]
        for h in range(4):
            ACT(junk3, osum[:, h, :], AF.Square, r=[b_osum], w=[b_junk3, b_sm3[i]], accum=ssq4[:, h:h + 1])
        TS("dve", tm4, ssq4, 1.0 / 512, ALU.mult, EPS, ALU.add, r=[b_sm3[i]], w=[b_sm3[i]])
        ACT(tm4, tm4, AF.Sqrt, r=[b_sm3[i]], w=[b_sm3[i]])
        RECIP(rs4, tm4, r=[b_sm3[i]], w=[b_sm3[i]])
        for h in range(4):
            STT(obb[i][:, h * 512:(h + 1) * 512], osum[:, h, :], rs4[:, h:h + 1], gno[:, h, :], ALU.mult, ALU.mult,
                r=[b_osum, b_sm3[i], b_gno], w=[b_obb[i]])
        transpose16(obb[i], b_obb[i], lambda h, i=i: obst[i][:, h * 8:(h + 1) * 8, :], [b_obst[i]])
        DMA("sp", obT_d[:, cc].rearrange("(k p) c -> p k c", p=128), obst[i], r=[b_obst[i]], w=[b_obT])

    if stage_end():
        return nc

    P.stage = "stage_4a"
    A.off = PERSIST
    Wa = A([128, 8, 2048], BF16)
    Wb = A([128, 16, 2048], BF16)
    b_W = Buf()
    oag = [A([128, 8, 512], BF16) for _ in range(2)]
    obg = [A([128, 16, 512], BF16) for _ in range(2)]
    b_oag = [Buf(), Buf()]
    b_obg = [Buf(), Buf()]
    sga = [A([128, 512], BF16) for _ in range(2)]
    sgb = [A([128, 512], BF16) for _ in range(2)]
    b_sga = [Buf(), Buf()]
    b_sgb = [Buf(), Buf()]
    t1 = [A([128, 512]) for _ in range(2)]
    t2 = [A([128, 512]) for _ in range(2)]
    b_t1 = [Buf(), Buf()]
    b_t2 = [Buf(), Buf()]
    mst = [A([128, 512], BF16) for _ in range(2)]
    b_mst = [Buf(), Buf()]
    b_mgT = Buf()
    for q4 in range(4):
        load_w(Wa[:, :, q4 * 512:(q4 + 1) * 512], w_bra, q4 * 512, 512, [b_W])
        load_w(Wb[:, :, q4 * 512:(q4 + 1) * 512], w_brb, q4 * 512, 512, [b_W])
    it = 0
    for g in range(NG):
        gi = g % 2
        gs = slice(g * 512, (g + 1) * 512)
        DMA("sp", oag[gi], oaT_d[:, gs].rearrange("(k p) c -> p k c", p=128), r=[b_oaT], w=[b_oag[gi]])
        DMA("sp", obg[gi], obT_d[:, gs].rearrange("(k p) c -> p k c", p=128), r=[b_obT], w=[b_obg[gi]])
        for f in range(16):
            i = it % 2
            it += 1
            fs = slice(f * 128, (f + 1) * 128)
            DMA("sp", sga[i], sgaT_d[fs, gs], r=[sbuf_for("sgaT")], w=[b_sga[i]])
            DMA("sp", sgb[i], sgbT_d[fs, gs], r=[sbuf_for("sgbT")], w=[b_sgb[i]])
            ba, bb = (0, 1) if i == 0 else (2, 3)
            for k in range(8):
                MM(pf(ba), Wa[:, k, fs], oag[gi][:, k, :], start=(k == 0), stop=(k == 7), r=[b_W, b_oag[gi]],
                   w=[pbuf[ba]], inc=(k == 7))
            for k in range(16):
                MM(pf(bb), Wb[:, k, fs], obg[gi][:, k, :], start=(k == 0), stop=(k == 15), r=[b_W, b_obg[gi]],
                   w=[pbuf[bb]], inc=(k == 15))
            TT("dve", t1[i], pf(ba), sga[i], ALU.mult, r=[pbuf[ba], b_sga[i]], w=[b_t1[i]])
            TT("dve", t2[i], pf(bb), sgb[i], ALU.mult, r=[pbuf[bb], b_sgb[i]], w=[b_t2[i]])
            TT("pool", mst[i], t1[i], t2[i], ALU.add, r=[b_t1[i], b_t2[i]], w=[b_mst[i]])
            DMA("sp", mgT_d[fs, gs], mst[i], r=[b_mst[i]], w=[b_mgT])

    if stage_end():
        return nc

    P.stage = "stage_4b"
    A.off = PERSIST
    Wo = A([128, KT, 2048], BF16)
    b_Wo = Buf()
    bcA = A([128, D])
    bcB = A([128, D])
    bcC = A([128, D])
    tmpf = A([128, D])
    b_bc = Buf()
    b_tmpf = Buf()
    junk = A([128, D], BF16)
    b_junk = Buf()
    mgt = [A([128, KT, 128], BF16) for _ in range(2)]
    b_mgt = [Buf(), Buf()]
    xt = [A([128, D]) for _ in range(2)]
    b_xt = [Buf(), Buf()]
    x1t = A([128, D])
    b_x1t = Buf()
    hb = [A([128, D], BF16) for _ in range(1)]
    b_hb = [Buf()]
    h2st = [A([128, KT, 128], BF16) for _ in range(2)]
    b_h2st = [Buf(), Buf()]
    b_x1 = Buf()
    b_h2T = Buf()
    for q4 in range(4):
        load_w(Wo[:, :, q4 * 512:(q4 + 1) * 512], w_out, q4 * 512, 512, [b_Wo])
    DMA("sp", bcA, mod_d[2 * D:3 * D].partition_broadcast(128), r=[b_mod], w=[b_bc])
    DMA("sp", tmpf, npost1.partition_broadcast(128), w=[b_tmpf])
    TT("dve", bcA, bcA, tmpf, ALU.mult, r=[b_bc, b_tmpf], w=[b_bc])
    DMA("sp", bcB, npre2.partition_broadcast(128), w=[b_bc])
    DMA("sp", tmpf, mod_d[4 * D:5 * D].partition_broadcast(128), r=[b_mod, b_bc], w=[b_tmpf])
    STT(bcB, tmpf, 1.0, bcB, ALU.add, ALU.mult, r=[b_bc, b_tmpf], w=[b_bc])
    DMA("sp", bcC, mod_d[3 * D:4 * D].partition_broadcast(128), r=[b_mod], w=[b_bc])

    def norm2_and_T(src, np_, b_src, dst_fn, b_dst, sidx):
        norm_tile(src, np_, bcB, bcC, hb[0][0:np_], b_src, b_hb[0], sidx)
        transpose16(hb[0][0:np_], b_hb[0], dst_fn, b_dst, np_=np_)

    order = [NT - 1] + list(range(NT - 1))
    b_xl = Buf()
    b_xla = Buf()
    ysb = A([128, D])
    b_ysb = Buf()

    def mm4b(n_):
        ti = order[n_]
        i = n_ % 2
        cc = slice(ti * 128, (ti + 1) * 128)
        DMA("sp", mgt[i], mgT_d[:, cc].rearrange("(k p) c -> p k c", p=128), r=[b_mgT], w=[b_mgt[i]])
        DMA("sp", xt[i], x_d[cc, :], w=[b_xt[i]])
        for s4 in range(4):
            for k in range(KT):
                MM(pf(s4), mgt[i][:, k, :], Wo[:, k, s4 * 512:(s4 + 1) * 512], start=(k == 0), stop=(k == KT - 1),
                   r=[b_mgt[i], b_Wo], w=[pbuf[s4]], inc=(k == KT - 1))
        CP("act", ysb, PF[:, 0:2048], r=[pbuf[0], pbuf[1], pbuf[2], pbuf[3]], w=[b_ysb])

    mm4b(0)
    for n_, ti in enumerate(order):
        i = n_ % 2
        cc = slice(ti * 128, (ti + 1) * 128)
        ss, tm, rs = small[:, 16:17], small[:, 17:18], small[:, 18:19]
        ACT(junk, ysb, AF.Square, r=[b_ysb], w=[b_junk, b_small], accum=ss)
        rms_scale(ss, tm, rs, r=[b_small], w=[b_small])
        STT(tmpf, ysb, rs, bcA, ALU.mult, ALU.mult, r=[b_ysb, b_small, b_bc], w=[b_tmpf])
        if n_ + 1 < NT:
            mm4b(n_ + 1)
        TT("pool", x1t, tmpf, xt[i], ALU.add, r=[b_tmpf, b_xt[i]], w=[b_x1t])
        DMA("sp", x1_d[cc, :], x1t, r=[b_x1t], w=[b_x1])
        if n_ == 0:
            DMA("sp", xl_d, x1t[126:128, :], r=[b_x1t], w=[b_xl])
            if solo:
                for i3 in range(4):
                    DMA("sp", xla_d[2 * i3:2 * i3 + 2, :], xl_d, r=[b_xl], w=[b_xla])
            else:
                P.add("pool", lambda e: e.collective_compute("AllGather", ALU.bypass,
                                                             replica_groups=[[0, 1, 2, 3], [4, 5, 6, 7]],
                                                             ins=[xl_d.opt()], outs=[xla_d.opt()]), [b_xl], [b_xla],
                      cc=True)
        norm2_and_T(x1t, 128, b_x1t, lambda h, i=i: h2st[i][:, h * 8:(h + 1) * 8, :], [b_h2st[i]], 20)
        DMA("sp", h2T_d[:, cc].rearrange("(k p) c -> p k c", p=128), h2st[i], r=[b_h2st[i]], w=[b_h2T])
    xc = Wo.rearrange("p a b -> p (a b)")[0:2, :].bitcast(F32)[:, 0:3 * D].rearrange("p (a b) -> p a b", a=3)
    b_xc = Buf()
    xhh = x1t[0:2, :]
    DMA("sp", xc, xla_d[0:6, :].rearrange("(i r) d -> r i d", r=2), r=[b_xla], w=[b_xc, b_Wo])
    TS("dve", xhh, xc[:, 0, :], cfg[0:2, 5:6], ALU.mult, r=[b_xc, bK], w=[b_x1t])
    for i3 in (1, 2):
        STT(xhh, xc[:, i3, :], cfg[0:2, 5 + i3:6 + i3], xhh, ALU.mult, ALU.add, r=[b_xc, bK, b_x1t], w=[b_x1t])
    hst = A([128, KT, 2], BF16)
    b_hst = Buf()
    norm2_and_T(xhh, 2, b_x1t, lambda h: hst[:, h * 8:(h + 1) * 8, :], [b_hst], 24)
    TS("dve", h2Th.rearrange("p a b -> p (a b)"), hst.rearrange("p a b -> p (a b)"), cfg[:, 0:1], ALU.mult,
       r=[b_hst, bK], w=[b_h2Th])

    if stage_end():
        return nc

    P.stage = "stage_6"
    A.off = PERSIST
    h2T = A([128, KT, T], BF16)
    b_h2 = [Buf() for _ in range(NG)]
    for g in range(NG):
        DMA("sp", h2T[:, :, g * 512:(g + 1) * 512], h2T_d[:, g * 512:(g + 1) * 512].rearrange("(k p) c -> p k c",
                                                                                                p=128),
            r=[b_h2T], w=[b_h2[g]])
    CH = min(1024, T)
    NCH = T // CH
    us = [[A([128, KT, 256], BF16) for _ in range(2)] for _ in range(2)]
    b_us = [[Buf(), Buf()] for _ in range(2)]
    U = [[A([128, 2 + T]) for _ in range(2)] for _ in range(2)]
    b_U = [[Buf(), Buf()] for _ in range(2)]
    Cb = [[A([128, CH]) for _ in range(2)] for _ in range(2)]
    b_C = [[Buf(), Buf()] for _ in range(2)]
    Gs = [A([128, CH]) for _ in range(2)]
    b_Gs = [Buf(), Buf()]
    ast = [A([128, CH], BF16) for _ in range(2)]
    b_ast = [Buf(), Buf()]
    b_actT = Buf()
    bkc = 0
    def load_us(sx):
        sj = sx % 2
        load_w(us[sj][0], w_up, sx * 256, 256, [b_us[sj][0]])
        load_w(us[sj][1], w_up, DFF + sx * 256, 256, [b_us[sj][1]])
    load_us(0)
    for f in range(FT):
        sidx = f // 2
        if f % 2 == 0 and sidx + 1 < FT // 2:
            load_us(sidx + 1)
        si = sidx % 2
        fo = (f % 2) * 128
        ui = f % 2
        for hv in range(2):
            Ub = U[ui][hv]
            fcol = f + hv * FT
            bk = bkc % 6
            bkc += 1
            for k in range(KT):
                MM(pf(bk, 2), us[si][hv][:, k, fo:fo + 128], h2Th[:, k, :], start=(k == 0), stop=(k == KT - 1),
                   r=[b_us[si][hv], b_h2Th], w=[pbuf[bk]], inc=(k == KT - 1))
            CP("act", Ub[:, 0:2], pf(bk, 2), r=[pbuf[bk]], w=[b_U[ui][hv]])
            for g in range(NG):
                bk = bkc % 6
                bkc += 1
                for k in range(KT):
                    MM(pf(bk), us[si][hv][:, k, fo:fo + 128], h2T[:, k, g * 512:(g + 1) * 512], start=(k == 0),
                       stop=(k == KT - 1), r=[b_us[si][hv], b_h2[g]], w=[pbuf[bk]], inc=(k == KT - 1))
                CP("act", Ub[:, 2 + g * 512:2 + (g + 1) * 512], pf(bk), r=[pbuf[bk]], w=[b_U[ui][hv]])
        for ch in range(NCH):
            ci = (f * NCH + ch) % 2
            o0 = ch * CH
            for hv in range(2):
                Ub = U[ui][hv]
                fcol = f + hv * FT
                Cc = Cb[ci][hv]
                ACT(Cc, Ub[:, 2 + o0:2 + o0 + CH], AF.Identity, r=[b_U[ui][hv], bK], w=[b_C[ci][hv]],
                    bias=convp[:, 3, fcol:fcol + 1], scale=convp[:, 2, fcol:fcol + 1])
                STT(Cc, Ub[:, 1 + o0:1 + o0 + CH], convp[:, 1, fcol:fcol + 1], Cc, ALU.mult, ALU.add,
                    r=[b_U[ui][hv], bK, b_C[ci][hv]], w=[b_C[ci][hv]])
                STT(Cc, Ub[:, o0:o0 + CH], convp[:, 0, fcol:fcol + 1], Cc, ALU.mult, ALU.add,
                    r=[b_U[ui][hv], bK, b_C[ci][hv]], w=[b_C[ci][hv]])
            ACT(Gs[ci], Cb[ci][0], AF.Silu, r=[b_C[ci][0]], w=[b_Gs[ci]])
            TT("pool", ast[ci], Gs[ci], Cb[ci][1], ALU.mult, r=[b_Gs[ci], b_C[ci][1]], w=[b_ast[ci]])
            DMA("sp", actT_d[f * 128:(f + 1) * 128, o0:o0 + CH], ast[ci], r=[b_ast[ci]], w=[b_actT])

    if stage_end():
        return nc

    P.stage = "stage_7"
    A.off = PERSIST
    TG = min(1024, T)
    NTG = T // TG
    ssq7 = A([128, NT, 8])
    S7_KEEP = A.off
    ag = [A([128, FT, 512], BF16) for _ in range(TG // 512)]
    b_ag = [Buf() for _ in range(TG // 512)]
    ds = [A([128, FT, 256], BF16) for _ in range(2)]
    b_ds = [Buf(), Buf()]
    yst = [A([128, 256]) for _ in range(4)]
    b_yst = [Buf() for _ in range(4)]
    b_ssq7 = Buf()
    junk7 = A([128, 256], BF16)
    b_junk7 = Buf()
    b_y2 = Buf()
    it = 0
    for tg in range(NTG):
        for hh in range(TG // 512):
            c0 = tg * TG + hh * 512
            for k0 in range(0, FT, 11):
                DMA("sp", ag[hh][:, k0:k0 + 11, :], actT_d[k0 * 128:(k0 + 11) * 128, c0:c0 + 512].rearrange(
                    "(k p) c -> p k c", p=128), r=[b_actT], w=[b_ag[hh]])
        for s8 in range(8):
            si = (tg * 8 + s8) % 2
            for k0 in range(0, FT, 11):
                DMA("pool", ds[si][:, k0:k0 + 11, :], wview(w_down, s8 * 256, 256)[:, k0:k0 + 11, :], w=[b_ds[si]])
            for tt in range(TG // 128):
                ti = tg * (TG // 128) + tt
                hh, toff = tt // 4, (tt % 4) * 128
                bk = it % 6
                sj = it % 4
                it += 1
                for k in range(FT):
                    MM(pf(bk, 256), ag[hh][:, k, toff:toff + 128], ds[si][:, k, :], start=(k == 0), stop=(k == FT - 1),
                       r=[b_ag[hh], b_ds[si]], w=[pbuf[bk]], inc=(k == FT - 1))
                CP("act", yst[sj], pf(bk, 256), r=[pbuf[bk]], w=[b_yst[sj]])
                DMA("sp", y2_d[ti * 128:(ti + 1) * 128, s8 * 256:(s8 + 1) * 256], yst[sj], r=[b_yst[sj]], w=[b_y2])

    if stage_end():
        return nc
    A.off = S7_KEEP
    P.stage = "stage_8"
    gp2 = A([128, D])
    tm8 = A([128, D])
    b_gp2 = Buf()
    b_tm8 = Buf()
    ss8 = A([128, NT])
    rs8 = A([128, NT])
    b_ss8 = Buf()
    y2t = [A([128, D]) for _ in range(2)]
    x1b = [A([128, D]) for _ in range(2)]
    b_y2t = [Buf(), Buf()]
    b_x1b = [Buf(), Buf()]
    b_out = Buf()
    DMA("sp", gp2, mod_d[5 * D:6 * D].partition_broadcast(128), r=[b_mod], w=[b_gp2])
    DMA("sp", tm8, npost2.partition_broadcast(128), w=[b_tm8])
    TT("dve", gp2, gp2, tm8, ALU.mult, r=[b_gp2, b_tm8], w=[b_gp2])
    junk8 = A([128, D], BF16)
    b_junk8 = Buf()
    def load8(ti):
        i = ti % 2
        cc = slice(ti * 128, (ti + 1) * 128)
        DMA("sp", y2t[i], y2_d[cc, :], r=[b_y2], w=[b_y2t[i]])
        DMA("sp", x1b[i], x1_d[cc, :], r=[b_x1], w=[b_x1b[i]])

    load8(0)
    for ti in range(NT):
        i = ti % 2
        cc = slice(ti * 128, (ti + 1) * 128)
        if ti + 1 < NT:
            load8(ti + 1)
        ACT(junk8, y2t[i], AF.Square, r=[b_y2t[i]], w=[b_junk8, b_ss8], accum=ss8[:, ti:ti + 1])
        TS("dve", ss8[:, ti:ti + 1], ss8[:, ti:ti + 1], 1.0 / D, ALU.mult, EPS, ALU.add, r=[b_ss8], w=[b_ss8])
        ACT(ss8[:, ti:ti + 1], ss8[:, ti:ti + 1], AF.Sqrt, r=[b_ss8], w=[b_ss8])
        RECIP(rs8[:, ti:ti + 1], ss8[:, ti:ti + 1], r=[b_ss8], w=[b_ss8])
        STT(y2t[i], y2t[i], rs8[:, ti:ti + 1], gp2, ALU.mult, ALU.mult, r=[b_y2t[i], b_ss8, b_gp2], w=[b_y2t[i]])
        TT("pool", x1b[i], x1b[i], y2t[i], ALU.add, r=[b_x1b[i], b_y2t[i]], w=[b_x1b[i]])
        DMA("sp", out_d[cc, :], x1b[i], r=[b_x1b[i]], w=[b_out])
    P.add("sp", None, r=[b_out], inc=False)
    P.barrier()
    P.emit()
    st.close()
    return nc


def host_constants():
    i = np.arange(128)[:, None]
    j = np.arange(256)[None, :]
    dist = 128 + i - j
    band = np.where((dist >= 0) & (dist < 128), 0.0, NEG).astype(np.float32)
    jj = np.arange(128)[:, None]
    ii = np.arange(128)[None, :]
    tri = (jj <= ii).astype(np.float32)
    scanm = np.ones((128, 1024), np.float32)
    scanm[:, ::128] = 0.0
    invf = (10000.0 ** (-np.arange(0, 64, 2, dtype=np.float32) / 64)).astype(np.float32)
    invf = np.broadcast_to(invf[None, :], (128, 32)).copy()
    ident = np.eye(128, dtype=np.float32)
    return dict(band=band, tri=tri, scanm=scanm, invf=invf, ident=ident)


def make_in_maps(inputs, T):
    x = np.asarray(inputs["x"], np.float32)
    c = np.asarray(inputs["c"], np.float32)
    pos = np.asarray(inputs["positions"], np.int32)
    NT = T // 128
    consts = host_constants()
    g = lambda k: np.ascontiguousarray(np.asarray(inputs[k])[0])
    shared = dict(
        w_mod=g("w_mod"), b_mod=g("b_mod"), mix_norm_pre=g("mix_norm_pre"), mix_norm_post=g("mix_norm_post"),
        w_in=g("w_in"), attn_sinks=g("attn_sinks"),
        wgk=np.ascontiguousarray(np.concatenate([g("w_gk_up"), g("b_gk")[None, :]], 0)),
        gla_norm=g("gla_norm"), w_branch_attn=g("w_branch_attn"), w_branch_gla=g("w_branch_gla"),
        w_out=g("w_out"), ffn_norm_pre=g("ffn_norm_pre"), ffn_norm_post=g("ffn_norm_post"), w_up=g("w_up"),
        w_down=g("w_down"), **consts)
    cw = g("conv_w")
    cb = g("conv_b")
    convp = np.stack([cw[0].reshape(88, 128).T, cw[1].reshape(88, 128).T, cw[2].reshape(88, 128).T,
                      cb.reshape(88, 128).T], axis=1)
    shared["convp"] = np.ascontiguousarray(convp.astype(np.float32))
    maps = []
    for r in range(8):
        b, p = r // 4, r % 4
        t0 = p * T
        m = dict(shared)
        m["x"] = np.ascontiguousarray(x[b, t0:t0 + T])
        m["xh"] = np.ascontiguousarray(x[b, t0 - 128:t0]) if p > 0 else np.zeros((128, D), np.float32)
        m["c"] = np.ascontiguousarray(c[b].reshape(KT, 128).T)
        pp = np.zeros((128, NT + 1), np.int32)
        pp[:, :NT] = pos[b, t0:t0 + T].reshape(NT, 128).T
        if p > 0:
            pp[:, NT] = pos[b, t0 - 128:t0]
        m["pos"] = pp
        cfg = np.zeros((128, 16), np.float32)
        cfg[:, 0] = 1.0 if p > 0 else 0.0
        cfg[:, 1] = 0.0 if p > 0 else NEG
        for i in range(3):
            cfg[:, 2 + i] = 1.0 if i < p else 0.0
            cfg[:, 5 + i] = 1.0 if i == p - 1 else 0.0
            cfg[:, 8 + i] = (1.0 / 16) if i < p else 0.0
        m["cfg"] = cfg
        maps.append(m)
    return maps


_CACHE = {}


def kernel(**inputs):
    S = np.asarray(inputs["x"]).shape[1]
    T = S // 4
    if T not in _CACHE:
        _CACHE[T] = build(T)
    nc = _CACHE[T]
    maps = make_in_maps(inputs, T)
    res = run_bass_kernel_spmd(nc, maps, core_ids=list(range(8)))
    out = np.zeros((2, S, D), np.float32)
    for r in range(8):
        b, p = r // 4, r % 4
        out[b, p * T:(p + 1) * T] = res.results[r]["out"]
    return out
```
